# Optimizing a Trainium2 kernel written in Bass

```python
import jax, jax.numpy as jnp
from jax import lax

D_MODEL = 1024
BATCH = 4
SEQ = 4096
DEPTH = 2

D_RNN = 1024
RNN_BLOCKS = 16
RNN_BLOCK = D_RNN // RNN_BLOCKS
CONV_A = 4
LRU_C = 8.0
HEAD_DIM = 64
ATTN_GROUPS = ((128, 1), (512, 4), (2048, 16))
N_GROUPS = len(ATTN_GROUPS)
HEADS_PER_GROUP = 4
N_ATTN_HEADS = N_GROUPS * HEADS_PER_GROUP
D_ATTN = N_ATTN_HEADS * HEAD_DIM
D_ATTN_OUT = HEADS_PER_GROUP * HEAD_DIM
ROT_DIM = HEAD_DIM // 4
ROPE_THETA = 500000.0
Q_BLOCK = 128
RWKV_HEAD = 64
D_RWKV = 1024
N_RWKV_HEADS = D_RWKV // RWKV_HEAD
DECAY_LORA = 64
AAA_LORA = 64
GATE_LORA = 128
MV_LORA = 32
RWKV_GN_EPS = 64e-5
N_SHIFT = 3 * D_RWKV + DECAY_LORA + AAA_LORA + GATE_LORA
N_BRANCH = 3
IN_SPLITS = (D_RNN, D_RNN, D_ATTN, D_ATTN, D_ATTN, N_SHIFT, N_BRANCH * D_MODEL)
N_IN = sum(IN_SPLITS)
RWKV_SPLITS = (D_RWKV, D_RWKV, D_RWKV, DECAY_LORA, AAA_LORA, GATE_LORA)
D_FF = 2816
CONV_F = 3
ALPHA = (2 * DEPTH) ** 0.25
BETA = (8 * DEPTH) ** -0.25
LN_EPS = 1e-5

kernel_name = 'hybrid_rglru_dilattn_rwkv7_deepnorm'


def _split(z, sizes):
    idx, acc = [], 0
    for s in sizes[:-1]:
        acc += s
        idx.append(acc)
    return jnp.split(z, idx, axis=-1)


def _layer_norm(x, w, b):
    xf = x.astype(jnp.float32)
    mu = xf.mean(-1, keepdims=True)
    var = jnp.square(xf - mu).mean(-1, keepdims=True)
    return ((xf - mu) * lax.rsqrt(var + LN_EPS) * w + b).astype(x.dtype)


def _causal_dwconv(u, w, b):
    k_w = w.shape[0]
    s = u.shape[1]
    up = jnp.pad(u, ((0, 0), (k_w - 1, 0), (0, 0)))
    return b + sum(w[j] * up[:, k_w - 1 - j:k_w - 1 - j + s] for j in range(k_w))


def _token_shift(z, mu):
    z_prev = jnp.pad(z, ((0, 0), (1, 0), (0, 0)))[:, :-1]
    return z + (z_prev - z) * mu


def _partial_rope(t, positions):
    half = ROT_DIM // 2
    inv_freq = ROPE_THETA ** (-jnp.arange(half, dtype=jnp.float32) / half)
    ang = positions.astype(jnp.float32)[..., None] * inv_freq
    cos = jnp.cos(ang)[:, :, None, :]
    sin = jnp.sin(ang)[:, :, None, :]
    tf = t.astype(jnp.float32)
    x1, x2, rest = tf[..., :half], tf[..., half:ROT_DIM], tf[..., ROT_DIM:]
    out = jnp.concatenate([x1 * cos - x2 * sin, x2 * cos + x1 * sin, rest], axis=-1)
    return out.astype(t.dtype)


def _rg_lru(xa, wa, ba, wx, bx, lam):
    bsz, s, _ = xa.shape
    xf = xa.astype(jnp.float32)
    xb = xf.reshape(bsz, s, RNN_BLOCKS, RNN_BLOCK)
    r = jax.nn.sigmoid(jnp.einsum('bsgi,gij->bsgj', xb, wa).reshape(bsz, s, D_RNN) + ba)
    i = jax.nn.sigmoid(jnp.einsum('bsgi,gij->bsgj', xb, wx).reshape(bsz, s, D_RNN) + bx)
    log_a = -LRU_C * r * jax.nn.softplus(-lam)
    a = jnp.exp(log_a)
    b = jnp.sqrt(-jnp.expm1(2.0 * log_a)) * (i * xf)

    def combine(left, right):
        a1, b1 = left
        a2, b2 = right
        return a1 * a2, a2 * b1 + b2

    _, h = lax.associative_scan(combine, (a, b), axis=1)
    return h.astype(xa.dtype)


def _dilated_attention(q, k, v):
    bsz, s = q.shape[:2]
    n_blocks = s // Q_BLOCK
    scale = HEAD_DIM ** -0.5
    qg = q.reshape(bsz, s, N_GROUPS, HEADS_PER_GROUP, HEAD_DIM)
    kg = k.reshape(bsz, s, N_GROUPS, HEADS_PER_GROUP, HEAD_DIM)
    vg = v.reshape(bsz, s, N_GROUPS, HEADS_PER_GROUP, HEAD_DIM)
    k_groups = [kg[:, :, g] for g in range(N_GROUPS)]
    v_groups = [vg[:, :, g] for g in range(N_GROUPS)]

    def block(q0):
        t = q0 + jnp.arange(Q_BLOCK)
        qb = lax.dynamic_slice_in_dim(qg, q0, Q_BLOCK, axis=1).astype(jnp.float32) * scale
        outs, lses = [], []
        for g, (window, dil) in enumerate(ATTN_GROUPS):
            n_keys = window // dil + 1
            idx = t[:, None] - dil * jnp.arange(n_keys)[None, :]
            valid = idx >= 0
            idx = jnp.maximum(idx, 0)
            kb = jnp.take(k_groups[g], idx, axis=1).astype(jnp.float32)
            vb = jnp.take(v_groups[g], idx, axis=1).astype(jnp.float32)
            sc = jnp.einsum('bqhd,bqjhd->bhqj', qb[:, :, g], kb)
            sc = jnp.where(valid[None, None], sc, -jnp.inf)
            lse = jax.nn.logsumexp(sc, axis=-1)
            p = jnp.exp(sc - lse[..., None])
            outs.append(jnp.einsum('bhqj,bqjhd->bqhd', p, vb))
            lses.append(lse)
        wts = jax.nn.softmax(jnp.stack(lses, 0), axis=0)
        return jnp.einsum('gbhq,gbqhd->bqhd', wts, jnp.stack(outs, 0))

    out = lax.map(block, jnp.arange(n_blocks) * Q_BLOCK)
    out = out.transpose(1, 0, 2, 3, 4).reshape(bsz, s, D_ATTN_OUT)
    return out.astype(q.dtype)


def _wkv7(r, decay, k, v, kk, a):
    bsz, _, nh, n = r.shape

    def step(st, inp):
        r_t, w_t, k_t, v_t, kk_t, a_t = inp
        sa = jnp.einsum('bhij,bhj->bhi', st, -kk_t)
        st = (st * w_t[:, :, None, :] + sa[..., None] * (kk_t * a_t)[:, :, None, :]
              + v_t[..., None] * k_t[:, :, None, :])
        return st, jnp.einsum('bhij,bhj->bhi', st, r_t)

    xs = tuple(jnp.moveaxis(t, 1, 0) for t in (r, decay, k, v, kk, a))
    _, y = lax.scan(step, jnp.zeros((bsz, nh, n, n), jnp.float32), xs)
    return jnp.moveaxis(y, 0, 1)


def _rwkv7_branch(zc, v_res, w0, w2, a0, a2, g2, k_k, k_a, r_k, ln_w, ln_b):
    bsz, s, _ = zc.shape
    r, k, v, zw, za, zg = _split(zc.astype(jnp.float32), RWKV_SPLITS)
    w_log = -jax.nn.softplus(-(w0 + jnp.tanh(zw) @ w2)) - 0.5
    decay = jnp.exp(-jnp.exp(w_log))
    a = jax.nn.sigmoid(a0 + za @ a2)
    g = jax.nn.sigmoid(zg) @ g2
    if v_res is None:
        v_first = v
    else:
        v_first, zv1, v0, v2 = v_res
        v = v + (v_first - v) * jax.nn.sigmoid(v0 + zv1.astype(jnp.float32) @ v2)

    def heads(t):
        return t.reshape(bsz, s, N_RWKV_HEADS, RWKV_HEAD)

    kk = heads(k * k_k)
    kk = kk * lax.rsqrt(jnp.sum(kk * kk, -1, keepdims=True) + 1e-12)
    k = k * (1.0 + (a - 1.0) * k_a)
    rh, kh, vh = heads(r), heads(k), heads(v)
    y = _wkv7(rh, heads(decay), kh, vh, kk, heads(a))
    mu = y.mean(-1, keepdims=True)
    var = jnp.square(y - mu).mean(-1, keepdims=True)
    y = ((y - mu) * lax.rsqrt(var + RWKV_GN_EPS)).reshape(bsz, s, D_RWKV) * ln_w + ln_b
    bonus = (jnp.sum(rh * kh * r_k, -1, keepdims=True) * vh).reshape(bsz, s, D_RWKV)
    return ((y + bonus) * g).astype(zc.dtype), v_first


def setup_inputs(seed: int = 0) -> dict:
    key = jax.random.key(seed)
    keys = iter(jax.random.split(key, 64))

    def nrm(shape, scale):
        return jax.random.normal(next(keys), shape, jnp.float32) * scale

    def uni(shape, lo, hi):
        return jax.random.uniform(next(keys), shape, jnp.float32, lo, hi)

    L, LV = DEPTH, DEPTH - 1
    x = nrm((BATCH, SEQ, D_MODEL), 1.0)
    c = nrm((BATCH, D_MODEL), 1.0)
    offset = jax.random.randint(next(keys), (BATCH, 1), 0, 1024, jnp.int32)
    positions = offset + jnp.arange(SEQ, dtype=jnp.int32)[None, :]
    u = uni((L, D_RNN), 0.9, 0.999)
    s_lam = u ** (1.0 / LRU_C)
    lru_lambda = jnp.log(s_lam) - jnp.log1p(-s_lam)
    return {
        'x': x, 'c': c, 'positions': positions,
        'mod_w': nrm((L, D_MODEL, 6 * D_MODEL), 0.5 * D_MODEL ** -0.5),
        'mod_b': nrm((L, 6 * D_MODEL), 0.01),
        'w_in': nrm((L, D_MODEL, N_IN), D_MODEL ** -0.5),
        'w_in_vres': nrm((LV, D_MODEL, MV_LORA), D_MODEL ** -0.5),
        'conv_a_w': nrm((L, CONV_A, D_RNN), CONV_A ** -0.5),
        'conv_a_b': nrm((L, D_RNN), 0.01),
        'lru_wa': nrm((L, RNN_BLOCKS, RNN_BLOCK, RNN_BLOCK), RNN_BLOCK ** -0.5),
        'lru_ba': nrm((L, D_RNN), 0.01),
        'lru_wx': nrm((L, RNN_BLOCKS, RNN_BLOCK, RNN_BLOCK), RNN_BLOCK ** -0.5),
        'lru_bx': nrm((L, D_RNN), 0.01),
        'lru_lambda': lru_lambda,
        'rwkv_mu': uni((L, N_SHIFT), 0.0, 1.0),
        'mu_vres': uni((LV, MV_LORA), 0.0, 1.0),
        'w0': uni((L, D_RWKV), -4.0, 1.0),
        'w2': nrm((L, DECAY_LORA, D_RWKV), 0.1 * DECAY_LORA ** -0.5),
        'a0': nrm((L, D_RWKV), 0.1),
        'a2': nrm((L, AAA_LORA, D_RWKV), 0.1 * AAA_LORA ** -0.5),
        'g2': nrm((L, GATE_LORA, D_RWKV), GATE_LORA ** -0.5),
        'v0': nrm((LV, D_RWKV), 0.1),
        'v2': nrm((LV, MV_LORA, D_RWKV), 0.1 * MV_LORA ** -0.5),
        'k_k': uni((L, D_RWKV), 0.7, 1.0),
        'k_a': uni((L, D_RWKV), 0.8, 1.2),
        'r_k': nrm((L, N_RWKV_HEADS, RWKV_HEAD), 0.1),
        'ln_x_w': 1.0 + nrm((L, D_RWKV), 0.02),
        'ln_x_b': nrm((L, D_RWKV), 0.01),
        'proj_a': nrm((L, D_RNN, D_MODEL), BETA * D_RNN ** -0.5),
        'proj_b': nrm((L, D_ATTN_OUT, D_MODEL), BETA * D_ATTN_OUT ** -0.5),
        'proj_c': nrm((L, D_RWKV, D_MODEL), BETA * D_RWKV ** -0.5),
        'w_o': nrm((L, D_MODEL, D_MODEL), BETA * D_MODEL ** -0.5),
        'ln1_w': 1.0 + nrm((L, D_MODEL), 0.02),
        'ln1_b': nrm((L, D_MODEL), 0.01),
        'ffn_up': nrm((L, D_MODEL, 2 * D_FF), D_MODEL ** -0.5),
        'ffn_conv_w': nrm((L, CONV_F, 2 * D_FF), CONV_F ** -0.5),
        'ffn_conv_b': nrm((L, 2 * D_FF), 0.01),
        'ffn_down': nrm((L, D_FF, D_MODEL), BETA * D_FF ** -0.5),
        'ln2_w': 1.0 + nrm((L, D_MODEL), 0.02),
        'ln2_b': nrm((L, D_MODEL), 0.01),
    }


def reference(x, c, positions, mod_w, mod_b, w_in, w_in_vres, conv_a_w, conv_a_b,
              lru_wa, lru_ba, lru_wx, lru_bx, lru_lambda, rwkv_mu, mu_vres, w0, w2,
              a0, a2, g2, v0, v2, k_k, k_a, r_k, ln_x_w, ln_x_b, proj_a, proj_b,
              proj_c, w_o, ln1_w, ln1_b, ffn_up, ffn_conv_w, ffn_conv_b, ffn_down,
              ln2_w, ln2_b):
    bsz, s = x.shape[:2]
    v_first = None
    for l in range(DEPTH):
        mod = jax.nn.silu(c) @ mod_w[l] + mod_b[l]
        sh1, sc1, gt1, sh2, sc2, gt2 = jnp.split(mod[:, None, :], 6, axis=-1)

        h = x * (1.0 + sc1) + sh1
        if l == 0:
            z = h @ w_in[l]
            xa, ga, q, k, v, zc, zgate = _split(z, IN_SPLITS)
            v_res = None
        else:
            z = h @ jnp.concatenate([w_in[l], w_in_vres[l - 1]], axis=1)
            xa, ga, q, k, v, zc, zgate, zv1 = _split(z, IN_SPLITS + (MV_LORA,))
            v_res = (v_first, _token_shift(zv1, mu_vres[l - 1]), v0[l - 1], v2[l - 1])

        xa = _causal_dwconv(xa, conv_a_w[l], conv_a_b[l])
        y_a = _rg_lru(xa, lru_wa[l], lru_ba[l], lru_wx[l], lru_bx[l], lru_lambda[l]) * jax.nn.gelu(ga)

        q = _partial_rope(q.reshape(bsz, s, N_ATTN_HEADS, HEAD_DIM), positions)
        k = _partial_rope(k.reshape(bsz, s, N_ATTN_HEADS, HEAD_DIM), positions)
        y_b = _dilated_attention(q, k, v.reshape(bsz, s, N_ATTN_HEADS, HEAD_DIM))

        y_c, v_first = _rwkv7_branch(_token_shift(zc, rwkv_mu[l]), v_res, w0[l], w2[l], a0[l],
                                     a2[l], g2[l], k_k[l], k_a[l], r_k[l], ln_x_w[l], ln_x_b[l])

        g_a, g_b, g_c = jnp.split(jax.nn.sigmoid(zgate), N_BRANCH, axis=-1)
        merged = g_a * (y_a @ proj_a[l]) + g_b * (y_b @ proj_b[l]) + g_c * (y_c @ proj_c[l])
        x = _layer_norm(ALPHA * x + (1.0 + gt1) * (merged @ w_o[l]), ln1_w[l], ln1_b[l])

        h = x * (1.0 + sc2) + sh2
        u = _causal_dwconv(h @ ffn_up[l], ffn_conv_w[l], ffn_conv_b[l])
        u_g, u_v = jnp.split(u, 2, axis=-1)
        y = (jax.nn.silu(u_g) * u_v) @ ffn_down[l]
        x = _layer_norm(ALPHA * x + (1.0 + gt2) * y, ln2_w[l], ln2_b[l])
    return x
```

```python
import contextlib
import numpy as np
import concourse.bass as bass
import concourse.mybir as mybir
from concourse.bass_utils import run_bass_kernel_spmd

F32 = mybir.dt.float32
BF16 = mybir.dt.bfloat16
I32 = mybir.dt.int32
AF = mybir.ActivationFunctionType
ALU = mybir.AluOpType
AX = mybir.AxisListType

D_MODEL = 1024
SEQ = 4096
BATCH = 4
DEPTH = 2
D_FF = 2816
ALPHA = (2 * DEPTH) ** 0.25
LN_EPS = 1e-5
GN_EPS = 64e-5


class Sched:
    ENGS = ('pe', 'act', 'dve', 'pool', 'sp')
    EP = 30000
    R = 8

    def __init__(self, nc):
        self.nc = nc
        self.ops = {e: [] for e in self.ENGS}
        self.ncomp = {e: 0 for e in self.ENGS}
        self.ndma = {e: 0 for e in self.ENGS}
        self.last_w = {}
        self.readers = {}

    def add(self, eng, fn, reads=(), writes=(), dma=False):
        deps = set()
        for k in reads:
            w = self.last_w.get(k)
            if w is not None:
                deps.add(w)
        for k in writes:
            w = self.last_w.get(k)
            if w is not None:
                deps.add(w)
            for r in self.readers.get(k, ()):
                deps.add(r)
        if dma:
            me = ('d', eng, self.ndma[eng])
            self.ndma[eng] += 1
        else:
            me = ('c', eng, self.ncomp[eng])
            self.ncomp[eng] += 1
        deps.discard(me)
        self.ops[eng].append((me, fn, deps))
        for k in reads:
            self.readers.setdefault(k, []).append(me)
        for k in writes:
            self.last_w[k] = me
            self.readers[k] = []
        return me

    def op(self, eng, name, *args, r=(), w=(), **kw):
        def fn(e, name=name, args=args, kw=kw):
            return getattr(e, name)(*args, **kw)
        return self.add(eng, fn, r, w)

    def pe(self, name, *args, r=(), w=(), **kw): return self.op('pe', name, *args, r=r, w=w, **kw)
    def act(self, name, *args, r=(), w=(), **kw): return self.op('act', name, *args, r=r, w=w, **kw)
    def dve(self, name, *args, r=(), w=(), **kw): return self.op('dve', name, *args, r=r, w=w, **kw)
    def pool(self, name, *args, r=(), w=(), **kw): return self.op('pool', name, *args, r=r, w=w, **kw)

    def dma(self, out, in_, r=(), w=(), q='sp'):
        def fn(e, out=out, in_=in_):
            return e.dma_start(out=out, in_=in_)
        return self.add(q, fn, r, w, dma=True)

    def emit(self):
        nc = self.nc
        with contextlib.ExitStack() as st:
            csem = {}
            for e in self.ENGS:
                nep = (self.ncomp[e] + self.EP - 1) // self.EP
                csem[e] = [st.enter_context(nc.semaphore(f"c_{e}_{i}")) for i in range(nep)]
            dsem = {}
            for e in self.ENGS:
                if self.ndma[e]:
                    dsem[e] = [st.enter_context(nc.semaphore(f"d_{e}_{i}"))
                               for i in range(min(self.R, self.ndma[e]))]
            block = st.enter_context(nc.Block())
            engobj = {'pe': block.tensor, 'act': block.scalar, 'dve': block.vector,
                      'pool': block.gpsimd, 'sp': block.sync}
            ops, ndma, EP, R = self.ops, self.ndma, self.EP, self.R

            def make(e):
                def body(eng):
                    waited_c = {}
                    waited_d = {}

                    def wait_dep(d):
                        kind, D, n = d
                        if kind == 'c':
                            if D == e and e == 'pe':
                                return
                            if waited_c.get(D, -1) >= n:
                                return
                            waited_c[D] = n
                            eng.wait_ge(csem[D][n // EP], (n % EP) + 1)
                        else:
                            slot = n % R
                            val = 16 * (n // R + 1)
                            if waited_d.get((D, slot), 0) >= val:
                                return
                            waited_d[(D, slot)] = val
                            eng.wait_ge(dsem[D][slot], val)

                    for (me, fn, deps) in ops[e]:
                        for d in sorted(deps):
                            wait_dep(d)
                        kind, _, n = me
                        if kind == 'd':
                            if n >= R:
                                wait_dep(('d', e, n - R))
                            fn(eng).then_inc(dsem[e][n % R], 16)
                        else:
                            fn(eng).then_inc(csem[e][n // EP], 1)
                    for j in range(max(0, ndma[e] - R), ndma[e]):
                        wait_dep(('d', e, j))
                return body
            for e in self.ENGS:
                if ops[e]:
                    engobj[e](make(e))


class Ctx:
    def __init__(self):
        self.nc = bass.Bass("TRN2", target_bir_lowering=False)
        self.st = contextlib.ExitStack()
        self.S = Sched(self.nc)
        self.nps = 0
        self.ps = None

    def din(self, name, shape, dt=F32):
        return self.nc.dram_tensor(name, list(shape), dt, kind="ExternalInput").ap()

    def dout(self, name, shape, dt=F32):
        return self.nc.dram_tensor(name, list(shape), dt, kind="ExternalOutput").ap()

    def sb(self, name, shape, dt=F32):
        return self.st.enter_context(self.nc.sbuf_tensor("sb_" + name, list(shape), dt))

    def init_psum(self):
        self.ps = [self.st.enter_context(self.nc.psum_tensor(f"ps{i}", [128, 512], F32)) for i in range(8)]

    def pst(self):
        i = self.nps % 8
        self.nps += 1
        return self.ps[i], f"ps{i}"

    def finish(self):
        self.S.emit()
        self.st.close()
        return self.nc


def emit_mod(C, modw, modb_sb, cT_sb, ncols, out_sb):
    S = C.S
    sc = C.sb("silu_c", [128, 8])
    wbuf = [C.sb("modw0", [128, 8, 256]), C.sb("modw1", [128, 8, 256])]
    S.act('activation', out=sc[:], in_=cT_sb[:], func=AF.Silu, r=['cT'], w=['silu_c'])
    nj = ncols // 128
    pt, pk = C.pst()
    wv = modw.rearrange("(k p) c -> p k c", p=128)
    for blk in range(ncols // 256):
        wt = wbuf[blk % 2]
        wkey = f"modw{blk % 2}"
        S.dma(out=wt[:], in_=wv[:, :, blk * 256:(blk + 1) * 256], w=[wkey])
        for jj in range(2):
            j = blk * 2 + jj
            for k in range(8):
                S.pe('matmul',
                    pt[:, j:j + 1], wt[:, k, jj * 128:(jj + 1) * 128], sc[:, k:k + 1],
                    start=(k == 0), stop=(k == 7), r=[wkey, 'silu_c'], w=[pk])
    S.dve('tensor_tensor', out=out_sb[:], in0=pt[:, 0:nj], in1=modb_sb[:], op=ALU.add,
          r=[pk, 'modb'], w=['mod'])


TS = 256
NSEG = SEQ // TS
NCH = TS // 64
VEC_NAMES = ['mu_r', 'mu_k', 'mu_v', 'w0', 'a0', 'kk', 'ka', 'rk', 'lnw', 'lnb', 'v0']


def vcol(name, p):
    return VEC_NAMES.index(name) * 4 + p


V_MUZW, V_MUZA, V_MUZG, V_MUZV1 = 44, 45, 46, 47
NVEC = 48
C_ID, C_MTS, C_MST, C_MSTI, C_RST, C_EYE, C_BONES = 0, 64, 64 + TS, 64 + 2 * TS, 64 + 3 * TS, 64 + 4 * TS, 64 + 5 * TS
NCST = 64 + 5 * TS + 128


def rwkv_consts():
    c = np.zeros((128, NCST), np.float32)
    pp = np.arange(128)[:, None] % 64
    f = np.arange(TS)[None, :] % 64
    c[:, C_ID:C_ID + 64] = (pp == np.arange(64)[None, :])
    c[:, C_MTS:C_MTS + TS] = (pp > f)
    c[:, C_MST:C_MST + TS] = (f > pp)
    c[:, C_MSTI:C_MSTI + TS] = (f >= pp)
    c[:, C_RST:C_RST + TS] = (f != 0)
    c[:, C_EYE:C_EYE + TS] = (f == pp)
    c[:, C_BONES:C_BONES + 128] = (np.arange(128)[:, None] // 64 == np.arange(128)[None, :] // 64)
    return c


def build_rwkv(layer):
    C = Ctx()
    S = C.S
    NCW = 1792 + (32 if layer else 0)
    xT = C.din("xT", [1024, SEQ])
    cT = C.din("cT", [128, 8])
    modw = C.din("modw", [1024, 2048])
    modb = C.din("modb", [128, 16])
    wc = C.din("wc", [1024, NCW])
    vec = C.din("vec", [128, NVEC])
    w2c = C.din("w2c", [64, 512])
    a2c = C.din("a2c", [64, 512])
    g2c = C.din("g2c", [128, 512])
    cst = C.din("cst", [128, NCST])
    if layer:
        v2c = C.din("v2c", [32, 512])
        vfT = C.din("vfT", [512, SEQ])
    else:
        vT = C.dout("vT", [512, SEQ])
    ycT = C.dout("ycT", [512, SEQ])
    C.init_psum()

    def ld(name, ap, shape, q='sp'):
        t = C.sb(name, shape)
        S.dma(out=t[:], in_=ap, w=[name], q=q)
        return t
    cT_sb = ld("cT", cT, [128, 8])
    modb_sb = ld("modb", modb, [128, 16])
    vec_sb = ld("vec", vec, [128, NVEC])
    w2_sb = ld("w2c", w2c, [64, 512])
    a2_sb = ld("a2c", a2c, [64, 512])
    g2_sb = ld("g2c", g2c, [128, 512])
    cst_sb = ld("cst", cst, [128, NCST])
    if layer:
        v2_sb = ld("v2c", v2c, [32, 512])
    wcb = C.sb("wcb", [128, 8, NCW], BF16)
    wcv = wc.rearrange("(k p) c -> p k c", p=128)
    c0 = 0
    while c0 < NCW:
        n = min(512, NCW - c0)
        S.dma(out=wcb[:, :, c0:c0 + n], in_=wcv[:, :, c0:c0 + n],
              w=['wcb'], q='pool')
        c0 += n
    mod_sb = C.sb("mod", [128, 16])
    emit_mod(C, modw, modb_sb, cT_sb, 2048, mod_sb)
    S.dve('tensor_scalar', out=mod_sb[:, 8:16], in0=mod_sb[:, 8:16], scalar1=1.0, scalar2=None, op0=ALU.add,
          r=['mod'], w=['mod'])
    omv = C.sb("omv", [128, NVEC])
    S.dve('tensor_scalar', out=omv[:], in0=vec_sb[:], scalar1=-1.0, scalar2=1.0, op0=ALU.mult, op1=ALU.add,
          r=['vec'], w=['omv'])

    def V(name, p, rows=128):
        c = vcol(name, p)
        return vec_sb[0:rows, c:c + 1]

    ident = cst_sb[:, C_ID:C_ID + 64]
    m_ts = cst_sb[:, C_MTS:C_MTS + TS]
    m_st = cst_sb[:, C_MST:C_MST + TS]
    m_sti = cst_sb[:, C_MSTI:C_MSTI + TS]
    rst = cst_sb[:, C_RST:C_RST + TS]
    eye = cst_sb[:, C_EYE:C_EYE + TS]
    bones = cst_sb[:, C_BONES:C_BONES + 128]

    hs = C.sb("hs", [128, 8, TS + 1], BF16)
    xk = [C.sb("xk0", [128, TS + 1]), C.sb("xk1", [128, TS + 1])]
    zwp = C.sb("zwp", [64, TS + 1]); tzw = C.sb("tzw", [64, TS])
    zap = C.sb("zap", [64, TS + 1])
    zgp = C.sb("zgp", [128, TS + 1]); szg = C.sb("szg", [128, TS])
    if layer:
        zvp = C.sb("zvp", [32, TS + 1])
    tmpsh = C.sb("tmpsh", [128, TS])
    zr = C.sb("zr", [128, TS + 1]); zk = C.sb("zk", [128, TS + 1]); zv = C.sb("zv", [128, TS + 1])
    Wt = {n: C.sb("w_" + n, [128, TS]) for n in
          ['ld', 'cl', 'a', 'kk', 'sq', 'kkn', 'km', 'epos', 'eneg', 'eprev', 'Bt', 'Kt', 'rkr', 'lam', 'vf']}
    PP = [{n: C.sb(f"{n}{p}", [128, TS]) for n in ['At', 'Rt', 'AKT', 'RBT', 'RKT', 'TT', 'bon', 'gT']}
          for p in range(4)]
    TOK = [{n: C.sb(f"{n}{p}", [128, NCH, 64]) for n in ['Btok', 'Ktok', 'Vtok']} for p in range(4)]
    gam = C.sb("gam", [128, 4, NCH])
    Nm = C.sb("Nm", [128, TS]); NT = C.sb("NT", [128, TS])
    Pw = [C.sb("Pw0", [128, TS]), C.sb("Pw1", [128, TS])]
    PTw = [C.sb("PTw0", [128, TS]), C.sb("PTw1", [128, TS])]
    IP = C.sb("IP", [128, TS])
    TTw = [C.sb("TTw0", [128, TS]), C.sb("TTw1", [128, TS])]
    Hb = [C.sb("Hb0", [128, 256]), C.sb("Hb1", [128, 256])]
    Xsb = C.sb("Xsb", [128, 256]); Usb = C.sb("Usb", [128, 256]); Htmp = C.sb("Htmp", [128, 256])
    Ysb = C.sb("Ysb", [128, NCH, 4, 64]); dY = C.sb("dY", [128, NCH, 4, 64]); sqY = C.sb("sqY", [128, NCH, 4, 64])
    st1 = C.sb("st1", [128, NCH * 4]); st2 = C.sb("st2", [128, NCH * 4])
    ycs = [C.sb(f"ycs{p}", [128, TS]) for p in range(4)]
    S.pool('memset', Hb[0][:], 0.0, w=['Hb0'])

    def proj(cols0, M, dst, key):
        for (cc, n) in ((0, 1), (1, TS)):
            pt, pk = C.pst()
            for k in range(8):
                S.pe('matmul',
                    pt[0:M, 0:n], wcb[:, k, cols0:cols0 + M], hs[:, k, cc:cc + n], start=(k == 0), stop=(k == 7),
                    r=['wcb', 'hs'], w=[pk])
            S.act('activation', out=dst[0:M, cc:cc + n], in_=pt[0:M, 0:n], func=AF.Identity,
                  r=[pk], w=[key])

    def shift(z, key, M, mucol):
        S.dve('tensor_scalar', out=tmpsh[0:M, :], in0=z[0:M, 0:TS], scalar1=vec_sb[0:M, mucol:mucol + 1],
                                        scalar2=None, op0=ALU.mult, r=[key, 'vec'], w=['tmpsh'])
        S.dve('scalar_tensor_tensor', out=z[0:M, 1:TS + 1], in0=z[0:M, 1:TS + 1],
                                               scalar=omv[0:M, mucol:mucol + 1], in1=tmpsh[0:M, :],
                                               op0=ALU.mult, op1=ALU.add, r=[key, 'omv', 'tmpsh'], w=[key])

    def prod_half(dst_pt, L, R, e_, c):
        pb = 64 * e_
        sl = slice(c * 64, (c + 1) * 64)
        return (dst_pt[pb:pb + 64, sl], L[pb:pb + 64, sl], R[pb:pb + 64, sl])

    def product(L, Lk, R, Rk, dst, dkey, mask=None, eng_alt=0):
        for e_ in range(2):
            pb = 64 * e_
            pt, pk = C.pst()
            for c in range(NCH):
                S.pe('matmul', *prod_half(pt, L, R, e_, c), start=True, stop=True, r=[Lk, Rk], w=[pk])
            if mask is not None:
                S.dve('tensor_tensor', out=dst[pb:pb + 64, :], in0=pt[pb:pb + 64, 0:TS],
                                                              in1=mask[pb:pb + 64, :], op=ALU.mult,
                      r=[pk, 'cst'], w=[dkey])
            elif (e_ + eng_alt) % 2 == 0:
                S.act('activation', out=dst[pb:pb + 64, :], in_=pt[pb:pb + 64, 0:TS], func=AF.Identity,
                      r=[pk], w=[dkey])
            else:
                S.dve('tensor_copy', out=dst[pb:pb + 64, :], in_=pt[pb:pb + 64, 0:TS],
                      r=[pk], w=[dkey])

    cur = 0
    for sg in range(NSEG):
        t0 = sg * TS
        lo = 1 if sg == 0 else 0
        for k in range(8):
            xb = xk[k % 2]
            xkey = f"xk{k % 2}"
            S.dma(out=xb[:, lo:TS + 1], in_=xT[k * 128:(k + 1) * 128, t0 - 1 + lo:t0 + TS],
                  w=[xkey])
            S.act('activation', out=hs[:, k, lo:TS + 1], in_=xb[:, lo:TS + 1], func=AF.Identity,
                                                     scale=mod_sb[:, 8 + k:9 + k], bias=mod_sb[:, k:k + 1],
                  r=[xkey, 'mod'], w=['hs'])
        if sg == 0:
            S.pool('memset', hs[:, :, 0:1], 0.0, w=['hs'])
        proj(1536, 64, zwp, 'zwp'); shift(zwp, 'zwp', 64, V_MUZW)
        S.act('activation', out=tzw[:], in_=zwp[:, 1:TS + 1], func=AF.Tanh, r=['zwp'], w=['tzw'])
        proj(1600, 64, zap, 'zap'); shift(zap, 'zap', 64, V_MUZA)
        proj(1664, 128, zgp, 'zgp'); shift(zgp, 'zgp', 128, V_MUZG)
        S.act('activation', out=szg[:], in_=zgp[:, 1:TS + 1], func=AF.Sigmoid, r=['zgp'], w=['szg'])
        if layer:
            proj(1792, 32, zvp, 'zvp'); shift(zvp, 'zvp', 32, V_MUZV1)
        for p in range(4):
            P_ = PP[p]
            K_ = lambda n, p=p: f"{n}{p}"
            proj(p * 128, 128, zr, 'zr'); shift(zr, 'zr', 128, vcol('mu_r', p))
            proj(512 + p * 128, 128, zk, 'zk'); shift(zk, 'zk', 128, vcol('mu_k', p))
            proj(1024 + p * 128, 128, zv, 'zv'); shift(zv, 'zv', 128, vcol('mu_v', p))
            rs, ks, vs = zr[:, 1:TS + 1], zk[:, 1:TS + 1], zv[:, 1:TS + 1]
            pc = slice(p * 128, (p + 1) * 128)
            if layer == 0:
                S.dma(out=vT[p * 128:(p + 1) * 128, t0:t0 + TS], in_=zv[:, 1:TS + 1], r=['zv'])
            else:
                pt, pk = C.pst()
                S.pe('matmul', pt[:, 0:TS], v2_sb[:, pc], zvp[:, 1:TS + 1], start=True, stop=True,
                     r=['v2c', 'zvp'], w=[pk])
                S.act('activation', out=Wt['lam'][:], in_=pt[:, 0:TS], func=AF.Sigmoid, bias=V('v0', p),
                      r=[pk, 'vec'], w=['w_lam'])
                S.dma(out=Wt['vf'][:], in_=vfT[p * 128:(p + 1) * 128, t0:t0 + TS], w=['w_vf'])
                S.dve('tensor_tensor', out=Wt['vf'][:], in0=Wt['vf'][:], in1=zv[:, 1:TS + 1], op=ALU.subtract,
                      r=['w_vf', 'zv'], w=['w_vf'])
                S.dve('tensor_tensor', out=Wt['vf'][:], in0=Wt['vf'][:], in1=Wt['lam'][:], op=ALU.mult,
                      r=['w_vf', 'w_lam'], w=['w_vf'])
                S.dve('tensor_tensor', out=zv[:, 1:TS + 1], in0=zv[:, 1:TS + 1], in1=Wt['vf'][:], op=ALU.add,
                      r=['w_vf', 'zv'], w=['zv'])
            pt, pk = C.pst()
            S.pe('matmul', pt[:, 0:TS], w2_sb[:, pc], tzw[:], start=True, stop=True,
                 r=['w2c', 'tzw'], w=[pk])
            S.act('activation', out=Wt['ld'][:], in_=pt[:, 0:TS], func=AF.Sigmoid, bias=V('w0', p),
                  r=[pk, 'vec'], w=['w_ld'])
            S.dve('tensor_scalar', out=Wt['ld'][:], in0=Wt['ld'][:], scalar1=-float(np.exp(-0.5)), scalar2=None,
                                            op0=ALU.mult, r=['w_ld'], w=['w_ld'])
            S.dve('tensor_tensor_scan', out=Wt['cl'][:], data0=rst, data1=Wt['ld'][:], initial=0.0,
                                                 op0=ALU.mult, op1=ALU.add, r=['w_ld', 'cst'], w=['w_cl'])
            pt, pk = C.pst()
            S.pe('matmul', pt[:, 0:TS], a2_sb[:, pc], zap[:, 1:TS + 1], start=True, stop=True,
                 r=['a2c', 'zap'], w=[pk])
            S.act('activation', out=Wt['a'][:], in_=pt[:, 0:TS], func=AF.Sigmoid, bias=V('a0', p),
                  r=[pk, 'vec'], w=['w_a'])
            pt, pk = C.pst()
            S.pe('matmul', pt[:, 0:TS], g2_sb[:, pc], szg[:], start=True, stop=True,
                 r=['g2c', 'szg'], w=[pk])
            S.act('activation', out=P_['gT'][:], in_=pt[:, 0:TS], func=AF.Identity, r=[pk], w=[K_('gT')])
            S.dve('tensor_scalar', out=Wt['kk'][:], in0=zk[:, 1:TS + 1], scalar1=V('kk', p), scalar2=None,
                                                 op0=ALU.mult, r=['zk', 'vec'], w=['w_kk'])
            S.pool('tensor_tensor', out=Wt['sq'][:], in0=Wt['kk'][:], in1=Wt['kk'][:], op=ALU.mult,
                   r=['w_kk'], w=['w_sq'])
            pt, pk = C.pst()
            S.pe('matmul', pt[:, 0:TS], bones, Wt['sq'][:], start=True, stop=True, r=['cst', 'w_sq'], w=[pk])
            S.act('activation', out=Wt['sq'][:], in_=pt[:, 0:TS], func=AF.Sqrt, bias=1e-12, r=[pk], w=['w_sq'])
            S.dve('reciprocal', out=Wt['sq'][:], in_=Wt['sq'][:], r=['w_sq'], w=['w_sq'])
            S.dve('tensor_tensor', out=Wt['kkn'][:], in0=Wt['kk'][:], in1=Wt['sq'][:], op=ALU.mult,
                  r=['w_kk', 'w_sq'], w=['w_kkn'])
            S.dve('tensor_scalar', out=Wt['km'][:], in0=Wt['a'][:], scalar1=V('ka', p),
                                                 scalar2=omv[:, vcol('ka', p):vcol('ka', p) + 1], op0=ALU.mult, op1=ALU.add,
                  r=['w_a', 'vec', 'omv'], w=['w_km'])
            S.dve('tensor_tensor', out=Wt['km'][:], in0=Wt['km'][:], in1=zk[:, 1:TS + 1], op=ALU.mult,
                  r=['w_km', 'zk'], w=['w_km'])
            S.act('activation', out=Wt['epos'][:], in_=Wt['cl'][:], func=AF.Exp, r=['w_cl'], w=['w_epos'])
            S.act('activation', out=Wt['eneg'][:], in_=Wt['cl'][:], func=AF.Exp, scale=-1.0, r=['w_cl'], w=['w_eneg'])
            S.dve('tensor_tensor', out=Wt['eprev'][:], in0=Wt['cl'][:], in1=Wt['ld'][:], op=ALU.subtract,
                  r=['w_cl', 'w_ld'], w=['w_eprev'])
            S.act('activation', out=Wt['eprev'][:], in_=Wt['eprev'][:], func=AF.Exp, r=['w_eprev'], w=['w_eprev'])
            S.dve('scalar_tensor_tensor', out=P_['At'][:], in0=Wt['kkn'][:], scalar=-1.0, in1=Wt['eprev'][:],
                                                   op0=ALU.mult, op1=ALU.mult, r=['w_kkn', 'w_eprev'], w=[K_('At')])
            S.pool('tensor_tensor', out=Wt['Bt'][:], in0=Wt['kkn'][:], in1=Wt['a'][:], op=ALU.mult,
                   r=['w_kkn', 'w_a'], w=['w_Bt'])
            S.dve('tensor_tensor', out=Wt['Bt'][:], in0=Wt['Bt'][:], in1=Wt['eneg'][:], op=ALU.mult,
                  r=['w_Bt', 'w_eneg'], w=['w_Bt'])
            S.pool('tensor_tensor', out=Wt['Kt'][:], in0=Wt['km'][:], in1=Wt['eneg'][:], op=ALU.mult,
                   r=['w_km', 'w_eneg'], w=['w_Kt'])
            S.dve('tensor_tensor', out=P_['Rt'][:], in0=zr[:, 1:TS + 1], in1=Wt['epos'][:], op=ALU.mult,
                  r=['zr', 'w_epos'], w=[K_('Rt')])
            S.dve('scalar_tensor_tensor', out=Wt['rkr'][:], in0=zr[:, 1:TS + 1], scalar=V('rk', p), in1=Wt['km'][:],
                                                        op0=ALU.mult, op1=ALU.mult, r=['zr', 'vec', 'w_km'], w=['w_rkr'])
            pt, pk = C.pst()
            S.pe('matmul', pt[:, 0:TS], bones, Wt['rkr'][:], start=True, stop=True, r=['cst', 'w_rkr'], w=[pk])
            S.dve('tensor_tensor', out=P_['bon'][:], in0=pt[:, 0:TS], in1=zv[:, 1:TS + 1], op=ALU.mult,
                  r=[pk, 'zv'], w=[K_('bon')])
            S.dve('tensor_copy', out=gam[:, p, :], in_=Wt['epos'][:].rearrange("q (c t) -> q c t", t=64)[:, :, 63],
                  r=['w_epos'], w=['gam'])
            for (src, skey, nm) in ((Wt['Bt'], 'w_Bt', 'Btok'), (Wt['Kt'], 'w_Kt', 'Ktok'), (zv, 'zv', 'Vtok')):
                off = 1 if nm == 'Vtok' else 0
                dst = TOK[p][nm]
                for e_ in range(2):
                    pb = 64 * e_
                    pt, pk = C.pst()
                    for c in range(NCH):
                        S.pe('matmul',
                            pt[pb:pb + 64, c * 64:(c + 1) * 64], src[pb:pb + 64, off + c * 64:off + (c + 1) * 64],
                            ident[pb:pb + 64, :], start=True, stop=True, r=[skey, 'cst'], w=[pk])
                    S.act('activation',
                        out=dst[pb:pb + 64, :, :], in_=pt[pb:pb + 64, 0:TS].rearrange("q (c t) -> q c t", t=64), func=AF.Identity,
                        r=[pk], w=[K_(nm)])
            product(P_['At'], K_('At'), Wt['Bt'], 'w_Bt', Nm, 'Nm', mask=m_ts)
            product(Wt['Bt'], 'w_Bt', P_['At'], K_('At'), NT, 'NT', mask=m_st)
            product(Wt['Kt'], 'w_Kt', P_['At'], K_('At'), P_['AKT'], K_('AKT'), mask=m_st)
            product(Wt['Bt'], 'w_Bt', P_['Rt'], K_('Rt'), P_['RBT'], K_('RBT'), mask=m_sti)
            product(Wt['Kt'], 'w_Kt', P_['Rt'], K_('Rt'), P_['RKT'], K_('RKT'), mask=m_sti)
            S.pool('tensor_tensor', out=TTw[0][:], in0=NT[:], in1=eye, op=ALU.add, r=['NT', 'cst'], w=['TTw0'])
            Pc, Pck, PTc, PTck = Nm, 'Nm', NT, 'NT'
            tcur = 0
            for lvl in range(5):
                Pn, Pnk = Pw[lvl % 2], f"Pw{lvl % 2}"
                product(PTc, PTck, Pc, Pck, Pn, Pnk, eng_alt=0)
                if lvl < 4:
                    PTn, PTnk = PTw[lvl % 2], f"PTw{lvl % 2}"
                    product(Pc, Pck, PTc, PTck, PTn, PTnk, eng_alt=1)
                S.pool('tensor_tensor', out=IP[:], in0=Pn[:], in1=eye, op=ALU.add, r=[Pnk, 'cst'], w=['IP'])
                if lvl < 4:
                    dstT, dstk = TTw[1 - tcur], f"TTw{1 - tcur}"
                else:
                    dstT, dstk = P_['TT'], K_('TT')
                product(IP, 'IP', TTw[tcur], f"TTw{tcur}", dstT, dstk, eng_alt=lvl)
                tcur = 1 - tcur
                Pc, Pck = Pn, Pnk
                if lvl < 4:
                    PTc, PTck = PTn, PTnk
        for c in range(NCH):
            sl = slice(c * 64, (c + 1) * 64)
            Hc, Hck = Hb[cur], f"Hb{cur}"
            Hn, Hnk = Hb[1 - cur], f"Hb{1 - cur}"
            psX = []
            for e_ in range(2):
                pb = 64 * e_
                pt, pk = C.pst()
                psX.append((pt, pk))
                for p in range(4):
                    cs = slice(p * 64, (p + 1) * 64)
                    S.pe('matmul', pt[pb:pb + 64, cs], PP[p]['At'][pb:pb + 64, sl],
                                                                      Hc[pb:pb + 64, cs], start=True, stop=False,
                         r=[f"At{p}", Hck], w=[pk])
                    S.pe('matmul', pt[pb:pb + 64, cs], PP[p]['AKT'][pb:pb + 64, sl],
                                                                      TOK[p]['Vtok'][pb:pb + 64, c, :], start=False, stop=True,
                         r=[f"AKT{p}", f"Vtok{p}"], w=[pk])
                S.act('activation', out=Xsb[pb:pb + 64, :], in_=pt[pb:pb + 64, 0:256], func=AF.Identity,
                      r=[pk], w=['Xsb'])
            psU = []
            for e_ in range(2):
                pb = 64 * e_
                pt, pk = C.pst()
                for p in range(4):
                    cs = slice(p * 64, (p + 1) * 64)
                    S.pe('matmul', pt[pb:pb + 64, cs], PP[p]['TT'][pb:pb + 64, sl],
                                                                      Xsb[pb:pb + 64, cs], start=True, stop=True,
                         r=[f"TT{p}", 'Xsb'], w=[pk])
                S.dve('tensor_copy', out=Usb[pb:pb + 64, :], in_=pt[pb:pb + 64, 0:256], r=[pk], w=['Usb'])
            for e_ in range(2):
                pb = 64 * e_
                pt, pk = C.pst()
                for p in range(4):
                    cs = slice(p * 64, (p + 1) * 64)
                    S.pe('matmul', pt[pb:pb + 64, cs], PP[p]['Rt'][pb:pb + 64, sl],
                                                                      Hc[pb:pb + 64, cs], start=True, stop=False,
                         r=[f"Rt{p}", Hck], w=[pk])
                    S.pe('matmul', pt[pb:pb + 64, cs], PP[p]['RBT'][pb:pb + 64, sl],
                                                                      Usb[pb:pb + 64, cs], start=False, stop=False,
                         r=[f"RBT{p}", 'Usb'], w=[pk])
                    S.pe('matmul', pt[pb:pb + 64, cs], PP[p]['RKT'][pb:pb + 64, sl],
                                                                      TOK[p]['Vtok'][pb:pb + 64, c, :], start=False, stop=True,
                         r=[f"RKT{p}", f"Vtok{p}"], w=[pk])
                S.act('activation',
                    out=Ysb[pb:pb + 64, c, :, :], in_=pt[pb:pb + 64, 0:256].rearrange("q (a b) -> q a b", b=64), func=AF.Identity,
                    r=[pk], w=['Ysb'])
                pt, pk = C.pst()
                for p in range(4):
                    cs = slice(p * 64, (p + 1) * 64)
                    S.pe('matmul', pt[pb:pb + 64, cs], TOK[p]['Btok'][pb:pb + 64, c, :],
                                                                      Usb[pb:pb + 64, cs], start=True, stop=False,
                         r=[f"Btok{p}", 'Usb'], w=[pk])
                    S.pe('matmul', pt[pb:pb + 64, cs], TOK[p]['Ktok'][pb:pb + 64, c, :],
                                                                      TOK[p]['Vtok'][pb:pb + 64, c, :], start=False, stop=True,
                         r=[f"Ktok{p}", f"Vtok{p}"], w=[pk])
                S.dve('tensor_tensor', out=Htmp[pb:pb + 64, :], in0=pt[pb:pb + 64, 0:256],
                                                              in1=Hc[pb:pb + 64, :], op=ALU.add, r=[pk, Hck], w=['Htmp'])
                S.pool('tensor_tensor',
                    out=Hn[pb:pb + 64, :].rearrange("q (a b) -> q a b", b=64),
                    in0=Htmp[pb:pb + 64, :].rearrange("q (a b) -> q a b", b=64),
                    in1=gam[pb:pb + 64, :, c:c + 1].broadcast_to([64, 4, 64]), op=ALU.mult,
                    r=['Htmp', 'gam'], w=[Hnk])
            cur = 1 - cur
        Yv = Ysb[:].rearrange("q c p i -> q (c p) i")
        dv = dY[:].rearrange("q c p i -> q (c p) i")
        sv = sqY[:].rearrange("q c p i -> q (c p) i")
        G = NCH * 4
        S.dve('tensor_reduce', out=st1[:], in_=Yv, axis=AX.X, op=ALU.add, r=['Ysb'], w=['st1'])
        S.dve('tensor_scalar', out=st1[:], in0=st1[:], scalar1=-1.0 / 64, scalar2=None, op0=ALU.mult, r=['st1'], w=['st1'])
        S.dve('tensor_tensor', out=dv, in0=Yv, in1=st1[:].unsqueeze(2).broadcast_to([128, G, 64]), op=ALU.add,
              r=['Ysb', 'st1'], w=['dY'])
        S.pool('tensor_tensor', out=sv, in0=dv, in1=dv, op=ALU.mult, r=['dY'], w=['sqY'])
        S.dve('tensor_reduce', out=st2[:], in_=sv, axis=AX.X, op=ALU.add, r=['sqY'], w=['st2'])
        S.dve('tensor_scalar', out=st2[:], in0=st2[:], scalar1=1.0 / 64, scalar2=GN_EPS, op0=ALU.mult, op1=ALU.add,
              r=['st2'], w=['st2'])
        S.act('activation', out=st2[:], in_=st2[:], func=AF.Sqrt, r=['st2'], w=['st2'])
        S.dve('reciprocal', out=st2[:], in_=st2[:], r=['st2'], w=['st2'])
        S.dve('tensor_tensor', out=dv, in0=dv, in1=st2[:].unsqueeze(2).broadcast_to([128, G, 64]), op=ALU.mult,
              r=['dY', 'st2'], w=['dY'])
        for p in range(4):
            for e_ in range(2):
                pb = 64 * e_
                pt, pk = C.pst()
                for c in range(NCH):
                    S.pe('matmul', pt[pb:pb + 64, c * 64:(c + 1) * 64], dY[pb:pb + 64, c, p, :],
                                                                    ident[pb:pb + 64, :], start=True, stop=True,
                         r=['dY', 'cst'], w=[pk])
                S.act('activation',
                    out=ycs[p][pb:pb + 64, :], in_=pt[pb:pb + 64, 0:TS], func=AF.Identity,
                    scale=vec_sb[pb:pb + 64, vcol('lnw', p):vcol('lnw', p) + 1],
                    bias=vec_sb[pb:pb + 64, vcol('lnb', p):vcol('lnb', p) + 1], r=[pk, 'vec'], w=[f"ycs{p}"])
            S.dve('tensor_tensor', out=ycs[p][:], in0=ycs[p][:], in1=PP[p]['bon'][:], op=ALU.add,
                  r=[f"ycs{p}", f"bon{p}"], w=[f"ycs{p}"])
            S.pool('tensor_tensor', out=ycs[p][:], in0=ycs[p][:], in1=PP[p]['gT'][:], op=ALU.mult,
                   r=[f"ycs{p}", f"gT{p}"], w=[f"ycs{p}"])
            S.dma(out=ycT[p * 128:(p + 1) * 128, t0:t0 + TS], in_=ycs[p][:], r=[f"ycs{p}"])
    return C.finish()


def pair_rows(hh):
    idx = np.zeros(512, np.int64)
    for p in range(4):
        for e in range(2):
            h = hh * 8 + p + 4 * e
            idx[p * 128 + e * 64:p * 128 + e * 64 + 64] = h * 64 + np.arange(64)
    return idx


def arr128(v):
    return np.ascontiguousarray(v.reshape(-1, 128).T)


def rwkv_inputs(inp, l, b, hh, xT_b, vfT=None):
    idx = pair_rows(hh)
    w_in = inp['w_in'][l]
    base = 1024 + 1024 + 768 * 3
    cols = np.concatenate([base + idx, base + 1024 + idx, base + 2048 + idx,
                           base + 3072 + np.arange(64), base + 3136 + np.arange(64), base + 3200 + np.arange(128)])
    wc = w_in[:, cols]
    if l:
        wc = np.concatenate([wc, inp['w_in_vres'][l - 1]], axis=1)
    mu = inp['rwkv_mu'][l]
    vec = np.zeros((128, NVEC), np.float32)

    def setv(name, full):
        for p in range(4):
            vec[:, vcol(name, p)] = full[idx[p * 128:(p + 1) * 128]]
    setv('mu_r', mu[0:1024]); setv('mu_k', mu[1024:2048]); setv('mu_v', mu[2048:3072])
    setv('w0', inp['w0'][l]); setv('a0', inp['a0'][l]); setv('kk', inp['k_k'][l]); setv('ka', inp['k_a'][l])
    setv('rk', inp['r_k'][l].reshape(-1)); setv('lnw', inp['ln_x_w'][l]); setv('lnb', inp['ln_x_b'][l])
    if l:
        setv('v0', inp['v0'][l - 1])
        vec[0:32, V_MUZV1] = inp['mu_vres'][l - 1]
    vec[0:64, V_MUZW] = mu[3072:3136]
    vec[0:64, V_MUZA] = mu[3136:3200]
    vec[:, V_MUZG] = mu[3200:3328]
    d = {
        'xT': xT_b, 'cT': arr128(inp['c'][b]),
        'modw': np.ascontiguousarray(inp['mod_w'][l][:, 0:2048]),
        'modb': arr128(inp['mod_b'][l][0:2048]),
        'wc': np.ascontiguousarray(wc), 'vec': vec,
        'w2c': np.ascontiguousarray(inp['w2'][l][:, idx]), 'a2c': np.ascontiguousarray(inp['a2'][l][:, idx]),
        'g2c': np.ascontiguousarray(inp['g2'][l][:, idx]), 'cst': rwkv_consts(),
    }
    if l:
        d['v2c'] = np.ascontiguousarray(inp['v2'][l - 1][:, idx])
        d['vfT'] = vfT
    return d


def common_inputs(C):
    xT = C.din("xT", [1024, SEQ])
    cT = C.din("cT", [128, 8])
    modw = C.din("modw", [1024, 2048])
    modb = C.din("modb", [128, 16])
    return xT, cT, modw, modb


def common_mod(C, cT, modw, modb):
    S = C.S
    cT_sb = C.sb("cT", [128, 8]); S.dma(out=cT_sb[:], in_=cT, w=['cT'])
    modb_sb = C.sb("modb", [128, 16]); S.dma(out=modb_sb[:], in_=modb, w=['modb'])
    mod_sb = C.sb("mod", [128, 16])
    emit_mod(C, modw, modb_sb, cT_sb, 2048, mod_sb)
    S.dve('tensor_scalar', out=mod_sb[:, 8:16], in0=mod_sb[:, 8:16], scalar1=1.0, scalar2=None, op0=ALU.add,
          r=['mod'], w=['mod'])
    return mod_sb


LT = 512
LV_NAMES = ['cw0', 'cw1', 'cw2', 'cw3', 'cb', 'ba', 'bx', 'lam']


def build_lru():
    C = Ctx(); S = C.S
    xT, cT, modw, modb = common_inputs(C)
    wa = C.din("wa", [1024, 1024])
    vecl = C.din("vecl", [128, 32])
    gw = C.din("gw", [128, 8, 128])
    yaT = C.dout("yaT", [512, SEQ])
    C.init_psum()
    mod_sb = common_mod(C, cT, modw, modb)
    vl = C.sb("vecl", [128, 32]); S.dma(out=vl[:], in_=vecl, w=['vecl'])
    gw_sb = C.sb("gw", [128, 8, 128]); S.dma(out=gw_sb[:], in_=gw, w=['gw'])
    wab = C.sb("wab", [128, 8, 1024], BF16)
    wav = wa.rearrange("(k p) c -> p k c", p=128)
    for c0 in (0, 512):
        S.dma(out=wab[:, :, c0:c0 + 512], in_=wav[:, :, c0:c0 + 512], w=['wab'], q='pool')
    LVc = lambda n, j: vl[:, LV_NAMES.index(n) * 4 + j:LV_NAMES.index(n) * 4 + j + 1]
    clam = C.sb("clam", [128, 4])
    S.act('activation', out=clam[:], in_=vl[:, 28:32], func=AF.Exp, scale=-1.0, r=['vecl'], w=['clam'])
    S.act('activation', out=clam[:], in_=clam[:], func=AF.Ln, bias=1.0, r=['clam'], w=['clam'])
    S.dve('tensor_scalar', out=clam[:], in0=clam[:], scalar1=-8.0, scalar2=None, op0=ALU.mult, r=['clam'], w=['clam'])
    hs = C.sb("hs", [128, 8, LT + 3], BF16)
    xk = [C.sb("xk0", [128, LT + 3]), C.sb("xk1", [128, LT + 3])]
    xa = C.sb("xa", [128, LT + 3]); gg = C.sb("gg", [128, LT]); xc = C.sb("xc", [128, LT])
    rr = C.sb("rr", [128, LT]); ii = C.sb("ii", [128, LT]); aa = C.sb("aa", [128, LT]); om = C.sb("om", [128, LT])
    hq = C.sb("hq", [128, LT]); carry = C.sb("carry", [128, 4])
    S.pool('memset', carry[:], 0.0, w=['carry'])
    for sg in range(SEQ // LT):
        t0 = sg * LT
        lo = 3 if sg == 0 else 0
        for k in range(8):
            xb, xkey = xk[k % 2], f"xk{k % 2}"
            S.dma(out=xb[:, lo:LT + 3], in_=xT[k * 128:(k + 1) * 128, t0 - 3 + lo:t0 + LT], w=[xkey])
            S.act('activation', out=hs[:, k, lo:LT + 3], in_=xb[:, lo:LT + 3], func=AF.Identity,
                  scale=mod_sb[:, 8 + k:9 + k], bias=mod_sb[:, k:k + 1], r=[xkey, 'mod'], w=['hs'])
        if sg == 0:
            S.pool('memset', hs[:, :, 0:3], 0.0, w=['hs'])
        for j in range(4):
            for (cc, n) in ((0, 3), (3, LT)):
                pt, pk = C.pst()
                for k in range(8):
                    S.pe('matmul', pt[:, 0:n], wab[:, k, j * 128:(j + 1) * 128], hs[:, k, cc:cc + n],
                         start=(k == 0), stop=(k == 7), r=['wab', 'hs'], w=[pk])
                S.act('activation', out=xa[:, cc:cc + n], in_=pt[:, 0:n], func=AF.Identity, r=[pk], w=['xa'])
            pt, pk = C.pst()
            for k in range(8):
                S.pe('matmul', pt[:, 0:LT], wab[:, k, 512 + j * 128:512 + (j + 1) * 128], hs[:, k, 3:LT + 3],
                     start=(k == 0), stop=(k == 7), r=['wab', 'hs'], w=[pk])
            S.act('activation', out=gg[:], in_=pt[:, 0:LT], func=AF.Gelu, r=[pk], w=['gg'])
            S.dve('tensor_scalar', out=xc[:], in0=xa[:, 3:LT + 3], scalar1=LVc('cw0', j), scalar2=LVc('cb', j),
                  op0=ALU.mult, op1=ALU.add, r=['xa', 'vecl'], w=['xc'])
            for jj in (1, 2, 3):
                S.dve('scalar_tensor_tensor', out=xc[:], in0=xa[:, 3 - jj:LT + 3 - jj], scalar=LVc(f'cw{jj}', j), in1=xc[:],
                      op0=ALU.mult, op1=ALU.add, r=['xa', 'vecl', 'xc'], w=['xc'])
            pt, pk = C.pst()
            S.pe('matmul', pt[:, 0:LT], gw_sb[:, j, :], xc[:], start=True, stop=True, r=['gw', 'xc'], w=[pk])
            S.act('activation', out=rr[:], in_=pt[:, 0:LT], func=AF.Sigmoid, bias=LVc('ba', j), r=[pk, 'vecl'], w=['rr'])
            pt, pk = C.pst()
            S.pe('matmul', pt[:, 0:LT], gw_sb[:, 4 + j, :], xc[:], start=True, stop=True, r=['gw', 'xc'], w=[pk])
            S.act('activation', out=ii[:], in_=pt[:, 0:LT], func=AF.Sigmoid, bias=LVc('bx', j), r=[pk, 'vecl'], w=['ii'])
            S.act('activation', out=aa[:], in_=rr[:], func=AF.Exp, scale=clam[:, j:j + 1], r=['rr', 'clam'], w=['aa'])
            S.pool('tensor_tensor', out=om[:], in0=aa[:], in1=aa[:], op=ALU.mult, r=['aa'], w=['om'])
            S.dve('tensor_scalar', out=om[:], in0=om[:], scalar1=-1.0, scalar2=1.0, op0=ALU.mult, op1=ALU.add, r=['om'], w=['om'])
            S.dve('tensor_scalar', out=om[:], in0=om[:], scalar1=1e-30, scalar2=None, op0=ALU.max, r=['om'], w=['om'])
            S.act('activation', out=om[:], in_=om[:], func=AF.Sqrt, r=['om'], w=['om'])
            S.dve('tensor_tensor', out=ii[:], in0=ii[:], in1=xc[:], op=ALU.mult, r=['ii', 'xc'], w=['ii'])
            S.dve('tensor_tensor', out=ii[:], in0=ii[:], in1=om[:], op=ALU.mult, r=['ii', 'om'], w=['ii'])
            S.dve('tensor_tensor_scan', out=hq[:], data0=aa[:], data1=ii[:], initial=carry[:, j:j + 1], op0=ALU.mult, op1=ALU.add,
                  r=['aa', 'ii', 'carry'], w=['hq'])
            S.dve('tensor_copy', out=carry[:, j:j + 1], in_=hq[:, LT - 1:LT], r=['hq'], w=['carry'])
            S.pool('tensor_tensor', out=hq[:], in0=hq[:], in1=gg[:], op=ALU.mult, r=['hq', 'gg'], w=['hq'])
            S.dma(out=yaT[j * 128:(j + 1) * 128, t0:t0 + LT], in_=hq[:], r=['hq'])
    return C.finish()


def mod_inputs(inp, l, b, xT_b, lo=0):
    return {'xT': xT_b, 'cT': arr128(inp['c'][b]),
            'modw': np.ascontiguousarray(inp['mod_w'][l][:, lo:lo + 2048]),
            'modb': arr128(inp['mod_b'][l][lo:lo + 2048])}


def lru_inputs(inp, l, b, hh, xT_b):
    d = mod_inputs(inp, l, b, xT_b)
    ch = hh * 512 + np.arange(512)
    w_in = inp['w_in'][l]
    d['wa'] = np.ascontiguousarray(np.concatenate([w_in[:, ch], w_in[:, 1024 + ch]], axis=1))
    vecl = np.zeros((128, 32), np.float32)
    srcs = {'cw0': inp['conv_a_w'][l][0], 'cw1': inp['conv_a_w'][l][1], 'cw2': inp['conv_a_w'][l][2],
            'cw3': inp['conv_a_w'][l][3], 'cb': inp['conv_a_b'][l], 'ba': inp['lru_ba'][l], 'bx': inp['lru_bx'][l],
            'lam': inp['lru_lambda'][l]}
    for n, v in srcs.items():
        vecl[:, LV_NAMES.index(n) * 4:LV_NAMES.index(n) * 4 + 4] = arr128(v[ch])
    d['vecl'] = vecl
    gw = np.zeros((128, 8, 128), np.float32)
    for j in range(4):
        for e in range(2):
            g = hh * 8 + j * 2 + e
            gw[e * 64:(e + 1) * 64, j, e * 64:(e + 1) * 64] = inp['lru_wa'][l][g]
            gw[e * 64:(e + 1) * 64, 4 + j, e * 64:(e + 1) * 64] = inp['lru_wx'][l][g]
    d['gw'] = gw
    return d


AT = 512
DMAXS = (1, 4, 16)
DILS = (1, 4, 16)
NSTRIP = sum(d + 1 for d in DMAXS)
TWO_PI = float(2 * np.pi)
CW1 = 6.28125
CW2 = float(2 * np.pi - 6.28125)


def attn_strips():
    st = np.zeros((128, NSTRIP, 128), np.float32)
    tk = np.arange(128)[:, None]
    tq = np.arange(128)[None, :]
    o = 0
    for g in range(3):
        dil, dm = DILS[g], DMAXS[g]
        for bi in range(dm + 1):
            delta = dm - bi
            dist = 128 * delta + tq - tk
            st[:, o + bi, :] = (dist >= 0) & (dist <= 128 * dil) & (dist % dil == 0)
        o += dm + 1
    return st.reshape(128, NSTRIP * 128)


def build_attn():
    C = Ctx(); S = C.S
    xT, cT, modw, modb = common_inputs(C)
    wqk = C.din("wqk", [1024, 768])
    wv = C.din("wv", [1024, 384])
    pos = C.din("pos", [1, SEQ], I32)
    freq = C.din("freq", [128, 1])
    strips = C.din("strips", [128, NSTRIP * 128])
    ybT = C.dout("ybT", [128, SEQ])
    C.init_psum()
    mod_sb = common_mod(C, cT, modw, modb)
    fr = C.sb("freq", [128, 1]); S.dma(out=fr[:], in_=freq, w=['freq'])
    strip_sb = C.sb("strips", [128, NSTRIP * 128]); S.dma(out=strip_sb[:], in_=strips, w=['strips'])
    wqkb = C.sb("wqkb", [128, 8, 768], BF16)
    S.dma(out=wqkb[:, :, 0:384], in_=wqk.rearrange("(k p) c -> p k c", p=128)[:, :, 0:384], w=['wqkb'], q='pool')
    S.dma(out=wqkb[:, :, 384:768], in_=wqk.rearrange("(k p) c -> p k c", p=128)[:, :, 384:768], w=['wqkb'], q='pool')
    wvb = C.sb("wvb", [128, 8, 384], BF16)
    S.dma(out=wvb[:], in_=wv.rearrange("(k p) c -> p k c", p=128), w=['wvb'], q='pool')
    W2 = C.sb("W2", [128, 8, 768], BF16)
    S.pool('memset', W2[:], 0.0, w=['W2'])
    for ch in range(6):
        for hd in range(2):
            bs = ch * 128 + hd * 64
            S.dve('tensor_scalar', out=W2[:, :, bs:bs + 8], in0=wqkb[:, :, bs + 8:bs + 16], scalar1=-1.0, scalar2=None,
                  op0=ALU.mult, r=['wqkb', 'W2'], w=['W2'])
            S.dve('tensor_copy', out=W2[:, :, bs + 8:bs + 16], in_=wqkb[:, :, bs:bs + 8], r=['wqkb', 'W2'], w=['W2'])
    ones_bf = C.sb("ones_bf", [128, 64], BF16)
    S.pool('memset', ones_bf[:], 1.0, w=['ones_bf'])
    hs = C.sb("hs", [128, 8, AT], BF16)
    xk = [C.sb("xk0", [128, AT]), C.sb("xk1", [128, AT])]
    qk = [C.sb(f"qk{ch}", [128, SEQ], BF16) for ch in range(6)]
    Vtok = C.sb("Vtok", [128, SEQ // 128, 384], BF16)
    posi = C.sb("posi", [128, AT], I32)
    ang = C.sb("ang", [128, AT]); a2 = C.sb("a2", [128, AT]); tq_ = C.sb("tq", [128, AT]); ki = C.sb("ki", [128, AT], I32)
    kf = C.sb("kf", [128, AT]); mk = C.sb("mk", [128, AT])
    Ct = C.sb("Ct", [128, AT]); St = C.sb("St", [128, AT])
    t1 = C.sb("t1", [128, AT]); t2 = C.sb("t2", [128, AT])
    Pbuf = C.sb("Pbuf", [128, 2, NSTRIP, 128], BF16)
    rd = C.sb("rd", [64, 2, 128]); yb = C.sb("yb", [64, 2, 128])

    for tl in range(SEQ // AT):
        t0 = tl * AT
        for k in range(8):
            xb, xkey = xk[k % 2], f"xk{k % 2}"
            S.dma(out=xb[:], in_=xT[k * 128:(k + 1) * 128, t0:t0 + AT], w=[xkey])
            S.act('activation', out=hs[:, k, :], in_=xb[:], func=AF.Identity,
                  scale=mod_sb[:, 8 + k:9 + k], bias=mod_sb[:, k:k + 1], r=[xkey, 'mod'], w=['hs'])
        S.dma(out=posi[:], in_=pos[:, t0:t0 + AT].partition_broadcast(128), w=['posi'])
        S.dve('tensor_copy', out=ang[:], in_=posi[:], r=['posi'], w=['ang'])
        S.dve('tensor_scalar', out=ang[:], in0=ang[:], scalar1=fr[:, 0:1], scalar2=None, op0=ALU.mult, r=['ang', 'freq'], w=['ang'])
        for (tbl, tkey, shf) in ((St, 'St', 0.0), (Ct, 'Ct', float(np.pi / 2))):
            S.dve('tensor_scalar', out=a2[:], in0=ang[:], scalar1=shf, scalar2=None, op0=ALU.add, r=['ang'], w=['a2'])
            S.dve('tensor_scalar', out=tq_[:], in0=a2[:], scalar1=1.0 / TWO_PI, scalar2=None, op0=ALU.mult, r=['a2'], w=['tq'])
            S.dve('tensor_copy', out=ki[:], in_=tq_[:], r=['tq'], w=['ki'])
            S.dve('tensor_copy', out=kf[:], in_=ki[:], r=['ki'], w=['kf'])
            S.dve('scalar_tensor_tensor', out=a2[:], in0=kf[:], scalar=-CW1, in1=a2[:], op0=ALU.mult, op1=ALU.add,
                  r=['kf', 'a2'], w=['a2'])
            S.dve('scalar_tensor_tensor', out=a2[:], in0=kf[:], scalar=-CW2, in1=a2[:], op0=ALU.mult, op1=ALU.add,
                  r=['kf', 'a2'], w=['a2'])
            S.dve('tensor_scalar', out=mk[:], in0=a2[:], scalar1=float(np.pi), scalar2=None, op0=ALU.is_gt, r=['a2'], w=['mk'])
            S.dve('scalar_tensor_tensor', out=a2[:], in0=mk[:], scalar=-TWO_PI, in1=a2[:], op0=ALU.mult, op1=ALU.add,
                  r=['mk', 'a2'], w=['a2'])
            S.dve('tensor_scalar', out=mk[:], in0=a2[:], scalar1=-float(np.pi), scalar2=None, op0=ALU.is_lt, r=['a2'], w=['mk'])
            S.dve('scalar_tensor_tensor', out=a2[:], in0=mk[:], scalar=TWO_PI, in1=a2[:], op0=ALU.mult, op1=ALU.add,
                  r=['mk', 'a2'], w=['a2'])
            S.dve('tensor_scalar', out=a2[:], in0=a2[:], scalar1=-3.1415925, scalar2=3.1415925, op0=ALU.max, op1=ALU.min,
                  r=['a2'], w=['a2'])
            S.act('activation', out=tbl[:], in_=a2[:], func=AF.Sin, r=['a2'], w=[tkey])
        for ch in range(6):
            p1, p1k = C.pst()
            for k in range(8):
                S.pe('matmul', p1[:, 0:AT], wqkb[:, k, ch * 128:(ch + 1) * 128], hs[:, k, :], start=(k == 0), stop=(k == 7),
                     r=['wqkb', 'hs'], w=[p1k])
            p2, p2k = C.pst()
            for k in range(8):
                S.pe('matmul', p2[:, 0:AT], W2[:, k, ch * 128:(ch + 1) * 128], hs[:, k, :], start=(k == 0), stop=(k == 7),
                     r=['W2', 'hs'], w=[p2k])
            S.dve('tensor_tensor', out=t1[:], in0=p1[:, 0:AT], in1=Ct[:], op=ALU.mult, r=[p1k, 'Ct'], w=['t1'])
            S.dve('tensor_tensor', out=t2[:], in0=p2[:, 0:AT], in1=St[:], op=ALU.mult, r=[p2k, 'St'], w=['t2'])
            S.pool('tensor_tensor', out=qk[ch][:, t0:t0 + AT], in0=t1[:], in1=t2[:], op=ALU.add, r=['t1', 't2'], w=[f"qk{ch}"])
        for sub in range(AT // 128):
            blk = tl * (AT // 128) + sub
            pt, pk = C.pst()
            for k in range(8):
                S.pe('matmul', pt[:, 0:384], hs[:, k, sub * 128:(sub + 1) * 128], wvb[:, k, :], start=(k == 0), stop=(k == 7),
                     r=['hs', 'wvb'], w=[pk])
            S.act('activation', out=Vtok[:, blk, :], in_=pt[:, 0:384], func=AF.Identity, r=[pk], w=['Vtok'])

    for i in range(SEQ // 128):
        entries = []
        j = 0
        for g in range(3):
            kb0 = max(0, i - DMAXS[g])
            sbase = sum(d + 1 for d in DMAXS[:g])
            kbs = list(range(kb0, i + 1))
            for u0 in range(0, len(kbs), 4):
                grp = kbs[u0:u0 + 4]
                n = len(grp)
                for s in range(2):
                    pt, pk = C.pst()
                    for u, kb in enumerate(grp):
                        S.pe('matmul', pt[:, u * 128:(u + 1) * 128], qk[3 + g][s * 64:(s + 1) * 64, kb * 128:(kb + 1) * 128],
                             qk[g][s * 64:(s + 1) * 64, i * 128:(i + 1) * 128], start=True, stop=True,
                             r=[f"qk{3 + g}", f"qk{g}"], w=[pk])
                    S.act('activation', out=Pbuf[:, s, j:j + n, :], in_=pt[:, 0:n * 128].rearrange("p (n q) -> p n q", q=128),
                          func=AF.Exp, scale=0.125, r=[pk], w=['Pbuf'])
                so = sbase + (DMAXS[g] - (i - grp[0]))
                mview = strip_sb[:, so * 128:(so + n) * 128].rearrange("p (n q) -> p n q", q=128).unsqueeze(1).broadcast_to([128, 2, n, 128])
                S.dve('tensor_tensor', out=Pbuf[:, :, j:j + n, :], in0=Pbuf[:, :, j:j + n, :], in1=mview, op=ALU.mult,
                      r=['Pbuf', 'strips'], w=['Pbuf'])
                for kb in grp:
                    entries.append((g, kb, j))
                    j += 1
        pt, pk = C.pst()
        ne = len(entries)
        for s in range(2):
            for idx, (g, kb, jj) in enumerate(entries):
                S.pe('matmul', pt[0:64, s * 256:s * 256 + 128], Vtok[:, kb, g * 128 + s * 64:g * 128 + (s + 1) * 64], Pbuf[:, s, jj, :],
                     start=(idx == 0), stop=(idx == ne - 1), r=['Vtok', 'Pbuf'], w=[pk])
            for idx, (g, kb, jj) in enumerate(entries):
                S.pe('matmul', pt[0:64, s * 256 + 128:s * 256 + 256], ones_bf[:], Pbuf[:, s, jj, :],
                     start=(idx == 0), stop=(idx == ne - 1), r=['ones_bf', 'Pbuf'], w=[pk])
        pv = pt[0:64, :].rearrange("p (s x q) -> p s x q", s=2, x=2)
        S.dve('reciprocal', out=rd[:], in_=pv[:, :, 1, :], r=[pk], w=['rd'])
        S.dve('tensor_tensor', out=yb[:], in0=pv[:, :, 0, :], in1=rd[:], op=ALU.mult, r=[pk, 'rd'], w=['yb'])
        S.dma(out=ybT.rearrange("(s d) t -> d s t", d=64)[:, :, i * 128:(i + 1) * 128], in_=yb[:], r=['yb'])
    return C.finish()


def attn_inputs(inp, l, b, hh, xT_b):
    d = mod_inputs(inp, l, b, xT_b)
    w_in = inp['w_in'][l]
    qb, kb_, vb = 2048, 2048 + 768, 2048 + 1536
    cols = np.concatenate([(g * 4 + 2 * hh) * 64 + np.arange(128) for g in range(3)])
    d['wqk'] = np.ascontiguousarray(np.concatenate([w_in[:, qb + cols], w_in[:, kb_ + cols]], axis=1))
    d['wv'] = np.ascontiguousarray(w_in[:, vb + cols])
    d['pos'] = np.ascontiguousarray(inp['positions'][b][None, :]).astype(np.int32)
    fr = np.zeros((128, 1), np.float32)
    inv = (500000.0 ** (-np.arange(8, dtype=np.float32) / np.float32(8))).astype(np.float32)
    for p in range(128):
        if p % 64 < 16:
            fr[p, 0] = inv[(p % 64) % 8]
    d['freq'] = fr
    d['strips'] = attn_strips()
    return d


DN = 410
DTOK = 2050
NDT = 5
DV_NAMES = ['ln1w', 'ln1b', 'ln2w', 'ln2b']
FV_NAMES = ['fw0', 'fw1', 'fw2', 'fb']


def build_dense():
    C = Ctx(); S = C.S
    xT = C.din("xT", [1024, DTOK])
    cT = C.din("cT", [128, 8])
    modw = C.din("modw", [1024, 6144])
    modb = C.din("modb", [128, 48])
    yaT = C.din("yaT", [1024, DTOK]); ybT = C.din("ybT", [256, DTOK]); ycT = C.din("ycT", [1024, DTOK])
    WG = C.din("WG", [8, 1024, 384])
    WP = C.din("WP", [8, 2304, 128])
    WO = C.din("WO", [1024, 1024])
    WU = C.din("WU", [22, 1024, 256])
    WD = C.din("WD", [2816, 1024])
    dvec = C.din("dvec", [128, 32])
    fvec = C.din("fvec", [128, 176])
    flag = C.din("flag", [128, 1])
    onesm = C.din("onesm", [128, 128])
    outT = C.dout("outT", [1024, 2048])
    C.init_psum()
    cT_sb = C.sb("cT", [128, 8]); S.dma(out=cT_sb[:], in_=cT, w=['cT'])
    modb_sb = C.sb("modb", [128, 48]); S.dma(out=modb_sb[:], in_=modb, w=['modb'])
    mod_sb = C.sb("mod", [128, 48])
    emit_mod(C, modw, modb_sb, cT_sb, 6144, mod_sb)
    for lo_ in (8, 16, 32, 40):
        S.dve('tensor_scalar', out=mod_sb[:, lo_:lo_ + 8], in0=mod_sb[:, lo_:lo_ + 8], scalar1=1.0, scalar2=None, op0=ALU.add,
              r=['mod'], w=['mod'])
    SH1, SC1, GT1, SH2, SC2, GT2 = 0, 8, 16, 24, 32, 40
    dv = C.sb("dvec", [128, 32]); S.dma(out=dv[:], in_=dvec, w=['dvec'])
    fv = C.sb("fvec", [128, 176]); S.dma(out=fv[:], in_=fvec, w=['fvec'])
    fl = C.sb("flag", [128, 1]); S.dma(out=fl[:], in_=flag, w=['flag'])
    om = C.sb("onesm", [128, 128]); S.dma(out=om[:], in_=onesm, w=['onesm'])
    DVc = lambda n, k: dv[:, DV_NAMES.index(n) * 8 + k:DV_NAMES.index(n) * 8 + k + 1]
    FVc = lambda n, j: fv[:, FV_NAMES.index(n) * 44 + j:FV_NAMES.index(n) * 44 + j + 1]
    wg = [C.sb(f"wg{i}", [128, 8, 384], BF16) for i in range(2)]
    wp = [C.sb(f"wp{i}", [128, 18, 128], BF16) for i in range(2)]
    wo = [C.sb(f"wo{i}", [128, 8, 128], BF16) for i in range(2)]
    wu = [C.sb(f"wu{i}", [128, 8, 256], BF16) for i in range(2)]
    wd = [C.sb(f"wd{i}", [128, 22, 128], BF16) for i in range(2)]
    xs = C.sb("xs", [128, 8, DN]); hs = C.sb("hs", [128, 8, DN], BF16)
    ya = C.sb("ya", [128, 8, DN], BF16); yc = C.sb("yc", [128, 8, DN], BF16); yb = C.sb("yb", [128, 2, DN], BF16)
    mg = C.sb("mg", [128, 8, DN], BF16)
    r1 = C.sb("r1", [128, 8, DN]); sq = C.sb("sq", [128, 8, DN]); r2 = C.sb("r2", [128, 8, DN])
    sg_ = [C.sb(f"sg{i}", [128, DN]) for i in range(3)]
    acc = C.sb("acc", [128, DN]); tmp = C.sb("tmp", [128, DN])
    mu = C.sb("mu", [128, DN]); rstd = C.sb("rstd", [128, DN])
    h2 = C.sb("h2", [128, 8, DN], BF16)
    ss = C.sb("ss", [128, 22, DN], BF16)
    upg = C.sb("upg", [128, DN + 2]); upv = C.sb("upv", [128, DN + 2])
    ucg = C.sb("ucg", [128, DN]); ucv = C.sb("ucv", [128, DN])
    tail = C.sb("tail", [128, 44, 2])
    S.pool('memset', tail[:], 0.0, w=['tail'])

    def layer_norm(src, skey, wn, bn, dst, dkey):
        pt, pk = C.pst()
        for k in range(8):
            S.pe('matmul', pt[:, 0:DN], om[:], src[:, k, :], start=(k == 0), stop=(k == 7), r=['onesm', skey], w=[pk])
        S.act('activation', out=mu[:], in_=pt[:, 0:DN], func=AF.Identity, r=[pk], w=['mu'])
        for k in range(8):
            S.dve('tensor_tensor', out=src[:, k, :], in0=src[:, k, :], in1=mu[:], op=ALU.subtract, r=[skey, 'mu'], w=[skey])
        S.pool('tensor_tensor', out=sq[:], in0=src[:], in1=src[:], op=ALU.mult, r=[skey], w=['sq'])
        pt, pk = C.pst()
        for k in range(8):
            S.pe('matmul', pt[:, 0:DN], om[:], sq[:, k, :], start=(k == 0), stop=(k == 7), r=['onesm', 'sq'], w=[pk])
        S.act('activation', out=rstd[:], in_=pt[:, 0:DN], func=AF.Sqrt, bias=LN_EPS_AP[0],
              r=[pk, 'eps'], w=['rstd'])
        S.dve('reciprocal', out=rstd[:], in_=rstd[:], r=['rstd'], w=['rstd'])
        for k in range(8):
            S.dve('tensor_tensor', out=src[:, k, :], in0=src[:, k, :], in1=rstd[:], op=ALU.mult, r=[skey, 'rstd'], w=[skey])
            S.act('activation', out=dst[:, k, :], in_=src[:, k, :], func=AF.Identity, scale=DVc(wn, k), bias=DVc(bn, k),
                  r=[skey, 'dvec'], w=[dkey])

    epsT = C.sb("epsT", [128, 1])
    S.pool('memset', epsT[:], LN_EPS, w=['eps'])
    LN_EPS_AP = [epsT[:, 0:1]]
    nwp = 0
    for tl in range(NDT):
        c0 = tl * DN
        for k in range(8):
            S.dma(out=xs[:, k, :], in_=xT[k * 128:(k + 1) * 128, c0:c0 + DN], w=['xs'])
            S.act('activation', out=hs[:, k, :], in_=xs[:, k, :], func=AF.Identity,
                  scale=mod_sb[:, SC1 + k:SC1 + k + 1], bias=mod_sb[:, SH1 + k:SH1 + k + 1], r=['xs', 'mod'], w=['hs'])
        S.dma(out=ya[:], in_=yaT.rearrange("(k p) t -> p k t", p=128)[:, :, c0:c0 + DN], w=['ya'], q='pool')
        S.dma(out=yc[:], in_=ycT.rearrange("(k p) t -> p k t", p=128)[:, :, c0:c0 + DN], w=['yc'], q='pool')
        S.dma(out=yb[:], in_=ybT.rearrange("(k p) t -> p k t", p=128)[:, :, c0:c0 + DN], w=['yb'], q='pool')
        for m in range(8):
            wgb, wgk = wg[m % 2], f"wg{m % 2}"
            wpb, wpk = wp[m % 2], f"wp{m % 2}"
            S.dma(out=wgb[:], in_=WG[m].rearrange("(k p) c -> p k c", p=128), w=[wgk], q='pool')
            S.dma(out=wpb[:], in_=WP[m].rearrange("(k p) c -> p k c", p=128), w=[wpk], q='pool')
            for br in range(3):
                pt, pk = C.pst()
                for k in range(8):
                    S.pe('matmul', pt[:, 0:DN], wgb[:, k, br * 128:(br + 1) * 128], hs[:, k, :], start=(k == 0), stop=(k == 7),
                         r=[wgk, 'hs'], w=[pk])
                S.act('activation', out=sg_[br][:], in_=pt[:, 0:DN], func=AF.Sigmoid, r=[pk], w=[f"sg{br}"])
            for br, (src, skey, k0, nk) in enumerate(((ya, 'ya', 0, 8), (yb, 'yb', 8, 2), (yc, 'yc', 10, 8))):
                pt, pk = C.pst()
                for k in range(nk):
                    S.pe('matmul', pt[:, 0:DN], wpb[:, k0 + k, :], src[:, k, :], start=(k == 0), stop=(k == nk - 1),
                         r=[wpk, skey], w=[pk])
                if br == 0:
                    S.dve('tensor_tensor', out=acc[:], in0=pt[:, 0:DN], in1=sg_[0][:], op=ALU.mult, r=[pk, 'sg0'], w=['acc'])
                else:
                    S.dve('tensor_tensor', out=tmp[:], in0=pt[:, 0:DN], in1=sg_[br][:], op=ALU.mult, r=[pk, f"sg{br}"], w=['tmp'])
                    if br == 1:
                        S.pool('tensor_tensor', out=acc[:], in0=acc[:], in1=tmp[:], op=ALU.add, r=['acc', 'tmp'], w=['acc'])
                    else:
                        S.pool('tensor_tensor', out=mg[:, m, :], in0=acc[:], in1=tmp[:], op=ALU.add, r=['acc', 'tmp'], w=['mg'])
        for m in range(8):
            wob, wok = wo[m % 2], f"wo{m % 2}"
            S.dma(out=wob[:], in_=WO.rearrange("(k p) c -> p k c", p=128)[:, :, m * 128:(m + 1) * 128], w=[wok], q='pool')
            pt, pk = C.pst()
            for k in range(8):
                S.pe('matmul', pt[:, 0:DN], wob[:, k, :], mg[:, k, :], start=(k == 0), stop=(k == 7), r=[wok, 'mg'], w=[pk])
            S.act('activation', out=tmp[:], in_=pt[:, 0:DN], func=AF.Identity, scale=mod_sb[:, GT1 + m:GT1 + m + 1],
                  r=[pk, 'mod'], w=['tmp'])
            S.dve('scalar_tensor_tensor', out=r1[:, m, :], in0=xs[:, m, :], scalar=float(ALPHA), in1=tmp[:], op0=ALU.mult, op1=ALU.add,
                  r=['xs', 'tmp'], w=['r1'])
        layer_norm(r1, 'r1', 'ln1w', 'ln1b', r1, 'r1')
        for k in range(8):
            S.act('activation', out=h2[:, k, :], in_=r1[:, k, :], func=AF.Identity,
                  scale=mod_sb[:, SC2 + k:SC2 + k + 1], bias=mod_sb[:, SH2 + k:SH2 + k + 1], r=['r1', 'mod'], w=['h2'])
        for j in range(22):
            wub, wuk = wu[j % 2], f"wu{j % 2}"
            S.dma(out=wub[:], in_=WU[j].rearrange("(k p) c -> p k c", p=128), w=[wuk], q='pool')
            for half, (up, ukey, uc, uckey, jj) in enumerate(((upg, 'upg', ucg, 'ucg', j), (upv, 'upv', ucv, 'ucv', 22 + j))):
                pt, pk = C.pst()
                for k in range(8):
                    S.pe('matmul', pt[:, 0:DN], wub[:, k, half * 128:(half + 1) * 128], h2[:, k, :], start=(k == 0), stop=(k == 7),
                         r=[wuk, 'h2'], w=[pk])
                S.act('activation', out=up[:, 2:DN + 2], in_=pt[:, 0:DN], func=AF.Identity, r=[pk], w=[ukey])
                S.pool('tensor_copy', out=up[:, 0:2], in_=tail[:, jj, :], r=['tail'], w=[ukey])
                if tl == 0:
                    S.dve('tensor_scalar', out=up[:, 2:4], in0=up[:, 2:4], scalar1=fl[:, 0:1], scalar2=None, op0=ALU.mult,
                          r=[ukey, 'flag'], w=[ukey])
                S.pool('tensor_copy', out=tail[:, jj, :], in_=up[:, DN:DN + 2], r=[ukey], w=['tail'])
                S.dve('tensor_scalar', out=uc[:], in0=up[:, 2:DN + 2], scalar1=FVc('fw0', jj), scalar2=FVc('fb', jj),
                      op0=ALU.mult, op1=ALU.add, r=[ukey, 'fvec'], w=[uckey])
                S.dve('scalar_tensor_tensor', out=uc[:], in0=up[:, 1:DN + 1], scalar=FVc('fw1', jj), in1=uc[:], op0=ALU.mult, op1=ALU.add,
                      r=[ukey, 'fvec', uckey], w=[uckey])
                S.dve('scalar_tensor_tensor', out=uc[:], in0=up[:, 0:DN], scalar=FVc('fw2', jj), in1=uc[:], op0=ALU.mult, op1=ALU.add,
                      r=[ukey, 'fvec', uckey], w=[uckey])
            S.act('activation', out=ucg[:], in_=ucg[:], func=AF.Silu, r=['ucg'], w=['ucg'])
            S.pool('tensor_tensor', out=ss[:, j, :], in0=ucg[:], in1=ucv[:], op=ALU.mult, r=['ucg', 'ucv'], w=['ss'])
        for m in range(8):
            wdb, wdk = wd[m % 2], f"wd{m % 2}"
            S.dma(out=wdb[:], in_=WD.rearrange("(k p) c -> p k c", p=128)[:, :, m * 128:(m + 1) * 128], w=[wdk], q='pool')
            pt, pk = C.pst()
            for k in range(22):
                S.pe('matmul', pt[:, 0:DN], wdb[:, k, :], ss[:, k, :], start=(k == 0), stop=(k == 21), r=[wdk, 'ss'], w=[pk])
            S.act('activation', out=tmp[:], in_=pt[:, 0:DN], func=AF.Identity, scale=mod_sb[:, GT2 + m:GT2 + m + 1],
                  r=[pk, 'mod'], w=['tmp'])
            S.dve('scalar_tensor_tensor', out=r2[:, m, :], in0=r1[:, m, :], scalar=float(ALPHA), in1=tmp[:], op0=ALU.mult, op1=ALU.add,
                  r=['r1', 'tmp'], w=['r2'])
        layer_norm(r2, 'r2', 'ln2w', 'ln2b', r2, 'r2')
        lo_c = 2 if tl == 0 else 0
        for k in range(8):
            S.dma(out=outT[k * 128:(k + 1) * 128, c0 - 2 + lo_c:c0 - 2 + DN], in_=r2[:, k, lo_c:DN], r=['r2'])
    return C.finish()


def dense_weights(inp, l):
    w_in = inp['w_in'][l]
    zg = w_in[:, 7680:10752]
    WG = np.stack([np.concatenate([zg[:, br * 1024 + m * 128: br * 1024 + (m + 1) * 128] for br in range(3)], axis=1) for m in range(8)])
    pall = np.concatenate([inp['proj_a'][l], inp['proj_b'][l], inp['proj_c'][l]], axis=0)
    WP = np.stack([pall[:, m * 128:(m + 1) * 128] for m in range(8)])
    fu = inp['ffn_up'][l]
    WU = np.stack([np.concatenate([fu[:, j * 128:(j + 1) * 128], fu[:, 2816 + j * 128:2816 + (j + 1) * 128]], axis=1) for j in range(22)])
    dvec = np.concatenate([arr128(inp[n][l]) for n in ('ln1_w', 'ln1_b', 'ln2_w', 'ln2_b')], axis=1)
    fcw = inp['ffn_conv_w'][l]
    fvec = np.concatenate([arr128(fcw[0]), arr128(fcw[1]), arr128(fcw[2]), arr128(inp['ffn_conv_b'][l])], axis=1)
    return {'WG': np.ascontiguousarray(WG), 'WP': np.ascontiguousarray(WP), 'WO': np.ascontiguousarray(inp['w_o'][l]),
            'WU': np.ascontiguousarray(WU), 'WD': np.ascontiguousarray(inp['ffn_down'][l]),
            'dvec': np.ascontiguousarray(dvec), 'fvec': np.ascontiguousarray(fvec),
            'onesm': np.full((128, 128), 1.0 / 1024, np.float32),
            'modw': np.ascontiguousarray(inp['mod_w'][l]), 'modb': arr128(inp['mod_b'][l])}


def halo_T(full_T, sh):
    F = full_T.shape[0]
    out = np.zeros((F, DTOK), np.float32)
    if sh == 0:
        out[:, 2:] = full_T[:, 0:2048]
    else:
        out[:, :] = full_T[:, 2046:4096]
    return out


def dense_inputs(dw, c_b, sh, xT_b, yaT_b, ybT_b, ycT_b):
    d = dict(dw)
    d['cT'] = arr128(c_b)
    d['xT'] = halo_T(xT_b, sh); d['yaT'] = halo_T(yaT_b, sh); d['ybT'] = halo_T(ybT_b, sh); d['ycT'] = halo_T(ycT_b, sh)
    d['flag'] = np.full((128, 1), float(sh), np.float32)
    return d


_PROGS = {}


def _prog(name, fn):
    if name not in _PROGS:
        _PROGS[name] = fn()
    return _PROGS[name]


def _run(nc, maps):
    res = run_bass_kernel_spmd(nc, maps, core_ids=list(range(len(maps))))
    return res.results


def kernel(**inputs):
    inp = {k: np.asarray(v) for k, v in inputs.items()}
    x = inp['x'].astype(np.float32)
    xT = [np.ascontiguousarray(x[b].T) for b in range(BATCH)]
    cores = [(b, hh) for b in range(BATCH) for hh in range(2)]
    vfT = {}
    for l in range(DEPTH):
        res = _run(_prog('lru', build_lru), [lru_inputs(inp, l, b, hh, xT[b]) for (b, hh) in cores])
        yaT = [np.zeros((1024, SEQ), np.float32) for _ in range(BATCH)]
        for (b, hh), r in zip(cores, res):
            yaT[b][hh * 512:(hh + 1) * 512] = r['yaT']
        res = _run(_prog('attn', build_attn), [attn_inputs(inp, l, b, hh, xT[b]) for (b, hh) in cores])
        ybT = [np.zeros((256, SEQ), np.float32) for _ in range(BATCH)]
        for (b, hh), r in zip(cores, res):
            ybT[b][hh * 128:(hh + 1) * 128] = r['ybT']
        res = _run(_prog(f'rwkv{l}', lambda: build_rwkv(l)),
                   [rwkv_inputs(inp, l, b, hh, xT[b], vfT.get((b, hh))) for (b, hh) in cores])
        ycT = [np.zeros((1024, SEQ), np.float32) for _ in range(BATCH)]
        for (b, hh), r in zip(cores, res):
            ycT[b][pair_rows(hh)] = r['ycT']
            if l == 0:
                vfT[(b, hh)] = np.ascontiguousarray(r['vT'])
        dw = dense_weights(inp, l)
        res = _run(_prog('dense', build_dense),
                   [dense_inputs(dw, inp['c'][b], sh, xT[b], yaT[b], ybT[b], ycT[b]) for (b, sh) in cores])
        nxt = [np.zeros((1024, SEQ), np.float32) for _ in range(BATCH)]
        for (b, sh), r in zip(cores, res):
            nxt[b][:, sh * 2048:(sh + 1) * 2048] = r['outT']
        xT = nxt
    return np.ascontiguousarray(np.stack([xT[b].T for b in range(BATCH)])).astype(np.float32)
```

```python
import contextlib
import numpy as np
import concourse.bass as bass
import concourse.mybir as mybir
from concourse.bass_utils import run_bass_kernel_spmd

F32 = mybir.dt.float32
BF16 = mybir.dt.bfloat16
I32 = mybir.dt.int32
AF = mybir.ActivationFunctionType
ALU = mybir.AluOpType
AX = mybir.AxisListType

D_MODEL = 1024
SEQ = 4096
BATCH = 4
DEPTH = 2
D_FF = 2816
ALPHA = (2 * DEPTH) ** 0.25
LN_EPS = 1e-5
GN_EPS = 64e-5


class Sched:
    ENGS = ('pe', 'act', 'dve', 'pool', 'sp')
    EP = 30000
    R = 8

    def __init__(self, nc, tag=""):
        self.nc = nc
        self.tag = tag
        self.ops = {e: [] for e in self.ENGS}
        self.ncomp = {e: 0 for e in self.ENGS}
        self.ndma = {e: 0 for e in self.ENGS}
        self.last_w = {}
        self.readers = {}
        self.side = []
        self.side_every = 0
        self._cnt = 0
        self._in_side = False

    def add(self, eng, fn, reads=(), writes=(), dma=False):
        me = self._add(eng, fn, reads, writes, dma)
        if self.side and not self._in_side and self.side_every:
            self._cnt += 1
            if self._cnt % self.side_every == 0:
                self.replay_side(1)
        return me

    def replay_side(self, n=None):
        self._in_side = True
        k = 0
        while self.side and (n is None or k < n):
            self._add(*self.side.pop(0))
            k += 1
        self._in_side = False

    def _add(self, eng, fn, reads=(), writes=(), dma=False):
        deps = set()
        for k in reads:
            w = self.last_w.get(k)
            if w is not None:
                deps.add(w)
            if k.startswith('ps'):
                for r in self.readers.get(k, ()):
                    if r[1] != eng:
                        deps.add(r)
        for k in writes:
            w = self.last_w.get(k)
            if w is not None:
                deps.add(w)
            for r in self.readers.get(k, ()):
                deps.add(r)
        if dma:
            me = ('d', eng, self.ndma[eng])
            self.ndma[eng] += 1
        else:
            me = ('c', eng, self.ncomp[eng])
            self.ncomp[eng] += 1
        deps.discard(me)
        self.ops[eng].append((me, fn, deps))
        for k in reads:
            self.readers.setdefault(k, []).append(me)
        for k in writes:
            self.last_w[k] = me
            self.readers[k] = []
        return me

    def op(self, eng, name, *args, r=(), w=(), **kw):
        def fn(e, name=name, args=args, kw=kw):
            return getattr(e, name)(*args, **kw)
        return self.add(eng, fn, r, w)

    def pe(self, name, *args, r=(), w=(), **kw): return self.op('pe', name, *args, r=r, w=w, **kw)
    def act(self, name, *args, r=(), w=(), **kw): return self.op('act', name, *args, r=r, w=w, **kw)
    def dve(self, name, *args, r=(), w=(), **kw): return self.op('dve', name, *args, r=r, w=w, **kw)
    def pool(self, name, *args, r=(), w=(), **kw): return self.op('pool', name, *args, r=r, w=w, **kw)

    def dma(self, out, in_, r=(), w=(), q='sp', **kw):
        def fn(e, out=out, in_=in_, kw=kw):
            return e.dma_start(out=out, in_=in_, **kw)
        return self.add(q, fn, r, w, dma=True)

    def emit(self):
        nc = self.nc
        sems = []

        def new_sem(name):
            h = nc.alloc_semaphore(name=name)
            sems.append(h)
            return h
        with contextlib.ExitStack() as st:
            csem = {}
            for e in self.ENGS:
                nep = (self.ncomp[e] + self.EP - 1) // self.EP
                csem[e] = [new_sem(f"c_{self.tag}_{e}_{i}") for i in range(nep)]
            dsem = {}
            for e in self.ENGS:
                if self.ndma[e]:
                    dsem[e] = [new_sem(f"d_{self.tag}_{e}_{i}") for i in range(min(self.R, self.ndma[e]))]
            block = st.enter_context(nc.Block())
            engobj = {'pe': block.tensor, 'act': block.scalar, 'dve': block.vector,
                      'pool': block.gpsimd, 'sp': block.sync}
            ops, ndma, EP, R = self.ops, self.ndma, self.EP, self.R

            def make(e):
                def body(eng):
                    waited_c = {}
                    waited_d = {}

                    def wait_dep(d):
                        kind, D, n = d
                        if kind == 'c':
                            if D == e and e == 'pe':
                                return
                            if waited_c.get(D, -1) >= n:
                                return
                            waited_c[D] = n
                            eng.wait_ge(csem[D][n // EP], (n % EP) + 1)
                        else:
                            slot = n % R
                            val = 16 * (n // R + 1)
                            if waited_d.get((D, slot), 0) >= val:
                                return
                            waited_d[(D, slot)] = val
                            eng.wait_ge(dsem[D][slot], val)

                    for (me, fn, deps) in ops[e]:
                        red = {}
                        for d in deps:
                            kk_ = (d[0], d[1]) if d[0] == 'c' else (d[0], d[1], d[2] % R)
                            if kk_ not in red or red[kk_][2] < d[2]:
                                red[kk_] = d
                        for d in sorted(red.values()):
                            wait_dep(d)
                        kind, _, n = me
                        if kind == 'd':
                            if n >= R:
                                wait_dep(('d', e, n - R))
                            fn(eng).then_inc(dsem[e][n % R], 16)
                        else:
                            fn(eng).then_inc(csem[e][n // EP], 1)
                    for j in range(max(0, ndma[e] - R), ndma[e]):
                        wait_dep(('d', e, j))
                return body
            for e in self.ENGS:
                if ops[e]:
                    engobj[e](make(e))
        nc.clear_and_free_semaphores(sems)
        nc.all_engine_barrier()


class Ctx:
    def __init__(self, nc, tag):
        self.nc = nc
        self.tag = tag
        self.st = contextlib.ExitStack()
        self.S = Sched(self.nc, tag)
        self.nps = 0
        self.nps_side = 0
        self.ps = None

    def din(self, name, shape, dt=F32):
        return self.nc.dram_tensor(name, list(shape), dt, kind="ExternalInput").ap()

    def dout(self, name, shape, dt=F32):
        return self.nc.dram_tensor(name, list(shape), dt, kind="ExternalOutput").ap()

    def sb(self, name, shape, dt=F32):
        return self.st.enter_context(self.nc.sbuf_tensor(f"sb_{self.tag}_{name}", list(shape), dt))

    def init_psum(self):
        self.ps = [self.st.enter_context(self.nc.psum_tensor(f"ps_{self.tag}_{i}", [128, 512], F32)) for i in range(8)]

    nmain = 8

    def pst(self):
        i = self.nps % self.nmain
        self.nps += 1
        return self.ps[i], f"ps{i}"

    def pst_side(self):
        n = 8 - self.nmain
        i = self.nmain + (self.nps_side % n)
        self.nps_side += 1
        return self.ps[i], f"ps{i}"

    def finish(self):
        self.S.emit()
        self.st.close()


class PrefSched:
    def __init__(self, S, pfx, defer=False):
        self.S, self.pfx, self.defer = S, pfx, defer

    def _k(self, keys):
        return [k if k.startswith('ps') else self.pfx + k for k in keys]

    def op(self, eng, name, *args, r=(), w=(), **kw):
        if self.defer:
            def fn(e, name=name, args=args, kw=kw):
                return getattr(e, name)(*args, **kw)
            self.S.side.append((eng, fn, self._k(r), self._k(w), False))
            return None
        return self.S.op(eng, name, *args, r=self._k(r), w=self._k(w), **kw)

    def pe(self, name, *args, r=(), w=(), **kw): return self.op('pe', name, *args, r=r, w=w, **kw)
    def act(self, name, *args, r=(), w=(), **kw): return self.op('act', name, *args, r=r, w=w, **kw)
    def dve(self, name, *args, r=(), w=(), **kw): return self.op('dve', name, *args, r=r, w=w, **kw)
    def pool(self, name, *args, r=(), w=(), **kw): return self.op('pool', name, *args, r=r, w=w, **kw)

    def dma(self, out, in_, r=(), w=(), q='sp', **kw):
        if self.defer:
            def fn(e, out=out, in_=in_, kw=kw):
                return e.dma_start(out=out, in_=in_, **kw)
            self.S.side.append((q, fn, self._k(r), self._k(w), True))
            return None
        return self.S.dma(out, in_, r=self._k(r), w=self._k(w), q=q, **kw)


class SubCtx:
    def __init__(self, C, pfx, defer=False):
        self.C, self.pfx = C, pfx
        self.nc = C.nc
        self.S = PrefSched(C.S, pfx, defer)

    def sb(self, name, shape, dt=F32):
        return self.C.sb(self.pfx + name, shape, dt)

    def pst(self):
        return self.C.pst_side() if self.S.defer else self.C.pst()


def emit_mod(C, modw, modb_sb, cT_sb, ncols, out_sb):
    S = C.S
    sc = C.sb("silu_c", [128, 8])
    wbuf = [C.sb("modw0", [128, 8, 256]), C.sb("modw1", [128, 8, 256])]
    S.act('activation', out=sc[:], in_=cT_sb[:], func=AF.Silu, r=['cT'], w=['silu_c'])
    nj = ncols // 128
    pt, pk = C.pst()
    wv = modw.rearrange("(k p) c -> p k c", p=128)
    for blk in range(ncols // 256):
        wt = wbuf[blk % 2]
        wkey = f"modw{blk % 2}"
        S.dma(out=wt[:], in_=wv[:, :, blk * 256:(blk + 1) * 256], w=[wkey])
        for jj in range(2):
            j = blk * 2 + jj
            for k in range(8):
                S.pe('matmul',
                    pt[:, j:j + 1], wt[:, k, jj * 128:(jj + 1) * 128], sc[:, k:k + 1],
                    start=(k == 0), stop=(k == 7), r=[wkey, 'silu_c'], w=[pk])
    S.dve('tensor_tensor', out=out_sb[:], in0=pt[:, 0:nj], in1=modb_sb[:], op=ALU.add,
          r=[pk, 'modb'], w=['mod'])


MERGE_LRU = True
SIDE_EVERY = 4
TS = 256
NSEG = SEQ // TS
NCH = TS // 64
VEC_NAMES = ['mu_r', 'mu_k', 'mu_v', 'w0', 'a0', 'kk', 'ka', 'rk', 'lnw', 'lnb', 'v0']


def vcol(name, p):
    return VEC_NAMES.index(name) * 4 + p


V_MUZW, V_MUZA, V_MUZG, V_MUZV1 = 44, 45, 46, 47
NVEC = 48
C_ID, C_MTS, C_MST, C_MSTI, C_RST, C_EYE, C_BONES = 0, 64, 64 + TS, 64 + 2 * TS, 64 + 3 * TS, 64 + 4 * TS, 64 + 5 * TS
C_HM = C_BONES + 128
C_H = C_HM + 2
C_OH = C_H + 6 * TS
NCST = C_OH + 2 * TS


def rwkv_consts():
    c = np.zeros((128, NCST), np.float32)
    pp = np.arange(128)[:, None] % 64
    f = np.arange(TS)[None, :] % 64
    c[:, C_ID:C_ID + 64] = (pp == np.arange(64)[None, :])
    c[:, C_MTS:C_MTS + TS] = (pp > f)
    c[:, C_MST:C_MST + TS] = (f > pp)
    c[:, C_MSTI:C_MSTI + TS] = (f >= pp)
    c[:, C_RST:C_RST + TS] = (f != 0)
    c[:, C_EYE:C_EYE + TS] = (f == pp)
    c[:, C_BONES:C_BONES + 128] = (np.arange(128)[:, None] // 64 == np.arange(128)[None, :] // 64)
    hm = np.stack([(np.arange(128) < 64), (np.arange(128) >= 64)], axis=1).astype(np.float32)
    c[:, C_HM:C_HM + 2] = hm
    for bi, base in enumerate((C_EYE, C_MTS, C_MST)):
        for e in range(2):
            c[:, C_H + (bi * 2 + e) * TS:C_H + (bi * 2 + e + 1) * TS] = c[:, base:base + TS] * hm[:, e:e + 1]
    for e in range(2):
        c[:, C_OH + e * TS:C_OH + (e + 1) * TS] = hm[:, e:e + 1]
    return c


def x_pieces(io, l, k, ta, tb):
    if l == 0:
        return [(0, tb - ta, io['xT'][k * 128:(k + 1) * 128, ta:tb])]
    out = []
    for r in range(2):
        a_, b_ = max(ta, r * 2048), min(tb, (r + 1) * 2048)
        if a_ < b_:
            rows = r * 256 + (k % 2) * 128
            out.append((a_ - ta, b_ - a_, io[f'xg{k // 2}'][rows:rows + 128, a_ - r * 2048:b_ - r * 2048]))
    return out


def load_mod(C, io, layer):
    mod_sb = C.sb("mod", [128, 48])
    C.S.dma(out=mod_sb[:], in_=io['modv'][layer], w=['mod'])
    return mod_sb


def emit_rwkv(nc, io, layer):
    C = Ctx(nc, f"rw{layer}")
    S = C.S
    NCW = 1792 + (32 if layer else 0)
    wc = io[f'wc_{layer}']; vec = io[f'vec_{layer}']
    w2c = io[f'w2c_{layer}']; a2c = io[f'a2c_{layer}']; g2c = io[f'g2c_{layer}']
    cst = io['cst']
    vfs = io['vfs']
    if layer:
        v2c = io['v2c_1']
    C.init_psum()

    def ld(name, ap, shape, q='sp'):
        t = C.sb(name, shape)
        S.dma(out=t[:], in_=ap, w=[name], q=q)
        return t
    vec_sb = ld("vec", vec, [128, NVEC])
    w2_sb = ld("w2c", w2c, [64, 512])
    a2_sb = ld("a2c", a2c, [64, 512])
    g2_sb = ld("g2c", g2c, [128, 512])
    cst_sb = ld("cst", cst, [128, NCST])
    if layer:
        v2_sb = ld("v2c", v2c, [32, 512])
    wcb = C.sb("wcb", [128, 8, NCW], BF16)
    wcv = wc.rearrange("(k p) c -> p k c", p=128)
    c0 = 0
    while c0 < NCW:
        n = min(512, NCW - c0)
        S.dma(out=wcb[:, :, c0:c0 + n], in_=wcv[:, :, c0:c0 + n], w=['wcb'], q='pool')
        c0 += n
    mod_sb = load_mod(C, io, layer)
    omv = C.sb("omv", [128, NVEC])
    S.dve('tensor_scalar', out=omv[:], in0=vec_sb[:], scalar1=-1.0, scalar2=1.0, op0=ALU.mult, op1=ALU.add,
          r=['vec'], w=['omv'])

    def V(name, p, rows=128):
        c = vcol(name, p)
        return vec_sb[0:rows, c:c + 1]

    ident = cst_sb[:, C_ID:C_ID + 64]
    m_ts = cst_sb[:, C_MTS:C_MTS + TS]
    m_st = cst_sb[:, C_MST:C_MST + TS]
    m_sti = cst_sb[:, C_MSTI:C_MSTI + TS]
    rst = cst_sb[:, C_RST:C_RST + TS]
    eye = cst_sb[:, C_EYE:C_EYE + TS]
    bones = cst_sb[:, C_BONES:C_BONES + 128]

    hs = C.sb("hs", [128, 8, TS], BF16)
    xk = [C.sb("xk0", [128, TS]), C.sb("xk1", [128, TS])]
    zwp = C.sb("zwp", [64, TS + 1]); tzw = C.sb("tzw", [64, TS])
    zap = C.sb("zap", [64, TS + 1])
    zgp = C.sb("zgp", [128, TS + 1]); szg = C.sb("szg", [128, TS])
    if layer:
        zvp = C.sb("zvp", [32, TS + 1])
    tmpsh = C.sb("tmpsh", [128, TS])
    zr = C.sb("zr", [128, TS + 1]); zk = C.sb("zk", [128, TS + 1]); zv = C.sb("zv", [128, TS + 1])
    Wt = {n: C.sb("w_" + n, [128, TS]) for n in
          ['ld', 'cl', 'a', 'kk', 'sq', 'kkn', 'km', 'epos', 'eneg', 'eprev', 'Bt', 'Kt', 'rkr', 'lam', 'vf']}
    PP = [{n: C.sb(f"{n}{p}", [128, TS]) for n in ['At', 'Rt', 'AKT', 'RBT', 'RKT', 'TT', 'bon', 'gT']}
          for p in range(4)]
    TOK = [{n: C.sb(f"{n}{p}", [128, NCH, 64]) for n in ['Btok', 'Ktok', 'Vtok']} for p in range(4)]
    gam = C.sb("gam", [128, 4, NCH])
    Nm = C.sb("Nm", [128, TS]); NT = C.sb("NT", [128, TS])
    Pw = [C.sb("Pw0", [128, TS]), C.sb("Pw1", [128, TS])]
    PTw = [C.sb("PTw0", [128, TS]), C.sb("PTw1", [128, TS])]
    TTw = [C.sb("TTw0", [128, TS]), C.sb("TTw1", [128, TS])]
    BD = {n: C.sb("bd_" + n, [128, NCH, 128]) for n in ['At', 'Bt', 'Kt', 'V', 'P0', 'P1', 'PT0', 'PT1', 'IP']}
    Hb = [C.sb("Hb0", [128, 256]), C.sb("Hb1", [128, 256])]
    Xsb = C.sb("Xsb", [128, 256]); Usb = C.sb("Usb", [128, 256]); Htmp = C.sb("Htmp", [128, 256])
    Ysb = C.sb("Ysb", [128, NCH, 4, 64]); dY = C.sb("dY", [128, NCH, 4, 64]); sqY = C.sb("sqY", [128, NCH, 4, 64])
    st1 = C.sb("st1", [128, NCH * 4]); st2 = C.sb("st2", [128, NCH * 4])
    ycs = [C.sb(f"ycs{p}", [128, TS]) for p in range(4)]
    S.pool('memset', Hb[0][:], 0.0, w=['Hb0'])

    zl = C.sb("zl", [128, 16])
    S.pool('memset', zl[:], 0.0, w=['zl'])

    def proj(cols0, M, dst, key, zid):
        pt, pk = C.pst()
        for k in range(8):
            S.pe('matmul', pt[0:M, 0:TS], wcb[:, k, cols0:cols0 + M], hs[:, k, :], start=(k == 0), stop=(k == 7),
                 r=['wcb', 'hs'], w=[pk])
        S.act('activation', out=dst[0:M, 1:TS + 1], in_=pt[0:M, 0:TS], func=AF.Identity, r=[pk], w=[key])
        S.pool('tensor_copy', out=dst[0:M, 0:1], in_=zl[0:M, zid:zid + 1], r=['zl'], w=[key])
        S.pool('tensor_copy', out=zl[0:M, zid:zid + 1], in_=dst[0:M, TS:TS + 1], r=[key], w=['zl'])

    def shift(z, key, M, mucol):
        S.dve('tensor_scalar', out=tmpsh[0:M, :], in0=z[0:M, 0:TS], scalar1=vec_sb[0:M, mucol:mucol + 1],
                                        scalar2=None, op0=ALU.mult, r=[key, 'vec'], w=['tmpsh'])
        S.dve('scalar_tensor_tensor', out=z[0:M, 1:TS + 1], in0=z[0:M, 1:TS + 1],
                                               scalar=omv[0:M, mucol:mucol + 1], in1=tmpsh[0:M, :],
                                               op0=ALU.mult, op1=ALU.add, r=[key, 'omv', 'tmpsh'], w=[key])

    hm = [cst_sb[:, C_HM:C_HM + 1], cst_sb[:, C_HM + 1:C_HM + 2]]
    cH = lambda bi, e_: cst_sb[:, C_H + (bi * 2 + e_) * TS:C_H + (bi * 2 + e_ + 1) * TS]
    v3 = lambda ap: ap.rearrange("q (c t) -> q c t", t=64)

    def product_bd(Lbd, Lk, R, Rk, roff=0):
        pt, pk = C.pst()
        for c in range(NCH):
            S.pe('matmul', pt[:, c * 64:(c + 1) * 64], Lbd[:, c, :], R[:, roff + c * 64:roff + (c + 1) * 64],
                 start=True, stop=True, r=[Lk, Rk], w=[pk])
        return pt, pk

    def to_bd(eng, dst, dkey, src, skeys, kind='copy'):
        for e_ in range(2):
            o = dst[:, :, e_ * 64:(e_ + 1) * 64]
            if kind == 'copy' and eng == 'pool':
                S.pool('tensor_tensor', out=o, in0=v3(src), in1=v3(cst_sb[:, C_OH + e_ * TS:C_OH + (e_ + 1) * TS]), op=ALU.mult,
                       r=list(skeys) + ['cst'], w=[dkey])
            elif kind == 'copy':
                if eng == 'act':
                    S.act('activation', out=o, in_=v3(src), func=AF.Identity, scale=hm[e_], r=list(skeys) + ['cst'], w=[dkey])
                else:
                    S.op(eng, 'tensor_scalar', out=o, in0=v3(src), scalar1=hm[e_], scalar2=None, op0=ALU.mult,
                         r=list(skeys) + ['cst'], w=[dkey])
            elif kind == 'eye':
                S.dve('scalar_tensor_tensor', out=o, in0=v3(src), scalar=hm[e_], in1=v3(cH(0, e_)), op0=ALU.mult, op1=ALU.add,
                      r=list(skeys) + ['cst'], w=[dkey])
            else:
                S.dve('tensor_tensor', out=o, in0=v3(src), in1=v3(cH(kind, e_)), op=ALU.mult, r=list(skeys) + ['cst'], w=[dkey])

    if MERGE_LRU:
        C.nmain = 6
        for _ in lru_body(SubCtx(C, 'L_', defer=True), io, layer):
            pass
        S.side_every = SIDE_EVERY
    lg = iter(())
    cur = 0
    for sg in range(NSEG):
        t0 = sg * TS
        for k in range(8):
            xb = xk[k % 2]
            xkey = f"xk{k % 2}"
            for (o_, n_, src_) in x_pieces(io, layer, k, t0, t0 + TS):
                S.dma(out=xb[:, o_:o_ + n_], in_=src_, w=[xkey], allow_slow_non_contiguous=(n_ < 2))
            S.act('activation', out=hs[:, k, :], in_=xb[:], func=AF.Identity,
                  scale=mod_sb[:, 8 + k:9 + k], bias=mod_sb[:, k:k + 1], r=[xkey, 'mod'], w=['hs'])
        proj(1536, 64, zwp, 'zwp', 0); shift(zwp, 'zwp', 64, V_MUZW)
        S.act('activation', out=tzw[:], in_=zwp[:, 1:TS + 1], func=AF.Tanh, r=['zwp'], w=['tzw'])
        proj(1600, 64, zap, 'zap', 1); shift(zap, 'zap', 64, V_MUZA)
        proj(1664, 128, zgp, 'zgp', 2); shift(zgp, 'zgp', 128, V_MUZG)
        S.act('activation', out=szg[:], in_=zgp[:, 1:TS + 1], func=AF.Sigmoid, r=['zgp'], w=['szg'])
        if layer:
            proj(1792, 32, zvp, 'zvp', 3); shift(zvp, 'zvp', 32, V_MUZV1)
        for p in range(4):
            P_ = PP[p]
            K_ = lambda n, p=p: f"{n}{p}"
            proj(p * 128, 128, zr, 'zr', 4 + p); shift(zr, 'zr', 128, vcol('mu_r', p))
            proj(512 + p * 128, 128, zk, 'zk', 8 + p); shift(zk, 'zk', 128, vcol('mu_k', p))
            proj(1024 + p * 128, 128, zv, 'zv', 12 + p); shift(zv, 'zv', 128, vcol('mu_v', p))
            rs, ks, vs = zr[:, 1:TS + 1], zk[:, 1:TS + 1], zv[:, 1:TS + 1]
            pc = slice(p * 128, (p + 1) * 128)
            if layer == 0:
                S.dma(out=vfs[p * 128:(p + 1) * 128, t0:t0 + TS], in_=zv[:, 1:TS + 1], r=['zv'])
            else:
                pt, pk = C.pst()
                S.pe('matmul', pt[:, 0:TS], v2_sb[:, pc], zvp[:, 1:TS + 1], start=True, stop=True,
                     r=['v2c', 'zvp'], w=[pk])
                S.act('activation', out=Wt['lam'][:], in_=pt[:, 0:TS], func=AF.Sigmoid, bias=V('v0', p),
                      r=[pk, 'vec'], w=['w_lam'])
                S.dma(out=Wt['vf'][:], in_=vfs[p * 128:(p + 1) * 128, t0:t0 + TS], w=['w_vf'])
                S.dve('tensor_tensor', out=Wt['vf'][:], in0=Wt['vf'][:], in1=zv[:, 1:TS + 1], op=ALU.subtract,
                      r=['w_vf', 'zv'], w=['w_vf'])
                S.dve('tensor_tensor', out=Wt['vf'][:], in0=Wt['vf'][:], in1=Wt['lam'][:], op=ALU.mult,
                      r=['w_vf', 'w_lam'], w=['w_vf'])
                S.dve('tensor_tensor', out=zv[:, 1:TS + 1], in0=zv[:, 1:TS + 1], in1=Wt['vf'][:], op=ALU.add,
                      r=['w_vf', 'zv'], w=['zv'])
            pt, pk = C.pst()
            S.pe('matmul', pt[:, 0:TS], w2_sb[:, pc], tzw[:], start=True, stop=True,
                 r=['w2c', 'tzw'], w=[pk])
            S.act('activation', out=Wt['ld'][:], in_=pt[:, 0:TS], func=AF.Sigmoid, bias=V('w0', p),
                  r=[pk, 'vec'], w=['w_ld'])
            S.dve('tensor_scalar', out=Wt['ld'][:], in0=Wt['ld'][:], scalar1=-float(np.exp(-0.5)), scalar2=None,
                                            op0=ALU.mult, r=['w_ld'], w=['w_ld'])
            S.dve('tensor_tensor_scan', out=Wt['cl'][:], data0=rst, data1=Wt['ld'][:], initial=0.0,
                                                 op0=ALU.mult, op1=ALU.add, r=['w_ld', 'cst'], w=['w_cl'])
            pt, pk = C.pst()
            S.pe('matmul', pt[:, 0:TS], a2_sb[:, pc], zap[:, 1:TS + 1], start=True, stop=True,
                 r=['a2c', 'zap'], w=[pk])
            S.act('activation', out=Wt['a'][:], in_=pt[:, 0:TS], func=AF.Sigmoid, bias=V('a0', p),
                  r=[pk, 'vec'], w=['w_a'])
            pt, pk = C.pst()
            S.pe('matmul', pt[:, 0:TS], g2_sb[:, pc], szg[:], start=True, stop=True,
                 r=['g2c', 'szg'], w=[pk])
            S.act('activation', out=P_['gT'][:], in_=pt[:, 0:TS], func=AF.Identity, r=[pk], w=[K_('gT')])
            S.dve('tensor_scalar', out=Wt['kk'][:], in0=zk[:, 1:TS + 1], scalar1=V('kk', p), scalar2=None,
                                                 op0=ALU.mult, r=['zk', 'vec'], w=['w_kk'])
            S.pool('tensor_tensor', out=Wt['sq'][:], in0=Wt['kk'][:], in1=Wt['kk'][:], op=ALU.mult,
                   r=['w_kk'], w=['w_sq'])
            pt, pk = C.pst()
            S.pe('matmul', pt[:, 0:TS], bones, Wt['sq'][:], start=True, stop=True, r=['cst', 'w_sq'], w=[pk])
            S.act('activation', out=Wt['sq'][:], in_=pt[:, 0:TS], func=AF.Sqrt, bias=1e-12, r=[pk], w=['w_sq'])
            S.dve('reciprocal', out=Wt['sq'][:], in_=Wt['sq'][:], r=['w_sq'], w=['w_sq'])
            S.pool('tensor_tensor', out=Wt['kkn'][:], in0=Wt['kk'][:], in1=Wt['sq'][:], op=ALU.mult,
                  r=['w_kk', 'w_sq'], w=['w_kkn'])
            S.dve('tensor_scalar', out=Wt['km'][:], in0=Wt['a'][:], scalar1=V('ka', p),
                                                 scalar2=omv[:, vcol('ka', p):vcol('ka', p) + 1], op0=ALU.mult, op1=ALU.add,
                  r=['w_a', 'vec', 'omv'], w=['w_km'])
            S.pool('tensor_tensor', out=Wt['km'][:], in0=Wt['km'][:], in1=zk[:, 1:TS + 1], op=ALU.mult,
                  r=['w_km', 'zk'], w=['w_km'])
            S.act('activation', out=Wt['epos'][:], in_=Wt['cl'][:], func=AF.Exp, r=['w_cl'], w=['w_epos'])
            S.act('activation', out=Wt['eneg'][:], in_=Wt['cl'][:], func=AF.Exp, scale=-1.0, r=['w_cl'], w=['w_eneg'])
            S.pool('tensor_tensor', out=Wt['eprev'][:], in0=Wt['cl'][:], in1=Wt['ld'][:], op=ALU.subtract,
                  r=['w_cl', 'w_ld'], w=['w_eprev'])
            S.act('activation', out=Wt['eprev'][:], in_=Wt['eprev'][:], func=AF.Exp, r=['w_eprev'], w=['w_eprev'])
            S.dve('scalar_tensor_tensor', out=P_['At'][:], in0=Wt['kkn'][:], scalar=-1.0, in1=Wt['eprev'][:],
                                                   op0=ALU.mult, op1=ALU.mult, r=['w_kkn', 'w_eprev'], w=[K_('At')])
            S.pool('tensor_tensor', out=Wt['Bt'][:], in0=Wt['kkn'][:], in1=Wt['a'][:], op=ALU.mult,
                   r=['w_kkn', 'w_a'], w=['w_Bt'])
            S.pool('tensor_tensor', out=Wt['Bt'][:], in0=Wt['Bt'][:], in1=Wt['eneg'][:], op=ALU.mult,
                  r=['w_Bt', 'w_eneg'], w=['w_Bt'])
            S.pool('tensor_tensor', out=Wt['Kt'][:], in0=Wt['km'][:], in1=Wt['eneg'][:], op=ALU.mult,
                   r=['w_km', 'w_eneg'], w=['w_Kt'])
            S.pool('tensor_tensor', out=P_['Rt'][:], in0=zr[:, 1:TS + 1], in1=Wt['epos'][:], op=ALU.mult,
                  r=['zr', 'w_epos'], w=[K_('Rt')])
            S.dve('scalar_tensor_tensor', out=Wt['rkr'][:], in0=zr[:, 1:TS + 1], scalar=V('rk', p), in1=Wt['km'][:],
                                                        op0=ALU.mult, op1=ALU.mult, r=['zr', 'vec', 'w_km'], w=['w_rkr'])
            pt, pk = C.pst()
            S.pe('matmul', pt[:, 0:TS], bones, Wt['rkr'][:], start=True, stop=True, r=['cst', 'w_rkr'], w=[pk])
            S.dve('tensor_tensor', out=P_['bon'][:], in0=pt[:, 0:TS], in1=zv[:, 1:TS + 1], op=ALU.mult,
                  r=[pk, 'zv'], w=[K_('bon')])
            S.dve('tensor_copy', out=gam[:, p, :], in_=Wt['epos'][:].rearrange("q (c t) -> q c t", t=64)[:, :, 63],
                  r=['w_epos'], w=['gam'])
            to_bd('act', BD['At'], 'bd_At', P_['At'][:], [K_('At')])
            to_bd('pool', BD['Bt'], 'bd_Bt', Wt['Bt'][:], ['w_Bt'])
            to_bd('act', BD['Kt'], 'bd_Kt', Wt['Kt'][:], ['w_Kt'])
            to_bd('pool', BD['V'], 'bd_V', zv[:, 1:TS + 1], ['zv'])
            for (bdn, nm) in (('Bt', 'Btok'), ('Kt', 'Ktok'), ('V', 'Vtok')):
                pt, pk = C.pst()
                for c in range(NCH):
                    S.pe('matmul', pt[:, c * 64:(c + 1) * 64], BD[bdn][:, c, :], ident, start=True, stop=True,
                         r=['bd_' + bdn, 'cst'], w=[pk])
                S.act('activation', out=TOK[p][nm][:], in_=v3(pt[:, 0:TS]), func=AF.Identity, r=[pk], w=[K_(nm)])
            pt, pk = product_bd(BD['At'], 'bd_At', Wt['Bt'], 'w_Bt')
            S.dve('tensor_tensor', out=Nm[:], in0=pt[:, 0:TS], in1=m_ts, op=ALU.mult, r=[pk, 'cst'], w=['Nm'])
            to_bd('dve', BD['P0'], 'bd_P0', pt[:, 0:TS], [pk], kind=1)
            pt, pk = product_bd(BD['Bt'], 'bd_Bt', P_['At'], K_('At'))
            S.dve('tensor_tensor', out=NT[:], in0=pt[:, 0:TS], in1=m_st, op=ALU.mult, r=[pk, 'cst'], w=['NT'])
            to_bd('dve', BD['PT0'], 'bd_PT0', pt[:, 0:TS], [pk], kind=2)
            pt, pk = product_bd(BD['Kt'], 'bd_Kt', P_['At'], K_('At'))
            S.dve('tensor_tensor', out=P_['AKT'][:], in0=pt[:, 0:TS], in1=m_st, op=ALU.mult, r=[pk, 'cst'], w=[K_('AKT')])
            pt, pk = product_bd(BD['Bt'], 'bd_Bt', P_['Rt'], K_('Rt'))
            S.dve('tensor_tensor', out=P_['RBT'][:], in0=pt[:, 0:TS], in1=m_sti, op=ALU.mult, r=[pk, 'cst'], w=[K_('RBT')])
            pt, pk = product_bd(BD['Kt'], 'bd_Kt', P_['Rt'], K_('Rt'))
            S.dve('tensor_tensor', out=P_['RKT'][:], in0=pt[:, 0:TS], in1=m_sti, op=ALU.mult, r=[pk, 'cst'], w=[K_('RKT')])
            S.pool('tensor_tensor', out=TTw[0][:], in0=NT[:], in1=eye, op=ALU.add, r=['NT', 'cst'], w=['TTw0'])
            Pc, Pck, PTc, PTck = Nm, 'Nm', NT, 'NT'
            Pbd, Pbdk, PTbd, PTbdk = BD['P0'], 'bd_P0', BD['PT0'], 'bd_PT0'
            tcur = 0
            for lvl in range(5):
                nb = (lvl + 1) % 2
                pt, pk = product_bd(PTbd, PTbdk, Pc, Pck)
                to_bd('dve', BD['IP'], 'bd_IP', pt[:, 0:TS], [pk], kind='eye')
                if lvl < 4:
                    Pn, Pnk = Pw[nb], f"Pw{nb}"
                    S.dve('tensor_copy', out=Pn[:], in_=pt[:, 0:TS], r=[pk], w=[Pnk])
                    Pnbd, Pnbdk = BD[f'P{nb}'], f'bd_P{nb}'
                    to_bd('dve', Pnbd, Pnbdk, pt[:, 0:TS], [pk])
                    pt2, pk2 = product_bd(Pbd, Pbdk, PTc, PTck)
                    PTn, PTnk = PTw[nb], f"PTw{nb}"
                    S.act('activation', out=PTn[:], in_=pt2[:, 0:TS], func=AF.Identity, r=[pk2], w=[PTnk])
                    PTnbd, PTnbdk = BD[f'PT{nb}'], f'bd_PT{nb}'
                    to_bd('act', PTnbd, PTnbdk, pt2[:, 0:TS], [pk2])
                if lvl < 4:
                    dstT, dstk = TTw[1 - tcur], f"TTw{1 - tcur}"
                else:
                    dstT, dstk = P_['TT'], K_('TT')
                pt3, pk3 = product_bd(BD['IP'], 'bd_IP', TTw[tcur], f"TTw{tcur}")
                S.act('activation', out=dstT[:], in_=pt3[:, 0:TS], func=AF.Identity, r=[pk3], w=[dstk])
                tcur = 1 - tcur
                if lvl < 4:
                    Pc, Pck, PTc, PTck = Pn, Pnk, PTn, PTnk
                    Pbd, Pbdk, PTbd, PTbdk = Pnbd, Pnbdk, PTnbd, PTnbdk
            next(lg, None)
        for c in range(NCH):
            sl = slice(c * 64, (c + 1) * 64)
            Hc, Hck = Hb[cur], f"Hb{cur}"
            Hn, Hnk = Hb[1 - cur], f"Hb{1 - cur}"
            psX = []
            for e_ in range(2):
                pb = 64 * e_
                pt, pk = C.pst()
                psX.append((pt, pk))
                for p in range(4):
                    cs = slice(p * 64, (p + 1) * 64)
                    S.pe('matmul', pt[pb:pb + 64, cs], PP[p]['At'][pb:pb + 64, sl],
                                                                      Hc[pb:pb + 64, cs], start=True, stop=False,
                         r=[f"At{p}", Hck], w=[pk])
                    S.pe('matmul', pt[pb:pb + 64, cs], PP[p]['AKT'][pb:pb + 64, sl],
                                                                      TOK[p]['Vtok'][pb:pb + 64, c, :], start=False, stop=True,
                         r=[f"AKT{p}", f"Vtok{p}"], w=[pk])
                S.act('activation', out=Xsb[pb:pb + 64, :], in_=pt[pb:pb + 64, 0:256], func=AF.Identity,
                      r=[pk], w=['Xsb'])
            psU = []
            for e_ in range(2):
                pb = 64 * e_
                pt, pk = C.pst()
                for p in range(4):
                    cs = slice(p * 64, (p + 1) * 64)
                    S.pe('matmul', pt[pb:pb + 64, cs], PP[p]['TT'][pb:pb + 64, sl],
                                                                      Xsb[pb:pb + 64, cs], start=True, stop=True,
                         r=[f"TT{p}", 'Xsb'], w=[pk])
                S.dve('tensor_copy', out=Usb[pb:pb + 64, :], in_=pt[pb:pb + 64, 0:256], r=[pk], w=['Usb'])
            for e_ in range(2):
                pb = 64 * e_
                pt, pk = C.pst()
                for p in range(4):
                    cs = slice(p * 64, (p + 1) * 64)
                    S.pe('matmul', pt[pb:pb + 64, cs], TOK[p]['Btok'][pb:pb + 64, c, :],
                                                                      Usb[pb:pb + 64, cs], start=True, stop=False,
                         r=[f"Btok{p}", 'Usb'], w=[pk])
                    S.pe('matmul', pt[pb:pb + 64, cs], TOK[p]['Ktok'][pb:pb + 64, c, :],
                                                                      TOK[p]['Vtok'][pb:pb + 64, c, :], start=False, stop=True,
                         r=[f"Ktok{p}", f"Vtok{p}"], w=[pk])
                S.dve('tensor_tensor', out=Htmp[pb:pb + 64, :], in0=pt[pb:pb + 64, 0:256],
                                                              in1=Hc[pb:pb + 64, :], op=ALU.add, r=[pk, Hck], w=['Htmp'])
                S.pool('tensor_tensor',
                    out=Hn[pb:pb + 64, :].rearrange("q (a b) -> q a b", b=64),
                    in0=Htmp[pb:pb + 64, :].rearrange("q (a b) -> q a b", b=64),
                    in1=gam[pb:pb + 64, :, c:c + 1].broadcast_to([64, 4, 64]), op=ALU.mult,
                    r=['Htmp', 'gam'], w=[Hnk])
                pt, pk = C.pst()
                for p in range(4):
                    cs = slice(p * 64, (p + 1) * 64)
                    S.pe('matmul', pt[pb:pb + 64, cs], PP[p]['Rt'][pb:pb + 64, sl],
                                                                      Hc[pb:pb + 64, cs], start=True, stop=False,
                         r=[f"Rt{p}", Hck], w=[pk])
                    S.pe('matmul', pt[pb:pb + 64, cs], PP[p]['RBT'][pb:pb + 64, sl],
                                                                      Usb[pb:pb + 64, cs], start=False, stop=False,
                         r=[f"RBT{p}", 'Usb'], w=[pk])
                    S.pe('matmul', pt[pb:pb + 64, cs], PP[p]['RKT'][pb:pb + 64, sl],
                                                                      TOK[p]['Vtok'][pb:pb + 64, c, :], start=False, stop=True,
                         r=[f"RKT{p}", f"Vtok{p}"], w=[pk])
                S.act('activation',
                    out=Ysb[pb:pb + 64, c, :, :], in_=pt[pb:pb + 64, 0:256].rearrange("q (a b) -> q a b", b=64), func=AF.Identity,
                    r=[pk], w=['Ysb'])
            cur = 1 - cur
        Yv = Ysb[:].rearrange("q c p i -> q (c p) i")
        dv = dY[:].rearrange("q c p i -> q (c p) i")
        sv = sqY[:].rearrange("q c p i -> q (c p) i")
        G = NCH * 4
        S.dve('tensor_reduce', out=st1[:], in_=Yv, axis=AX.X, op=ALU.add, r=['Ysb'], w=['st1'])
        S.dve('tensor_scalar', out=st1[:], in0=st1[:], scalar1=-1.0 / 64, scalar2=None, op0=ALU.mult, r=['st1'], w=['st1'])
        S.dve('tensor_tensor', out=dv, in0=Yv, in1=st1[:].unsqueeze(2).broadcast_to([128, G, 64]), op=ALU.add,
              r=['Ysb', 'st1'], w=['dY'])
        S.pool('tensor_tensor', out=sv, in0=dv, in1=dv, op=ALU.mult, r=['dY'], w=['sqY'])
        S.dve('tensor_reduce', out=st2[:], in_=sv, axis=AX.X, op=ALU.add, r=['sqY'], w=['st2'])
        S.dve('tensor_scalar', out=st2[:], in0=st2[:], scalar1=1.0 / 64, scalar2=GN_EPS, op0=ALU.mult, op1=ALU.add,
              r=['st2'], w=['st2'])
        S.act('activation', out=st2[:], in_=st2[:], func=AF.Sqrt, r=['st2'], w=['st2'])
        S.dve('reciprocal', out=st2[:], in_=st2[:], r=['st2'], w=['st2'])
        S.dve('tensor_tensor', out=dv, in0=dv, in1=st2[:].unsqueeze(2).broadcast_to([128, G, 64]), op=ALU.mult,
              r=['dY', 'st2'], w=['dY'])
        for p in range(4):
            for e_ in range(2):
                pb = 64 * e_
                pt, pk = C.pst()
                for c in range(NCH):
                    S.pe('matmul', pt[pb:pb + 64, c * 64:(c + 1) * 64], dY[pb:pb + 64, c, p, :],
                                                                    ident[pb:pb + 64, :], start=True, stop=True,
                         r=['dY', 'cst'], w=[pk])
                S.act('activation',
                    out=ycs[p][pb:pb + 64, :], in_=pt[pb:pb + 64, 0:TS], func=AF.Identity,
                    scale=vec_sb[pb:pb + 64, vcol('lnw', p):vcol('lnw', p) + 1],
                    bias=vec_sb[pb:pb + 64, vcol('lnb', p):vcol('lnb', p) + 1], r=[pk, 'vec'], w=[f"ycs{p}"])
            S.dve('tensor_tensor', out=ycs[p][:], in0=ycs[p][:], in1=PP[p]['bon'][:], op=ALU.add,
                  r=[f"ycs{p}", f"bon{p}"], w=[f"ycs{p}"])
            S.pool('tensor_tensor', out=ycs[p][:], in0=ycs[p][:], in1=PP[p]['gT'][:], op=ALU.mult,
                   r=[f"ycs{p}", f"gT{p}"], w=[f"ycs{p}"])
            S.dma(out=io[f"ys{5 + p}"][:, t0:t0 + TS], in_=ycs[p][:], r=[f"ycs{p}"])
    S.replay_side()
    C.finish()


def pair_rows(hh):
    idx = np.zeros(512, np.int64)
    for p in range(4):
        for e in range(2):
            h = hh * 8 + p + 4 * e
            idx[p * 128 + e * 64:p * 128 + e * 64 + 64] = h * 64 + np.arange(64)
    return idx


def arr128(v):
    return np.ascontiguousarray(v.reshape(-1, 128).T)


def rwkv_inputs(inp, l, b, hh, xT_b, vfT=None):
    idx = pair_rows(hh)
    w_in = inp['w_in'][l]
    base = 1024 + 1024 + 768 * 3
    cols = np.concatenate([base + idx, base + 1024 + idx, base + 2048 + idx,
                           base + 3072 + np.arange(64), base + 3136 + np.arange(64), base + 3200 + np.arange(128)])
    wc = w_in[:, cols]
    if l:
        wc = np.concatenate([wc, inp['w_in_vres'][l - 1]], axis=1)
    mu = inp['rwkv_mu'][l]
    vec = np.zeros((128, NVEC), np.float32)

    def setv(name, full):
        for p in range(4):
            vec[:, vcol(name, p)] = full[idx[p * 128:(p + 1) * 128]]
    setv('mu_r', mu[0:1024]); setv('mu_k', mu[1024:2048]); setv('mu_v', mu[2048:3072])
    setv('w0', inp['w0'][l]); setv('a0', inp['a0'][l]); setv('kk', inp['k_k'][l]); setv('ka', inp['k_a'][l])
    setv('rk', inp['r_k'][l].reshape(-1)); setv('lnw', inp['ln_x_w'][l]); setv('lnb', inp['ln_x_b'][l])
    if l:
        setv('v0', inp['v0'][l - 1])
        vec[0:32, V_MUZV1] = inp['mu_vres'][l - 1]
    vec[0:64, V_MUZW] = mu[3072:3136]
    vec[0:64, V_MUZA] = mu[3136:3200]
    vec[:, V_MUZG] = mu[3200:3328]
    d = {
        'xT': xT_b, 'cT': arr128(inp['c'][b]),
        'modw': np.ascontiguousarray(inp['mod_w'][l][:, 0:2048]),
        'modb': arr128(inp['mod_b'][l][0:2048]),
        'wc': np.ascontiguousarray(wc), 'vec': vec,
        'w2c': np.ascontiguousarray(inp['w2'][l][:, idx]), 'a2c': np.ascontiguousarray(inp['a2'][l][:, idx]),
        'g2c': np.ascontiguousarray(inp['g2'][l][:, idx]), 'cst': rwkv_consts(),
    }
    if l:
        d['v2c'] = np.ascontiguousarray(inp['v2'][l - 1][:, idx])
    return d


def common_inputs(C):
    xT = C.din("xT", [1024, SEQ])
    cT = C.din("cT", [128, 8])
    modw = C.din("modw", [1024, 2048])
    modb = C.din("modb", [128, 16])
    return xT, cT, modw, modb


def common_mod(C, cT, modw, modb):
    S = C.S
    cT_sb = C.sb("cT", [128, 8]); S.dma(out=cT_sb[:], in_=cT, w=['cT'])
    modb_sb = C.sb("modb", [128, 16]); S.dma(out=modb_sb[:], in_=modb, w=['modb'])
    mod_sb = C.sb("mod", [128, 16])
    emit_mod(C, modw, modb_sb, cT_sb, 2048, mod_sb)
    S.dve('tensor_scalar', out=mod_sb[:, 8:16], in0=mod_sb[:, 8:16], scalar1=1.0, scalar2=None, op0=ALU.add,
          r=['mod'], w=['mod'])
    return mod_sb


LT = 256
LV_NAMES = ['cw0', 'cw1', 'cw2', 'cw3', 'cb', 'ba', 'bx', 'lam']


def emit_lru(nc, io, layer):
    C = Ctx(nc, f"lr{layer}")
    C.init_psum()
    for _ in lru_body(C, io, layer):
        pass
    C.finish()


def lru_body(C, io, layer):
    S = C.S
    wa = io[f'wa_{layer}']; vecl = io[f'vecl_{layer}']; gw = io[f'gw_{layer}']
    mod_sb = load_mod(C, io, layer)
    vl = C.sb("vecl", [128, 32]); S.dma(out=vl[:], in_=vecl, w=['vecl'])
    gw_sb = C.sb("gw", [128, 8, 128]); S.dma(out=gw_sb[:], in_=gw, w=['gw'])
    wab = C.sb("wab", [128, 8, 1024], BF16)
    wav = wa.rearrange("(k p) c -> p k c", p=128)
    for c0 in (0, 512):
        S.dma(out=wab[:, :, c0:c0 + 512], in_=wav[:, :, c0:c0 + 512], w=['wab'], q='pool')
    if not _os.environ.get("KDEBUG_NOCAST"):
        for m in range(8):
            S.dma(out=io[f'WGb_{layer}'][m].rearrange("p (k c) -> p k c", k=8),
                  in_=io[f'WG_{layer}'][m].rearrange("(k p) c -> p k c", p=128), q='pool')
            S.dma(out=io[f'WPb_{layer}'][m].rearrange("p (k c) -> p k c", k=18),
                  in_=io[f'WP_{layer}'][m].rearrange("(k p) c -> p k c", p=128), q='pool')
            S.dma(out=io[f'WOb_{layer}'][m].rearrange("p (k c) -> p k c", k=8),
                  in_=io[f'WO_{layer}'].rearrange("(k p) c -> p k c", p=128)[:, :, m * 128:(m + 1) * 128], q='pool')
            S.dma(out=io[f'WDb_{layer}'][m].rearrange("p (k c) -> p k c", k=22),
                  in_=io[f'WD_{layer}'].rearrange("(k p) c -> p k c", p=128)[:, :, m * 128:(m + 1) * 128], q='pool')
        for j in range(22):
            S.dma(out=io[f'WUb_{layer}'][j].rearrange("p (k c) -> p k c", k=8),
                  in_=io[f'WU_{layer}'][j].rearrange("(k p) c -> p k c", p=128), q='pool')
    LVc = lambda n, j: vl[:, LV_NAMES.index(n) * 4 + j:LV_NAMES.index(n) * 4 + j + 1]
    clam = C.sb("clam", [128, 4])
    S.act('activation', out=clam[:], in_=vl[:, 28:32], func=AF.Exp, scale=-1.0, r=['vecl'], w=['clam'])
    S.act('activation', out=clam[:], in_=clam[:], func=AF.Ln, bias=1.0, r=['clam'], w=['clam'])
    S.dve('tensor_scalar', out=clam[:], in0=clam[:], scalar1=-8.0, scalar2=None, op0=ALU.mult, r=['clam'], w=['clam'])
    hs = C.sb("hs", [128, 8, LT], BF16)
    xk = [C.sb("xk0", [128, LT]), C.sb("xk1", [128, LT])]
    xal = C.sb("xal", [128, 4, 3])
    S.pool('memset', xal[:], 0.0, w=['xal'])
    xa = C.sb("xa", [128, LT + 3]); gg = C.sb("gg", [128, LT]); xc = C.sb("xc", [128, LT])
    rr = C.sb("rr", [128, LT]); ii = C.sb("ii", [128, LT]); aa = C.sb("aa", [128, LT]); om = C.sb("om", [128, LT])
    hq = C.sb("hq", [128, LT]); carry = C.sb("carry", [128, 4])
    S.pool('memset', carry[:], 0.0, w=['carry'])
    yield
    for sg in range(SEQ // LT):
        t0 = sg * LT
        for k in range(8):
            xb, xkey = xk[k % 2], f"xk{k % 2}"
            for (o_, n_, src_) in x_pieces(io, layer, k, t0, t0 + LT):
                S.dma(out=xb[:, o_:o_ + n_], in_=src_, w=[xkey], allow_slow_non_contiguous=(n_ < 2))
            S.act('activation', out=hs[:, k, :], in_=xb[:], func=AF.Identity,
                  scale=mod_sb[:, 8 + k:9 + k], bias=mod_sb[:, k:k + 1], r=[xkey, 'mod'], w=['hs'])
        for j in range(4):
            pt, pk = C.pst()
            for k in range(8):
                S.pe('matmul', pt[:, 0:LT], wab[:, k, j * 128:(j + 1) * 128], hs[:, k, :],
                     start=(k == 0), stop=(k == 7), r=['wab', 'hs'], w=[pk])
            S.pool('tensor_copy', out=xa[:, 0:3], in_=xal[:, j, :], r=['xal'], w=['xa'])
            S.act('activation', out=xa[:, 3:LT + 3], in_=pt[:, 0:LT], func=AF.Identity, r=[pk], w=['xa'])
            S.pool('tensor_copy', out=xal[:, j, :], in_=xa[:, LT:LT + 3], r=['xa'], w=['xal'])
            pt, pk = C.pst()
            for k in range(8):
                S.pe('matmul', pt[:, 0:LT], wab[:, k, 512 + j * 128:512 + (j + 1) * 128], hs[:, k, :],
                     start=(k == 0), stop=(k == 7), r=['wab', 'hs'], w=[pk])
            S.act('activation', out=gg[:], in_=pt[:, 0:LT], func=AF.Gelu, r=[pk], w=['gg'])
            S.dve('tensor_scalar', out=xc[:], in0=xa[:, 3:LT + 3], scalar1=LVc('cw0', j), scalar2=LVc('cb', j),
                  op0=ALU.mult, op1=ALU.add, r=['xa', 'vecl'], w=['xc'])
            for jj in (1, 2, 3):
                S.dve('scalar_tensor_tensor', out=xc[:], in0=xa[:, 3 - jj:LT + 3 - jj], scalar=LVc(f'cw{jj}', j), in1=xc[:],
                      op0=ALU.mult, op1=ALU.add, r=['xa', 'vecl', 'xc'], w=['xc'])
            pt, pk = C.pst()
            S.pe('matmul', pt[:, 0:LT], gw_sb[:, j, :], xc[:], start=True, stop=True, r=['gw', 'xc'], w=[pk])
            S.act('activation', out=rr[:], in_=pt[:, 0:LT], func=AF.Sigmoid, bias=LVc('ba', j), r=[pk, 'vecl'], w=['rr'])
            pt, pk = C.pst()
            S.pe('matmul', pt[:, 0:LT], gw_sb[:, 4 + j, :], xc[:], start=True, stop=True, r=['gw', 'xc'], w=[pk])
            S.act('activation', out=ii[:], in_=pt[:, 0:LT], func=AF.Sigmoid, bias=LVc('bx', j), r=[pk, 'vecl'], w=['ii'])
            S.act('activation', out=aa[:], in_=rr[:], func=AF.Exp, scale=clam[:, j:j + 1], r=['rr', 'clam'], w=['aa'])
            S.pool('tensor_tensor', out=om[:], in0=aa[:], in1=aa[:], op=ALU.mult, r=['aa'], w=['om'])
            S.dve('tensor_scalar', out=om[:], in0=om[:], scalar1=-1.0, scalar2=1.0, op0=ALU.mult, op1=ALU.add, r=['om'], w=['om'])
            S.dve('tensor_scalar', out=om[:], in0=om[:], scalar1=1e-30, scalar2=None, op0=ALU.max, r=['om'], w=['om'])
            S.act('activation', out=om[:], in_=om[:], func=AF.Sqrt, r=['om'], w=['om'])
            S.dve('tensor_tensor', out=ii[:], in0=ii[:], in1=xc[:], op=ALU.mult, r=['ii', 'xc'], w=['ii'])
            S.dve('tensor_tensor', out=ii[:], in0=ii[:], in1=om[:], op=ALU.mult, r=['ii', 'om'], w=['ii'])
            S.dve('tensor_tensor_scan', out=hq[:], data0=aa[:], data1=ii[:], initial=carry[:, j:j + 1], op0=ALU.mult, op1=ALU.add,
                  r=['aa', 'ii', 'carry'], w=['hq'])
            S.dve('tensor_copy', out=carry[:, j:j + 1], in_=hq[:, LT - 1:LT], r=['hq'], w=['carry'])
            S.pool('tensor_tensor', out=hq[:], in0=hq[:], in1=gg[:], op=ALU.mult, r=['hq', 'gg'], w=['hq'])
            S.dma(out=io[f'ys{j}'][:, t0:t0 + LT], in_=hq[:], r=['hq'])
            yield


def mod_inputs(inp, l, b, xT_b, lo=0):
    return {'xT': xT_b, 'cT': arr128(inp['c'][b]),
            'modw': np.ascontiguousarray(inp['mod_w'][l][:, lo:lo + 2048]),
            'modb': arr128(inp['mod_b'][l][lo:lo + 2048])}


def lru_inputs(inp, l, b, hh, xT_b):
    d = mod_inputs(inp, l, b, xT_b)
    ch = hh * 512 + np.arange(512)
    w_in = inp['w_in'][l]
    d['wa'] = np.ascontiguousarray(np.concatenate([w_in[:, ch], w_in[:, 1024 + ch]], axis=1))
    vecl = np.zeros((128, 32), np.float32)
    srcs = {'cw0': inp['conv_a_w'][l][0], 'cw1': inp['conv_a_w'][l][1], 'cw2': inp['conv_a_w'][l][2],
            'cw3': inp['conv_a_w'][l][3], 'cb': inp['conv_a_b'][l], 'ba': inp['lru_ba'][l], 'bx': inp['lru_bx'][l],
            'lam': inp['lru_lambda'][l]}
    for n, v in srcs.items():
        vecl[:, LV_NAMES.index(n) * 4:LV_NAMES.index(n) * 4 + 4] = arr128(v[ch])
    d['vecl'] = vecl
    gw = np.zeros((128, 8, 128), np.float32)
    for j in range(4):
        for e in range(2):
            g = hh * 8 + j * 2 + e
            gw[e * 64:(e + 1) * 64, j, e * 64:(e + 1) * 64] = inp['lru_wa'][l][g]
            gw[e * 64:(e + 1) * 64, 4 + j, e * 64:(e + 1) * 64] = inp['lru_wx'][l][g]
    d['gw'] = gw
    return d


AT = 512
DMAXS = (1, 4, 16)
DILS = (1, 4, 16)
NSTRIP = sum(d + 1 for d in DMAXS)
TWO_PI = float(2 * np.pi)
CW1 = 6.28125
CW2 = float(2 * np.pi - 6.28125)


def attn_strips():
    st = np.zeros((128, NSTRIP, 128), np.float32)
    tk = np.arange(128)[:, None]
    tq = np.arange(128)[None, :]
    o = 0
    for g in range(3):
        dil, dm = DILS[g], DMAXS[g]
        for bi in range(dm + 1):
            delta = dm - bi
            dist = 128 * delta + tq - tk
            st[:, o + bi, :] = (dist >= 0) & (dist <= 128 * dil) & (dist % dil == 0)
        o += dm + 1
    return st.reshape(128, NSTRIP * 128)


def emit_attn(nc, io, layer):
    C = Ctx(nc, f"at{layer}"); S = C.S
    wqk = io[f'wqk_{layer}']; wv = io[f'wv_{layer}']
    pos = io['pos']; freq = io['freq']; strips = io['strips']
    ybT = io['ys4']
    C.init_psum()
    mod_sb = load_mod(C, io, layer)
    fr = C.sb("freq", [128, 1]); S.dma(out=fr[:], in_=freq, w=['freq'])
    strip_sb = C.sb("strips", [128, NSTRIP * 128]); S.dma(out=strip_sb[:], in_=strips, w=['strips'])
    wqkb = C.sb("wqkb", [128, 8, 768], BF16)
    S.dma(out=wqkb[:, :, 0:384], in_=wqk.rearrange("(k p) c -> p k c", p=128)[:, :, 0:384], w=['wqkb'], q='pool')
    S.dma(out=wqkb[:, :, 384:768], in_=wqk.rearrange("(k p) c -> p k c", p=128)[:, :, 384:768], w=['wqkb'], q='pool')
    wvb = C.sb("wvb", [128, 8, 384], BF16)
    S.dma(out=wvb[:], in_=wv.rearrange("(k p) c -> p k c", p=128), w=['wvb'], q='pool')
    W2 = C.sb("W2", [128, 8, 768], BF16)
    S.pool('memset', W2[:], 0.0, w=['W2'])
    for ch in range(6):
        for hd in range(2):
            bs = ch * 128 + hd * 64
            S.dve('tensor_scalar', out=W2[:, :, bs:bs + 8], in0=wqkb[:, :, bs + 8:bs + 16], scalar1=-1.0, scalar2=None,
                  op0=ALU.mult, r=['wqkb', 'W2'], w=['W2'])
            S.dve('tensor_copy', out=W2[:, :, bs + 8:bs + 16], in_=wqkb[:, :, bs:bs + 8], r=['wqkb', 'W2'], w=['W2'])
    ones_bf = C.sb("ones_bf", [128, 64], BF16)
    S.pool('memset', ones_bf[:], 1.0, w=['ones_bf'])
    hs = C.sb("hs", [128, 8, AT], BF16)
    xk = [C.sb("xk0", [128, AT]), C.sb("xk1", [128, AT])]
    qk = [C.sb(f"qk{ch}", [128, SEQ], BF16) for ch in range(6)]
    Vtok = C.sb("Vtok", [128, SEQ // 128, 384], BF16)
    posi = C.sb("posi", [128, AT], I32)
    ang = C.sb("ang", [128, AT]); a2 = C.sb("a2", [128, AT]); tq_ = C.sb("tq", [128, AT]); ki = C.sb("ki", [128, AT], I32)
    kf = C.sb("kf", [128, AT]); mk = C.sb("mk", [128, AT])
    Ct = C.sb("Ct", [128, AT]); St = C.sb("St", [128, AT])
    t1 = C.sb("t1", [128, AT]); t2 = C.sb("t2", [128, AT])
    Pbuf = C.sb("Pbuf", [128, 2, NSTRIP, 128], BF16)
    rd = C.sb("rd", [64, 2, 128]); yb = C.sb("yb", [64, 2, 128])

    for tl in range(SEQ // AT):
        t0 = tl * AT
        for k in range(8):
            xb, xkey = xk[k % 2], f"xk{k % 2}"
            for (o_, n_, src_) in x_pieces(io, layer, k, t0, t0 + AT):
                S.dma(out=xb[:, o_:o_ + n_], in_=src_, w=[xkey], allow_slow_non_contiguous=(n_ < 2))
            S.act('activation', out=hs[:, k, :], in_=xb[:], func=AF.Identity,
                  scale=mod_sb[:, 8 + k:9 + k], bias=mod_sb[:, k:k + 1], r=[xkey, 'mod'], w=['hs'])
        S.dma(out=posi[:], in_=pos[:, t0:t0 + AT].partition_broadcast(128), w=['posi'])
        S.dve('tensor_copy', out=ang[:], in_=posi[:], r=['posi'], w=['ang'])
        S.dve('tensor_scalar', out=ang[:], in0=ang[:], scalar1=fr[:, 0:1], scalar2=None, op0=ALU.mult, r=['ang', 'freq'], w=['ang'])
        for (tbl, tkey, shf) in ((St, 'St', 0.0), (Ct, 'Ct', float(np.pi / 2))):
            S.dve('tensor_scalar', out=a2[:], in0=ang[:], scalar1=shf, scalar2=None, op0=ALU.add, r=['ang'], w=['a2'])
            S.dve('tensor_scalar', out=tq_[:], in0=a2[:], scalar1=1.0 / TWO_PI, scalar2=None, op0=ALU.mult, r=['a2'], w=['tq'])
            S.dve('tensor_copy', out=ki[:], in_=tq_[:], r=['tq'], w=['ki'])
            S.dve('tensor_copy', out=kf[:], in_=ki[:], r=['ki'], w=['kf'])
            S.dve('scalar_tensor_tensor', out=a2[:], in0=kf[:], scalar=-CW1, in1=a2[:], op0=ALU.mult, op1=ALU.add,
                  r=['kf', 'a2'], w=['a2'])
            S.dve('scalar_tensor_tensor', out=a2[:], in0=kf[:], scalar=-CW2, in1=a2[:], op0=ALU.mult, op1=ALU.add,
                  r=['kf', 'a2'], w=['a2'])
            S.dve('tensor_scalar', out=mk[:], in0=a2[:], scalar1=float(np.pi), scalar2=None, op0=ALU.is_gt, r=['a2'], w=['mk'])
            S.dve('scalar_tensor_tensor', out=a2[:], in0=mk[:], scalar=-TWO_PI, in1=a2[:], op0=ALU.mult, op1=ALU.add,
                  r=['mk', 'a2'], w=['a2'])
            S.dve('tensor_scalar', out=mk[:], in0=a2[:], scalar1=-float(np.pi), scalar2=None, op0=ALU.is_lt, r=['a2'], w=['mk'])
            S.dve('scalar_tensor_tensor', out=a2[:], in0=mk[:], scalar=TWO_PI, in1=a2[:], op0=ALU.mult, op1=ALU.add,
                  r=['mk', 'a2'], w=['a2'])
            S.dve('tensor_scalar', out=a2[:], in0=a2[:], scalar1=-3.1415925, scalar2=3.1415925, op0=ALU.max, op1=ALU.min,
                  r=['a2'], w=['a2'])
            S.act('activation', out=tbl[:], in_=a2[:], func=AF.Sin, r=['a2'], w=[tkey])
        for ch in range(6):
            p1, p1k = C.pst()
            for k in range(8):
                S.pe('matmul', p1[:, 0:AT], wqkb[:, k, ch * 128:(ch + 1) * 128], hs[:, k, :], start=(k == 0), stop=(k == 7),
                     r=['wqkb', 'hs'], w=[p1k])
            p2, p2k = C.pst()
            for k in range(8):
                S.pe('matmul', p2[:, 0:AT], W2[:, k, ch * 128:(ch + 1) * 128], hs[:, k, :], start=(k == 0), stop=(k == 7),
                     r=['W2', 'hs'], w=[p2k])
            S.dve('tensor_tensor', out=t1[:], in0=p1[:, 0:AT], in1=Ct[:], op=ALU.mult, r=[p1k, 'Ct'], w=['t1'])
            S.dve('tensor_tensor', out=t2[:], in0=p2[:, 0:AT], in1=St[:], op=ALU.mult, r=[p2k, 'St'], w=['t2'])
            S.pool('tensor_tensor', out=qk[ch][:, t0:t0 + AT], in0=t1[:], in1=t2[:], op=ALU.add, r=['t1', 't2'], w=[f"qk{ch}"])
        for sub in range(AT // 128):
            blk = tl * (AT // 128) + sub
            pt, pk = C.pst()
            for k in range(8):
                S.pe('matmul', pt[:, 0:384], hs[:, k, sub * 128:(sub + 1) * 128], wvb[:, k, :], start=(k == 0), stop=(k == 7),
                     r=['hs', 'wvb'], w=[pk])
            S.act('activation', out=Vtok[:, blk, :], in_=pt[:, 0:384], func=AF.Identity, r=[pk], w=['Vtok'])

    for i in range(SEQ // 128):
        entries = []
        j = 0
        for g in range(3):
            kb0 = max(0, i - DMAXS[g])
            sbase = sum(d + 1 for d in DMAXS[:g])
            kbs = list(range(kb0, i + 1))
            for u0 in range(0, len(kbs), 4):
                grp = kbs[u0:u0 + 4]
                n = len(grp)
                for s in range(2):
                    pt, pk = C.pst()
                    for u, kb in enumerate(grp):
                        S.pe('matmul', pt[:, u * 128:(u + 1) * 128], qk[3 + g][s * 64:(s + 1) * 64, kb * 128:(kb + 1) * 128],
                             qk[g][s * 64:(s + 1) * 64, i * 128:(i + 1) * 128], start=True, stop=True,
                             r=[f"qk{3 + g}", f"qk{g}"], w=[pk])
                    S.act('activation', out=Pbuf[:, s, j:j + n, :], in_=pt[:, 0:n * 128].rearrange("p (n q) -> p n q", q=128),
                          func=AF.Exp, scale=0.125, r=[pk], w=['Pbuf'])
                so = sbase + (DMAXS[g] - (i - grp[0]))
                mview = strip_sb[:, so * 128:(so + n) * 128].rearrange("p (n q) -> p n q", q=128).unsqueeze(1).broadcast_to([128, 2, n, 128])
                S.dve('tensor_tensor', out=Pbuf[:, :, j:j + n, :], in0=Pbuf[:, :, j:j + n, :], in1=mview, op=ALU.mult,
                      r=['Pbuf', 'strips'], w=['Pbuf'])
                for kb in grp:
                    entries.append((g, kb, j))
                    j += 1
        pt, pk = C.pst()
        ne = len(entries)
        for s in range(2):
            for idx, (g, kb, jj) in enumerate(entries):
                S.pe('matmul', pt[0:64, s * 128:(s + 1) * 128], Vtok[:, kb, g * 128 + s * 64:g * 128 + (s + 1) * 64], Pbuf[:, s, jj, :],
                     start=(idx == 0), stop=(idx == ne - 1), r=['Vtok', 'Pbuf'], w=[pk])
        for idx, (g, kb, jj) in enumerate(entries):
            S.pe('matmul', pt[0:64, 256:512], ones_bf[:], Pbuf[:, :, jj, :],
                 start=(idx == 0), stop=(idx == ne - 1), r=['ones_bf', 'Pbuf'], w=[pk])
        S.dve('reciprocal', out=rd[:], in_=pt[0:64, 256:512].rearrange("p (s q) -> p s q", s=2), r=[pk], w=['rd'])
        S.dve('tensor_tensor', out=yb[:], in0=pt[0:64, 0:256].rearrange("p (s q) -> p s q", s=2), in1=rd[:], op=ALU.mult,
              r=[pk, 'rd'], w=['yb'])
        S.dma(out=ybT.rearrange("(s d) t -> d s t", d=64)[:, :, i * 128:(i + 1) * 128], in_=yb[:], r=['yb'])
    C.finish()


def attn_inputs(inp, l, b, hh, xT_b):
    d = mod_inputs(inp, l, b, xT_b)
    w_in = inp['w_in'][l]
    qb, kb_, vb = 2048, 2048 + 768, 2048 + 1536
    cols = np.concatenate([(g * 4 + 2 * hh) * 64 + np.arange(128) for g in range(3)])
    d['wqk'] = np.ascontiguousarray(np.concatenate([w_in[:, qb + cols], w_in[:, kb_ + cols]], axis=1))
    d['wv'] = np.ascontiguousarray(w_in[:, vb + cols])
    d['pos'] = np.ascontiguousarray(inp['positions'][b][None, :]).astype(np.int32)
    fr = np.zeros((128, 1), np.float32)
    inv = (500000.0 ** (-np.arange(8, dtype=np.float32) / np.float32(8))).astype(np.float32)
    for p in range(128):
        if p % 64 < 16:
            fr[p, 0] = inv[(p % 64) % 8]
    d['freq'] = fr
    d['strips'] = attn_strips()
    return d


DN = 410
DTOK = 2050
NDT = 5
DV_NAMES = ['ln1w', 'ln1b', 'ln2w', 'ln2b']
FV_NAMES = ['fw0', 'fw1', 'fw2', 'fb']


def emit_dense(nc, io, layer):
    C = Ctx(nc, f"dn{layer}"); S = C.S
    WG = io[f'WGb_{layer}']; WP = io[f'WPb_{layer}']; WO = io[f'WOb_{layer}']; WU = io[f'WUb_{layer}']; WD = io[f'WDb_{layer}']
    dvec = io[f'dvec_{layer}']; fvec = io[f'fvec_{layer}']
    flag = io['flag']; onesm = io['onesm']
    C.init_psum()
    mod_sb = load_mod(C, io, layer)
    SH1, SC1, GT1, SH2, SC2, GT2 = 0, 8, 16, 24, 32, 40
    dv = C.sb("dvec", [128, 32]); S.dma(out=dv[:], in_=dvec, w=['dvec'])
    fv = C.sb("fvec", [128, 176]); S.dma(out=fv[:], in_=fvec, w=['fvec'])
    fl = C.sb("flag", [128, 2]); S.dma(out=fl[:], in_=flag, w=['flag'])
    om = C.sb("onesm", [128, 128]); S.dma(out=om[:], in_=onesm, w=['onesm'])
    DVc = lambda n, k: dv[:, DV_NAMES.index(n) * 8 + k:DV_NAMES.index(n) * 8 + k + 1]
    FVc = lambda n, j: fv[:, FV_NAMES.index(n) * 44 + j:FV_NAMES.index(n) * 44 + j + 1]
    wg = [C.sb(f"wg{i}", [128, 8, 384], BF16) for i in range(2)]
    wp = [C.sb(f"wp{i}", [128, 18, 128], BF16) for i in range(2)]
    wo = [C.sb(f"wo{i}", [128, 8, 128], BF16) for i in range(2)]
    wu = [C.sb(f"wu{i}", [128, 8, 256], BF16) for i in range(2)]
    wd = [C.sb(f"wd{i}", [128, 22, 128], BF16) for i in range(2)]
    xs = C.sb("xs", [128, 8, DN]); hs = C.sb("hs", [128, 8, DN], BF16)
    ya = C.sb("ya", [128, 8, DN], BF16); yc = C.sb("yc", [128, 8, DN], BF16); yb = C.sb("yb", [128, 2, DN], BF16)
    yA = C.sb("yA", [128, 8, DN]); yB = C.sb("yB", [128, 8, DN])
    S.pool('memset', yA[:], 0.0, w=['yA'])
    S.pool('memset', yB[:], 0.0, w=['yB'])
    mg = C.sb("mg", [128, 8, DN], BF16)
    r1b = [C.sb("r1_0", [128, 8, DN]), C.sb("r1_1", [128, 8, DN])]; r2 = C.sb("r2", [128, 8, DN])
    sg_ = [C.sb(f"sg{i}", [128, DN]) for i in range(3)]
    acc = C.sb("acc", [128, DN]); tmp = C.sb("tmp", [128, DN]); tmpB = C.sb("tmpB", [128, DN])
    lnt = {st_: {'mu': C.sb(f"mu{st_}", [128, DN]), 'rstd': C.sb(f"rstd{st_}", [128, DN]),
                 'sq': [C.sb(f"sq{st_}0", [128, DN]), C.sb(f"sq{st_}1", [128, DN])]} for st_ in 'AB'}
    h2 = C.sb("h2", [128, 8, DN], BF16)
    ss = C.sb("ss", [128, 22, DN], BF16)
    upg = C.sb("upg", [128, DN + 2]); upv = C.sb("upv", [128, DN + 2])
    ucg = C.sb("ucg", [128, DN]); ucv = C.sb("ucv", [128, DN])
    tail = C.sb("tail", [128, 44, 2])
    S.pool('memset', tail[:], 0.0, w=['tail'])

    def layer_norm(Sx, pstf, st_, src, skey, wn, bn):
        T = lnt[st_]
        mu, rstd = T['mu'], T['rstd']
        mk_, rk_ = 'mu' + st_, 'rstd' + st_
        pt, pk = pstf()
        for k in range(8):
            Sx.pe('matmul', pt[:, 0:DN], om[:], src[:, k, :], start=(k == 0), stop=(k == 7), r=['onesm', skey], w=[pk])
        Sx.act('activation', out=mu[:], in_=pt[:, 0:DN], func=AF.Identity, r=[pk], w=[mk_])
        for k in range(8):
            Sx.dve('tensor_tensor', out=src[:, k, :], in0=src[:, k, :], in1=mu[:], op=ALU.subtract, r=[skey, mk_], w=[skey])
        pt, pk = pstf()
        for k in range(8):
            sqt, sqk = T['sq'][k % 2], f"sq{st_}{k % 2}"
            Sx.pool('tensor_tensor', out=sqt[:], in0=src[:, k, :], in1=src[:, k, :], op=ALU.mult, r=[skey], w=[sqk])
            Sx.pe('matmul', pt[:, 0:DN], om[:], sqt[:], start=(k == 0), stop=(k == 7), r=['onesm', sqk], w=[pk])
        Sx.act('activation', out=rstd[:], in_=pt[:, 0:DN], func=AF.Sqrt, bias=LN_EPS_AP[0], r=[pk, 'eps'], w=[rk_])
        Sx.dve('reciprocal', out=rstd[:], in_=rstd[:], r=[rk_], w=[rk_])
        for k in range(8):
            Sx.dve('tensor_tensor', out=src[:, k, :], in0=src[:, k, :], in1=rstd[:], op=ALU.mult, r=[skey, rk_], w=[skey])
            Sx.act('activation', out=src[:, k, :], in_=src[:, k, :], func=AF.Identity, scale=DVc(wn, k), bias=DVc(bn, k),
                   r=[skey, 'dvec'], w=[skey])

    epsT = C.sb("epsT", [128, 1])
    S.pool('memset', epsT[:], LN_EPS, w=['eps'])
    LN_EPS_AP = [epsT[:, 0:1]]
    C.nmain = 6
    SB = PrefSched(S, '', defer=True)
    S.side_every = 1

    def emit_A(tl):
        c0 = tl * DN
        r1, r1k = r1b[tl % 2], f"r1_{tl % 2}"
        for k in range(8):
            if layer == 0:
                S.dma(out=xs[:, k, :], in_=io['xh'][k * 128:(k + 1) * 128, c0:c0 + DN], w=['xs'])
            elif tl == 0:
                S.dma(out=xs[:, k, 0:2], in_=io[f'xg{k // 2}'][(k % 2) * 128:(k % 2) * 128 + 128, 2046:2048], w=['xs'])
                S.dma(out=xs[:, k, 2:DN], in_=io[f'xo{k // 2}'][(k % 2) * 128:(k % 2) * 128 + 128, 0:DN - 2], w=['xs'])
            else:
                S.dma(out=xs[:, k, :], in_=io[f'xo{k // 2}'][(k % 2) * 128:(k % 2) * 128 + 128, c0 - 2:c0 - 2 + DN], w=['xs'])
            S.act('activation', out=hs[:, k, :], in_=xs[:, k, :], func=AF.Identity,
                  scale=mod_sb[:, SC1 + k:SC1 + k + 1], bias=mod_sb[:, SH1 + k:SH1 + k + 1], r=['xs', 'mod'], w=['hs'])
        for (dst, dkey, j0, kk) in ((ya, 'ya', 0, 4), (yb, 'yb', 4, 1), (yc, 'yc', 5, 4)):
            for fi, (buf, bkey) in enumerate(((yA, 'yA'), (yB, 'yB'))):
                for r_ in range(2):
                    for k_ in range(kk):
                        src = io[f'yg{j0 + k_}'][r_ * 128:(r_ + 1) * 128, :]
                        d_ = buf[:, r_ * kk + k_, :]
                        if fi == 1:
                            S.dma(out=d_, in_=src[:, 2046 + c0:2046 + c0 + DN], w=[bkey])
                        elif tl == 0:
                            S.dma(out=d_[:, 2:DN], in_=src[:, 0:DN - 2], w=[bkey])
                        else:
                            S.dma(out=d_, in_=src[:, c0 - 2:c0 - 2 + DN], w=[bkey])
            S.dve('tensor_scalar', out=dst[:, 0:2 * kk, :], in0=yA[:, 0:2 * kk, :], scalar1=fl[:, 0:1], scalar2=None, op0=ALU.mult,
                  r=['yA', 'flag'], w=[dkey])
            S.dve('scalar_tensor_tensor', out=dst[:, 0:2 * kk, :], in0=yB[:, 0:2 * kk, :], scalar=fl[:, 1:2], in1=dst[:, 0:2 * kk, :],
                  op0=ALU.mult, op1=ALU.add, r=['yB', 'flag', dkey], w=[dkey])
        for m in range(8):
            wgb, wgk = wg[m % 2], f"wg{m % 2}"
            wpb, wpk = wp[m % 2], f"wp{m % 2}"
            S.dma(out=wgb[:], in_=WG[m].rearrange("p (k c) -> p k c", k=8), w=[wgk])
            S.dma(out=wpb[:], in_=WP[m].rearrange("p (k c) -> p k c", k=18), w=[wpk])
            for br in range(3):
                pt, pk = C.pst()
                for k in range(8):
                    S.pe('matmul', pt[:, 0:DN], wgb[:, k, br * 128:(br + 1) * 128], hs[:, k, :], start=(k == 0), stop=(k == 7),
                         r=[wgk, 'hs'], w=[pk])
                S.act('activation', out=sg_[br][:], in_=pt[:, 0:DN], func=AF.Sigmoid, r=[pk], w=[f"sg{br}"])
            for br, (src, skey, k0, nk) in enumerate(((ya, 'ya', 0, 8), (yb, 'yb', 8, 2), (yc, 'yc', 10, 8))):
                pt, pk = C.pst()
                for k in range(nk):
                    S.pe('matmul', pt[:, 0:DN], wpb[:, k0 + k, :], src[:, k, :], start=(k == 0), stop=(k == nk - 1),
                         r=[wpk, skey], w=[pk])
                if br == 0:
                    S.dve('tensor_tensor', out=acc[:], in0=pt[:, 0:DN], in1=sg_[0][:], op=ALU.mult, r=[pk, 'sg0'], w=['acc'])
                else:
                    S.dve('tensor_tensor', out=tmp[:], in0=pt[:, 0:DN], in1=sg_[br][:], op=ALU.mult, r=[pk, f"sg{br}"], w=['tmp'])
                    if br == 1:
                        S.pool('tensor_tensor', out=acc[:], in0=acc[:], in1=tmp[:], op=ALU.add, r=['acc', 'tmp'], w=['acc'])
                    else:
                        S.pool('tensor_tensor', out=mg[:, m, :], in0=acc[:], in1=tmp[:], op=ALU.add, r=['acc', 'tmp'], w=['mg'])
        for m in range(8):
            wob, wok = wo[m % 2], f"wo{m % 2}"
            S.dma(out=wob[:], in_=WO[m].rearrange("p (k c) -> p k c", k=8), w=[wok])
            pt, pk = C.pst()
            for k in range(8):
                S.pe('matmul', pt[:, 0:DN], wob[:, k, :], mg[:, k, :], start=(k == 0), stop=(k == 7), r=[wok, 'mg'], w=[pk])
            S.act('activation', out=tmp[:], in_=pt[:, 0:DN], func=AF.Identity, scale=mod_sb[:, GT1 + m:GT1 + m + 1],
                  r=[pk, 'mod'], w=['tmp'])
            S.dve('scalar_tensor_tensor', out=r1[:, m, :], in0=xs[:, m, :], scalar=float(ALPHA), in1=tmp[:], op0=ALU.mult, op1=ALU.add,
                  r=['xs', 'tmp'], w=[r1k])
        layer_norm(S, C.pst, 'A', r1, r1k, 'ln1w', 'ln1b')

    def emit_B(tl):
        c0 = tl * DN
        r1, r1k = r1b[tl % 2], f"r1_{tl % 2}"
        for k in range(8):
            SB.act('activation', out=h2[:, k, :], in_=r1[:, k, :], func=AF.Identity,
                  scale=mod_sb[:, SC2 + k:SC2 + k + 1], bias=mod_sb[:, SH2 + k:SH2 + k + 1], r=[r1k, 'mod'], w=['h2'])
        for j in range(22):
            wub, wuk = wu[j % 2], f"wu{j % 2}"
            SB.dma(out=wub[:], in_=WU[j].rearrange("p (k c) -> p k c", k=8), w=[wuk])
            for half, (up, ukey, uc, uckey, jj) in enumerate(((upg, 'upg', ucg, 'ucg', j), (upv, 'upv', ucv, 'ucv', 22 + j))):
                pt, pk = C.pst_side()
                for k in range(8):
                    SB.pe('matmul', pt[:, 0:DN], wub[:, k, half * 128:(half + 1) * 128], h2[:, k, :], start=(k == 0), stop=(k == 7),
                         r=[wuk, 'h2'], w=[pk])
                SB.act('activation', out=up[:, 2:DN + 2], in_=pt[:, 0:DN], func=AF.Identity, r=[pk], w=[ukey])
                SB.pool('tensor_copy', out=up[:, 0:2], in_=tail[:, jj, :], r=['tail'], w=[ukey])
                if tl == 0:
                    SB.dve('tensor_scalar', out=up[:, 2:4], in0=up[:, 2:4], scalar1=fl[:, 1:2], scalar2=None, op0=ALU.mult,
                          r=[ukey, 'flag'], w=[ukey])
                SB.pool('tensor_copy', out=tail[:, jj, :], in_=up[:, DN:DN + 2], r=[ukey], w=['tail'])
                SB.dve('tensor_scalar', out=uc[:], in0=up[:, 2:DN + 2], scalar1=FVc('fw0', jj), scalar2=FVc('fb', jj),
                      op0=ALU.mult, op1=ALU.add, r=[ukey, 'fvec'], w=[uckey])
                SB.dve('scalar_tensor_tensor', out=uc[:], in0=up[:, 1:DN + 1], scalar=FVc('fw1', jj), in1=uc[:], op0=ALU.mult, op1=ALU.add,
                      r=[ukey, 'fvec', uckey], w=[uckey])
                SB.dve('scalar_tensor_tensor', out=uc[:], in0=up[:, 0:DN], scalar=FVc('fw2', jj), in1=uc[:], op0=ALU.mult, op1=ALU.add,
                      r=[ukey, 'fvec', uckey], w=[uckey])
            SB.act('activation', out=ucg[:], in_=ucg[:], func=AF.Silu, r=['ucg'], w=['ucg'])
            SB.pool('tensor_tensor', out=ss[:, j, :], in0=ucg[:], in1=ucv[:], op=ALU.mult, r=['ucg', 'ucv'], w=['ss'])
        for m in range(8):
            wdb, wdk = wd[m % 2], f"wd{m % 2}"
            SB.dma(out=wdb[:], in_=WD[m].rearrange("p (k c) -> p k c", k=22), w=[wdk])
            pt, pk = C.pst_side()
            for k in range(22):
                SB.pe('matmul', pt[:, 0:DN], wdb[:, k, :], ss[:, k, :], start=(k == 0), stop=(k == 21), r=[wdk, 'ss'], w=[pk])
            SB.act('activation', out=tmpB[:], in_=pt[:, 0:DN], func=AF.Identity, scale=mod_sb[:, GT2 + m:GT2 + m + 1],
                  r=[pk, 'mod'], w=['tmpB'])
            SB.dve('scalar_tensor_tensor', out=r2[:, m, :], in0=r1[:, m, :], scalar=float(ALPHA), in1=tmpB[:], op0=ALU.mult, op1=ALU.add,
                  r=[r1k, 'tmpB'], w=['r2'])
        layer_norm(SB, C.pst_side, 'B', r2, 'r2', 'ln2w', 'ln2b')
        lo_c = 2 if tl == 0 else 0
        for k in range(8):
            if layer == DEPTH - 1 or _os.environ.get('KDEBUG_OUT0'):
                SB.dma(out=io['outT'][k * 128:(k + 1) * 128, c0 - 2 + lo_c:c0 - 2 + DN], in_=r2[:, k, lo_c:DN], r=['r2'])
            else:
                SB.dma(out=io[f'xo{k // 2}'][(k % 2) * 128:(k % 2) * 128 + 128, c0 - 2 + lo_c:c0 - 2 + DN], in_=r2[:, k, lo_c:DN], r=['r2'])

    nB_prev = 0
    for tl in range(NDT):
        if len(S.side) > nB_prev:
            S.replay_side(len(S.side) - nB_prev)
        emit_A(tl)
        n0 = len(S.side)
        emit_B(tl)
        nB_prev = len(S.side) - n0
    S.replay_side()
    C.finish()


def dense_weights(inp, l, perm_c=None):
    w_in = inp['w_in'][l]
    zg = w_in[:, 7680:10752]
    WG = np.stack([np.concatenate([zg[:, br * 1024 + m * 128: br * 1024 + (m + 1) * 128] for br in range(3)], axis=1) for m in range(8)])
    pc_ = inp['proj_c'][l] if perm_c is None else inp['proj_c'][l][perm_c]
    pall = np.concatenate([inp['proj_a'][l], inp['proj_b'][l], pc_], axis=0)
    WP = np.stack([pall[:, m * 128:(m + 1) * 128] for m in range(8)])
    fu = inp['ffn_up'][l]
    WU = np.stack([np.concatenate([fu[:, j * 128:(j + 1) * 128], fu[:, 2816 + j * 128:2816 + (j + 1) * 128]], axis=1) for j in range(22)])
    dvec = np.concatenate([arr128(inp[n][l]) for n in ('ln1_w', 'ln1_b', 'ln2_w', 'ln2_b')], axis=1)
    fcw = inp['ffn_conv_w'][l]
    fvec = np.concatenate([arr128(fcw[0]), arr128(fcw[1]), arr128(fcw[2]), arr128(inp['ffn_conv_b'][l])], axis=1)
    return {'WG': np.ascontiguousarray(WG), 'WP': np.ascontiguousarray(WP), 'WO': np.ascontiguousarray(inp['w_o'][l]),
            'WU': np.ascontiguousarray(WU), 'WD': np.ascontiguousarray(inp['ffn_down'][l]),
            'dvec': np.ascontiguousarray(dvec), 'fvec': np.ascontiguousarray(fvec),
            'onesm': np.full((128, 128), 1.0 / 1024, np.float32),
            'modw': np.ascontiguousarray(inp['mod_w'][l]), 'modb': arr128(inp['mod_b'][l])}


def halo_T(full_T, sh):
    F = full_T.shape[0]
    out = np.zeros((F, DTOK), np.float32)
    if sh == 0:
        out[:, 2:] = full_T[:, 0:2048]
    else:
        out[:, :] = full_T[:, 2046:4096]
    return out


def dense_inputs(dw, c_b, sh, xT_b, yaT_b, ybT_b, ycT_b):
    d = dict(dw)
    d['cT'] = arr128(c_b)
    d['xT'] = halo_T(xT_b, sh); d['yaT'] = halo_T(yaT_b, sh); d['ybT'] = halo_T(ybT_b, sh); d['ycT'] = halo_T(ycT_b, sh)
    d['flag'] = np.full((128, 1), float(sh), np.float32)
    return d


PAIRS = [[0, 1], [2, 3], [4, 5], [6, 7]]


def emit_modphase(nc, io, layer):
    C = Ctx(nc, f"md{layer}"); S = C.S
    C.init_psum()
    cT_sb = C.sb("cT", [128, 8]); S.dma(out=cT_sb[:], in_=io['cT'], w=['cT'])
    modb_sb = C.sb("modb", [128, 48]); S.dma(out=modb_sb[:], in_=io[f'modb_{layer}'], w=['modb'])
    mod_sb = C.sb("mod", [128, 48])
    emit_mod(C, io[f'modw_{layer}'], modb_sb, cT_sb, 6144, mod_sb)
    for lo_ in (8, 16, 32, 40):
        S.dve('tensor_scalar', out=mod_sb[:, lo_:lo_ + 8], in0=mod_sb[:, lo_:lo_ + 8], scalar1=1.0, scalar2=None, op0=ALU.add,
              r=['mod'], w=['mod'])
    S.dma(out=io['modv'][layer], in_=mod_sb[:], r=['mod'])
    C.finish()


import os as _os


def emit_allgather(nc, src_h, dst_h, tag):
    if _os.environ.get("KDEBUG_NOCC"):
        return
    cc = nc.alloc_semaphore(name=f"cc_{tag}")
    with nc.Block() as block:
        @block.gpsimd
        def _(g):
            g.collective_compute("AllGather", ALU.bypass, replica_groups=PAIRS,
                                 ins=[src_h.ap().opt()], outs=[dst_h.ap().opt()]).then_inc(cc)
            g.wait_ge(cc, 1)
    nc.clear_and_free_semaphores([cc])
    nc.all_engine_barrier()


PER_LAYER = [('modw', [1024, 6144]), ('modb', [128, 48]), ('wa', [1024, 1024]), ('vecl', [128, 32]), ('gw', [128, 8, 128]),
             ('wqk', [1024, 768]), ('wv', [1024, 384]), ('vec', [128, NVEC]), ('w2c', [64, 512]), ('a2c', [64, 512]),
             ('g2c', [128, 512]), ('WG', [8, 1024, 384]), ('WP', [8, 2304, 128]), ('WO', [1024, 1024]),
             ('WU', [22, 1024, 256]), ('WD', [2816, 1024]), ('dvec', [128, 32]), ('fvec', [128, 176])]


def build_fused():
    nc = bass.Bass("TRN2", target_bir_lowering=False)
    io = {}

    def din(name, shape, dt=F32):
        io[name] = nc.dram_tensor(name, list(shape), dt, kind="ExternalInput").ap()
    din('xT', [1024, SEQ]); din('xh', [1024, DTOK]); din('cT', [128, 8]); din('pos', [1, SEQ], I32)
    din('freq', [128, 1]); din('strips', [128, NSTRIP * 128]); din('cst', [128, NCST]); din('flag', [128, 2])
    din('onesm', [128, 128]); din('v2c_1', [32, 512])
    for l in range(DEPTH):
        for (n, shp) in PER_LAYER:
            din(f'{n}_{l}', shp)
        din(f'wc_{l}', [1024, 1792 + (32 if l else 0)])
    io['outT'] = nc.dram_tensor('outT', [1024, 2048], F32, kind="ExternalOutput").ap()
    H = {}

    def dint(n, shp):
        H[n] = nc.dram_tensor(n, shp, F32)
        io[n] = H[n].ap()
    dint('modv', [DEPTH, 128, 48]); dint('vfs', [512, SEQ])
    for j in range(9):
        dint(f'ys{j}', [128, SEQ]); dint(f'yg{j}', [256, SEQ])
    for j in range(4):
        dint(f'xo{j}', [256, 2048]); dint(f'xg{j}', [512, 2048])
    for l in range(DEPTH):
        for (n, shp) in (('WG', [8, 128, 8 * 384]), ('WP', [8, 128, 18 * 128]), ('WO', [8, 128, 8 * 128]),
                         ('WU', [22, 128, 8 * 256]), ('WD', [8, 128, 22 * 128])):
            io[f'{n}b_{l}'] = nc.dram_tensor(f'{n}b_{l}', shp, BF16).ap()
    ph = _os.environ.get("KDEBUG_PH", "md,lr,at,rw,ag,dn").split(",")
    nl = int(_os.environ.get("KDEBUG_L", str(DEPTH)))
    for l in range(DEPTH):
        if 'md' in ph:
            emit_modphase(nc, io, l)
    for l in range(nl):
        if 'lr' in ph and not MERGE_LRU:
            emit_lru(nc, io, l)
        if 'at' in ph:
            emit_attn(nc, io, l)
        if 'rw' in ph:
            emit_rwkv(nc, io, l)
        if 'ag' in ph:
            for j in range(9):
                emit_allgather(nc, H[f'ys{j}'], H[f'yg{j}'], f"y{l}_{j}")
        if 'dn' in ph:
            emit_dense(nc, io, l)
        if l < DEPTH - 1 and 'ag' in ph:
            for j in range(4):
                emit_allgather(nc, H[f'xo{j}'], H[f'xg{j}'], f"x{l}_{j}")
    if _os.environ.get("KDEBUG_TAP"):
        tap = nc.dram_tensor('tap', [1152, SEQ], F32, kind="ExternalOutput").ap()
        C = Ctx(nc, "tap")
        for j in range(9):
            C.S.dma(out=tap[j * 128:(j + 1) * 128, :], in_=io[f'ys{j}'][:, :])
        C.finish()
    return nc


def core_inputs(inp, b, r):
    xT_b = np.ascontiguousarray(inp['x'][b].astype(np.float32).T)
    d = {'xT': xT_b, 'xh': halo_T(xT_b, r), 'cT': arr128(inp['c'][b]),
         'flag': np.tile(np.array([[1.0 - r, float(r)]], np.float32), (128, 1)),
         'onesm': np.full((128, 128), 1.0 / 1024, np.float32)}
    perm_c = np.concatenate([pair_rows(0), pair_rows(1)])
    for l in range(DEPTH):
        li = lru_inputs(inp, l, b, r, xT_b)
        ai = attn_inputs(inp, l, b, r, xT_b)
        ri = rwkv_inputs(inp, l, b, r, xT_b)
        dw = dense_weights(inp, l, perm_c)
        d[f'modw_{l}'] = np.ascontiguousarray(inp['mod_w'][l]); d[f'modb_{l}'] = arr128(inp['mod_b'][l])
        for n in ('wa', 'vecl', 'gw'):
            d[f'{n}_{l}'] = li[n]
        for n in ('wqk', 'wv'):
            d[f'{n}_{l}'] = ai[n]
        for n in ('wc', 'vec', 'w2c', 'a2c', 'g2c'):
            d[f'{n}_{l}'] = ri[n]
        if l:
            d['v2c_1'] = ri['v2c']
        for n in ('WG', 'WP', 'WO', 'WU', 'WD', 'dvec', 'fvec'):
            d[f'{n}_{l}'] = dw[n]
        if l == 0:
            d['pos'] = ai['pos']; d['freq'] = ai['freq']; d['strips'] = ai['strips']; d['cst'] = ri['cst']
    return d


_NC = []


def kernel(**inputs):
    inp = {k: np.asarray(v) for k, v in inputs.items()}
    if not _NC:
        _NC.append(build_fused())
    cores = [(b, r) for b in range(BATCH) for r in range(2)]
    maps = [core_inputs(inp, b, r) for (b, r) in cores]
    res = run_bass_kernel_spmd(_NC[0], maps, core_ids=list(range(8))).results
    out = np.zeros((BATCH, SEQ, D_MODEL), np.float32)
    for (b, r), rr in zip(cores, res):
        out[b, r * 2048:(r + 1) * 2048, :] = rr['outT'].T
    return out
```

```python
import contextlib
import numpy as np
import concourse.bass as bass
import concourse.mybir as mybir
from concourse.bass_utils import run_bass_kernel_spmd

F32 = mybir.dt.float32
BF16 = mybir.dt.bfloat16
I32 = mybir.dt.int32
AF = mybir.ActivationFunctionType
ALU = mybir.AluOpType
AX = mybir.AxisListType

D_MODEL = 1024
SEQ = 4096
BATCH = 4
DEPTH = 2
D_FF = 2816
ALPHA = (2 * DEPTH) ** 0.25
LN_EPS = 1e-5
GN_EPS = 64e-5


class Sched:
    ENGS = ('pe', 'act', 'dve', 'pool', 'sp')
    EP = 30000
    R = 8

    def __init__(self, nc, tag=""):
        self.nc = nc
        self.tag = tag
        self.ops = {e: [] for e in self.ENGS}
        self.ncomp = {e: 0 for e in self.ENGS}
        self.ndma = {e: 0 for e in self.ENGS}
        self.last_w = {}
        self.readers = {}
        self.side = []
        self.side_every = 0
        self._cnt = 0
        self._in_side = False

    def add(self, eng, fn, reads=(), writes=(), dma=False):
        me = self._add(eng, fn, reads, writes, dma)
        if self.side and not self._in_side and self.side_every:
            self._cnt += 1
            if self._cnt % self.side_every == 0:
                self.replay_side(1)
        return me

    def replay_side(self, n=None):
        self._in_side = True
        k = 0
        while self.side and (n is None or k < n):
            self._add(*self.side.pop(0))
            k += 1
        self._in_side = False

    def _add(self, eng, fn, reads=(), writes=(), dma=False):
        deps = set()
        for k in reads:
            w = self.last_w.get(k)
            if w is not None:
                deps.add(w)
            if k.startswith('ps'):
                for r in self.readers.get(k, ()):
                    if r[1] != eng:
                        deps.add(r)
        for k in writes:
            w = self.last_w.get(k)
            if w is not None:
                deps.add(w)
            for r in self.readers.get(k, ()):
                deps.add(r)
        if dma:
            me = ('d', eng, self.ndma[eng])
            self.ndma[eng] += 1
        else:
            me = ('c', eng, self.ncomp[eng])
            self.ncomp[eng] += 1
        deps.discard(me)
        self.ops[eng].append((me, fn, deps))
        for k in reads:
            self.readers.setdefault(k, []).append(me)
        for k in writes:
            self.last_w[k] = me
            self.readers[k] = []
        return me

    def op(self, eng, name, *args, r=(), w=(), **kw):
        def fn(e, name=name, args=args, kw=kw):
            return getattr(e, name)(*args, **kw)
        return self.add(eng, fn, r, w)

    def pe(self, name, *args, r=(), w=(), **kw): return self.op('pe', name, *args, r=r, w=w, **kw)
    def act(self, name, *args, r=(), w=(), **kw): return self.op('act', name, *args, r=r, w=w, **kw)
    def dve(self, name, *args, r=(), w=(), **kw): return self.op('dve', name, *args, r=r, w=w, **kw)
    def pool(self, name, *args, r=(), w=(), **kw): return self.op('pool', name, *args, r=r, w=w, **kw)

    def dma(self, out, in_, r=(), w=(), q='sp', **kw):
        def fn(e, out=out, in_=in_, kw=kw):
            return e.dma_start(out=out, in_=in_, **kw)
        return self.add(q, fn, r, w, dma=True)

    def emit(self):
        nc = self.nc
        sems = []

        def new_sem(name):
            h = nc.alloc_semaphore(name=name)
            sems.append(h)
            return h
        with contextlib.ExitStack() as st:
            csem = {}
            for e in self.ENGS:
                nep = (self.ncomp[e] + self.EP - 1) // self.EP
                csem[e] = [new_sem(f"c_{self.tag}_{e}_{i}") for i in range(nep)]
            dsem = {}
            for e in self.ENGS:
                if self.ndma[e]:
                    dsem[e] = [new_sem(f"d_{self.tag}_{e}_{i}") for i in range(min(self.R, self.ndma[e]))]
            block = st.enter_context(nc.Block())
            engobj = {'pe': block.tensor, 'act': block.scalar, 'dve': block.vector,
                      'pool': block.gpsimd, 'sp': block.sync}
            ops, ndma, EP, R = self.ops, self.ndma, self.EP, self.R

            def make(e):
                def body(eng):
                    waited_c = {}
                    waited_d = {}

                    def wait_dep(d):
                        kind, D, n = d
                        if kind == 'c':
                            if D == e and e == 'pe':
                                return
                            if waited_c.get(D, -1) >= n:
                                return
                            waited_c[D] = n
                            eng.wait_ge(csem[D][n // EP], (n % EP) + 1)
                        else:
                            slot = n % R
                            val = 16 * (n // R + 1)
                            if waited_d.get((D, slot), 0) >= val:
                                return
                            waited_d[(D, slot)] = val
                            eng.wait_ge(dsem[D][slot], val)

                    for (me, fn, deps) in ops[e]:
                        red = {}
                        for d in deps:
                            kk_ = (d[0], d[1]) if d[0] == 'c' else (d[0], d[1], d[2] % R)
                            if kk_ not in red or red[kk_][2] < d[2]:
                                red[kk_] = d
                        for d in sorted(red.values()):
                            wait_dep(d)
                        kind, _, n = me
                        if kind == 'd':
                            if n >= R:
                                wait_dep(('d', e, n - R))
                            fn(eng).then_inc(dsem[e][n % R], 16)
                        else:
                            fn(eng).then_inc(csem[e][n // EP], 1)
                    for j in range(max(0, ndma[e] - R), ndma[e]):
                        wait_dep(('d', e, j))
                return body
            for e in self.ENGS:
                if ops[e]:
                    engobj[e](make(e))
        nc.clear_and_free_semaphores(sems)
        nc.all_engine_barrier()


class Ctx:
    def __init__(self, nc, tag):
        self.nc = nc
        self.tag = tag
        self.st = contextlib.ExitStack()
        self.S = Sched(self.nc, tag)
        self.nps = 0
        self.nps_side = 0
        self.ps = None

    def din(self, name, shape, dt=F32):
        return self.nc.dram_tensor(name, list(shape), dt, kind="ExternalInput").ap()

    def dout(self, name, shape, dt=F32):
        return self.nc.dram_tensor(name, list(shape), dt, kind="ExternalOutput").ap()

    def sb(self, name, shape, dt=F32):
        return self.st.enter_context(self.nc.sbuf_tensor(f"sb_{self.tag}_{name}", list(shape), dt))

    def init_psum(self):
        self.ps = [self.st.enter_context(self.nc.psum_tensor(f"ps_{self.tag}_{i}", [128, 512], F32)) for i in range(8)]

    nmain = 8

    def pst(self):
        i = self.nps % self.nmain
        self.nps += 1
        return self.ps[i], f"ps{i}"

    def pst_side(self):
        n = 8 - self.nmain
        i = self.nmain + (self.nps_side % n)
        self.nps_side += 1
        return self.ps[i], f"ps{i}"

    def finish(self):
        self.S.emit()
        self.st.close()


class PrefSched:
    def __init__(self, S, pfx, defer=False):
        self.S, self.pfx, self.defer = S, pfx, defer

    def _k(self, keys):
        return [k if k.startswith('ps') else self.pfx + k for k in keys]

    def op(self, eng, name, *args, r=(), w=(), **kw):
        if self.defer:
            def fn(e, name=name, args=args, kw=kw):
                return getattr(e, name)(*args, **kw)
            self.S.side.append((eng, fn, self._k(r), self._k(w), False))
            return None
        return self.S.op(eng, name, *args, r=self._k(r), w=self._k(w), **kw)

    def pe(self, name, *args, r=(), w=(), **kw): return self.op('pe', name, *args, r=r, w=w, **kw)
    def act(self, name, *args, r=(), w=(), **kw): return self.op('act', name, *args, r=r, w=w, **kw)
    def dve(self, name, *args, r=(), w=(), **kw): return self.op('dve', name, *args, r=r, w=w, **kw)
    def pool(self, name, *args, r=(), w=(), **kw): return self.op('pool', name, *args, r=r, w=w, **kw)

    def dma(self, out, in_, r=(), w=(), q='sp', **kw):
        if self.defer:
            def fn(e, out=out, in_=in_, kw=kw):
                return e.dma_start(out=out, in_=in_, **kw)
            self.S.side.append((q, fn, self._k(r), self._k(w), True))
            return None
        return self.S.dma(out, in_, r=self._k(r), w=self._k(w), q=q, **kw)


class SubCtx:
    def __init__(self, C, pfx, defer=False):
        self.C, self.pfx = C, pfx
        self.nc = C.nc
        self.S = PrefSched(C.S, pfx, defer)

    def sb(self, name, shape, dt=F32):
        return self.C.sb(self.pfx + name, shape, dt)

    def pst(self):
        return self.C.pst_side() if self.S.defer else self.C.pst()


def emit_mod(C, modw, modb_sb, cT_sb, ncols, out_sb):
    S = C.S
    sc = C.sb("silu_c", [128, 8])
    wbuf = [C.sb(f"modw{i}", [128, 8, 512]) for i in range(3)]
    S.act('activation', out=sc[:], in_=cT_sb[:], func=AF.Silu, r=['cT'], w=['silu_c'])
    nj = ncols // 128
    pt, pk = C.pst()
    wv = modw.rearrange("(k p) c -> p k c", p=128)
    for blk in range(ncols // 512):
        wt = wbuf[blk % 3]
        wkey = f"modw{blk % 3}"
        S.dma(out=wt[:], in_=wv[:, :, blk * 512:(blk + 1) * 512], w=[wkey])
        for jj in range(4):
            j = blk * 4 + jj
            for k in range(8):
                S.pe('matmul',
                    pt[:, j:j + 1], wt[:, k, jj * 128:(jj + 1) * 128], sc[:, k:k + 1],
                    start=(k == 0), stop=(k == 7), r=[wkey, 'silu_c'], w=[pk])
    S.dve('tensor_tensor', out=out_sb[:], in0=pt[:, 0:nj], in1=modb_sb[:], op=ALU.add,
          r=[pk, 'modb'], w=['mod'])


MERGE_LRU = True
SIDE_EVERY = 4
TS = 256
NSEG = SEQ // TS
NCH = TS // 64
VEC_NAMES = ['mu_r', 'mu_k', 'mu_v', 'w0', 'a0', 'kk', 'ka', 'rk', 'lnw', 'lnb', 'v0']


def vcol(name, p):
    return VEC_NAMES.index(name) * 4 + p


V_MUZW, V_MUZA, V_MUZG, V_MUZV1 = 44, 45, 46, 47
NVEC = 48
C_ID, C_MTS, C_MST, C_MSTI, C_RST, C_EYE, C_BONES = 0, 64, 64 + TS, 64 + 2 * TS, 64 + 3 * TS, 64 + 4 * TS, 64 + 5 * TS
C_HM = C_BONES + 128
C_H = C_HM + 2
C_OH = C_H + 6 * TS
NCST = C_OH + 2 * TS


def rwkv_consts():
    c = np.zeros((128, NCST), np.float32)
    pp = np.arange(128)[:, None] % 64
    f = np.arange(TS)[None, :] % 64
    c[:, C_ID:C_ID + 64] = (pp == np.arange(64)[None, :])
    c[:, C_MTS:C_MTS + TS] = (pp > f)
    c[:, C_MST:C_MST + TS] = (f > pp)
    c[:, C_MSTI:C_MSTI + TS] = (f >= pp)
    c[:, C_RST:C_RST + TS] = (f != 0)
    c[:, C_EYE:C_EYE + TS] = (f == pp)
    c[:, C_BONES:C_BONES + 128] = (np.arange(128)[:, None] // 64 == np.arange(128)[None, :] // 64)
    hm = np.stack([(np.arange(128) < 64), (np.arange(128) >= 64)], axis=1).astype(np.float32)
    c[:, C_HM:C_HM + 2] = hm
    for bi, base in enumerate((C_EYE, C_MTS, C_MST)):
        for e in range(2):
            c[:, C_H + (bi * 2 + e) * TS:C_H + (bi * 2 + e + 1) * TS] = c[:, base:base + TS] * hm[:, e:e + 1]
    for e in range(2):
        c[:, C_OH + e * TS:C_OH + (e + 1) * TS] = hm[:, e:e + 1]
    return c


def x_pieces(io, l, k, ta, tb):
    if l == 0:
        return [(0, tb - ta, io['xT'][k * 128:(k + 1) * 128, ta:tb])]
    out = []
    for r in range(2):
        a_, b_ = max(ta, r * 2048), min(tb, (r + 1) * 2048)
        if a_ < b_:
            rows = r * 256 + (k % 2) * 128
            out.append((a_ - ta, b_ - a_, io[f'xg{k // 2}'][rows:rows + 128, a_ - r * 2048:b_ - r * 2048]))
    return out


def load_mod(C, io, layer):
    mod_sb = C.sb("mod", [128, 48])
    C.S.dma(out=mod_sb[:], in_=io['modv'][layer], w=['mod'])
    return mod_sb


def emit_rwkv(nc, io, layer):
    C = Ctx(nc, f"rw{layer}")
    S = C.S
    NCW = 1792 + (32 if layer else 0)
    wc = io[f'wc_{layer}']; vec = io[f'vec_{layer}']
    w2c = io[f'w2c_{layer}']; a2c = io[f'a2c_{layer}']; g2c = io[f'g2c_{layer}']
    cst = io['cst']
    vfs = io['vfs']
    if layer:
        v2c = io['v2c_1']
    C.init_psum()

    def ld(name, ap, shape, q='sp'):
        t = C.sb(name, shape)
        S.dma(out=t[:], in_=ap, w=[name], q=q)
        return t
    vec_sb = ld("vec", vec, [128, NVEC])
    w2_sb = ld("w2c", w2c, [64, 512])
    a2_sb = ld("a2c", a2c, [64, 512])
    g2_sb = ld("g2c", g2c, [128, 512])
    cst_sb = ld("cst", cst, [128, NCST])
    if layer:
        v2_sb = ld("v2c", v2c, [32, 512])
    wcb = C.sb("wcb", [128, 8, NCW], BF16)
    wcv = wc.rearrange("(k p) c -> p k c", p=128)
    c0 = 0
    while c0 < NCW:
        n = min(512, NCW - c0)
        S.dma(out=wcb[:, :, c0:c0 + n], in_=wcv[:, :, c0:c0 + n], w=['wcb'], q='pool')
        c0 += n
    mod_sb = load_mod(C, io, layer)
    omv = C.sb("omv", [128, NVEC])
    S.dve('tensor_scalar', out=omv[:], in0=vec_sb[:], scalar1=-1.0, scalar2=1.0, op0=ALU.mult, op1=ALU.add,
          r=['vec'], w=['omv'])

    def V(name, p, rows=128):
        c = vcol(name, p)
        return vec_sb[0:rows, c:c + 1]

    ident = cst_sb[:, C_ID:C_ID + 64]
    m_ts = cst_sb[:, C_MTS:C_MTS + TS]
    m_st = cst_sb[:, C_MST:C_MST + TS]
    m_sti = cst_sb[:, C_MSTI:C_MSTI + TS]
    rst = cst_sb[:, C_RST:C_RST + TS]
    eye = cst_sb[:, C_EYE:C_EYE + TS]
    bones = cst_sb[:, C_BONES:C_BONES + 128]

    hs = C.sb("hs", [128, 8, TS], BF16)
    xk = [C.sb("xk0", [128, TS]), C.sb("xk1", [128, TS])]
    zwp = C.sb("zwp", [64, TS + 1]); tzw = C.sb("tzw", [64, TS])
    zap = C.sb("zap", [64, TS + 1])
    zgp = C.sb("zgp", [128, TS + 1]); szg = C.sb("szg", [128, TS])
    if layer:
        zvp = C.sb("zvp", [32, TS + 1])
    tmpsh = C.sb("tmpsh", [128, TS])
    zr = C.sb("zr", [128, TS + 1]); zk = C.sb("zk", [128, TS + 1]); zv = C.sb("zv", [128, TS + 1])
    Wt = {n: C.sb("w_" + n, [128, TS]) for n in
          ['ld', 'cl', 'a', 'kk', 'sq', 'kkn', 'km', 'epos', 'eneg', 'eprev', 'Bt', 'Kt', 'rkr', 'lam', 'vf']}
    PP = [{n: C.sb(f"{n}{p}", [128, TS]) for n in ['At', 'Rt', 'AKT', 'RBT', 'RKT', 'TT', 'bon', 'gT']}
          for p in range(4)]
    TOK = [{n: C.sb(f"{n}{p}", [128, NCH, 64]) for n in ['Btok', 'Ktok', 'Vtok']} for p in range(4)]
    gam = C.sb("gam", [128, 4, NCH])
    Nm = C.sb("Nm", [128, TS]); NT = C.sb("NT", [128, TS])
    Pw = [C.sb("Pw0", [128, TS]), C.sb("Pw1", [128, TS])]
    PTw = [C.sb("PTw0", [128, TS]), C.sb("PTw1", [128, TS])]
    TTw = [C.sb("TTw0", [128, TS]), C.sb("TTw1", [128, TS])]
    BD = {n: C.sb("bd_" + n, [128, NCH, 128]) for n in ['At', 'Bt', 'Kt', 'V', 'P0', 'P1', 'PT0', 'PT1', 'IP']}
    Hb = [C.sb("Hb0", [128, 256]), C.sb("Hb1", [128, 256])]
    Xsb = C.sb("Xsb", [128, 256]); Usb = C.sb("Usb", [128, 256]); Htmp = C.sb("Htmp", [128, 256])
    Ysb = C.sb("Ysb", [128, NCH, 4, 64]); dY = C.sb("dY", [128, NCH, 4, 64]); sqY = C.sb("sqY", [128, NCH, 4, 64])
    st1 = C.sb("st1", [128, NCH * 4]); st2 = C.sb("st2", [128, NCH * 4])
    ycs = [C.sb(f"ycs{p}", [128, TS]) for p in range(4)]
    S.pool('memset', Hb[0][:], 0.0, w=['Hb0'])

    zl = C.sb("zl", [128, 16])
    S.pool('memset', zl[:], 0.0, w=['zl'])

    def proj(cols0, M, dst, key, zid):
        pt, pk = C.pst()
        for k in range(8):
            S.pe('matmul', pt[0:M, 0:TS], wcb[:, k, cols0:cols0 + M], hs[:, k, :], start=(k == 0), stop=(k == 7),
                 r=['wcb', 'hs'], w=[pk])
        S.act('activation', out=dst[0:M, 1:TS + 1], in_=pt[0:M, 0:TS], func=AF.Identity, r=[pk], w=[key])
        S.pool('tensor_copy', out=dst[0:M, 0:1], in_=zl[0:M, zid:zid + 1], r=['zl'], w=[key])
        S.pool('tensor_copy', out=zl[0:M, zid:zid + 1], in_=dst[0:M, TS:TS + 1], r=[key], w=['zl'])

    def shift(z, key, M, mucol):
        S.dve('tensor_scalar', out=tmpsh[0:M, :], in0=z[0:M, 0:TS], scalar1=vec_sb[0:M, mucol:mucol + 1],
                                        scalar2=None, op0=ALU.mult, r=[key, 'vec'], w=['tmpsh'])
        S.dve('scalar_tensor_tensor', out=z[0:M, 1:TS + 1], in0=z[0:M, 1:TS + 1],
                                               scalar=omv[0:M, mucol:mucol + 1], in1=tmpsh[0:M, :],
                                               op0=ALU.mult, op1=ALU.add, r=[key, 'omv', 'tmpsh'], w=[key])

    hm = [cst_sb[:, C_HM:C_HM + 1], cst_sb[:, C_HM + 1:C_HM + 2]]
    cH = lambda bi, e_: cst_sb[:, C_H + (bi * 2 + e_) * TS:C_H + (bi * 2 + e_ + 1) * TS]
    v3 = lambda ap: ap.rearrange("q (c t) -> q c t", t=64)

    def product_bd(Lbd, Lk, R, Rk, roff=0):
        pt, pk = C.pst()
        for c in range(NCH):
            S.pe('matmul', pt[:, c * 64:(c + 1) * 64], Lbd[:, c, :], R[:, roff + c * 64:roff + (c + 1) * 64],
                 start=True, stop=True, r=[Lk, Rk], w=[pk])
        return pt, pk

    def to_bd(eng, dst, dkey, src, skeys, kind='copy'):
        for e_ in range(2):
            o = dst[:, :, e_ * 64:(e_ + 1) * 64]
            if kind == 'copy' and eng == 'pool':
                S.pool('tensor_tensor', out=o, in0=v3(src), in1=v3(cst_sb[:, C_OH + e_ * TS:C_OH + (e_ + 1) * TS]), op=ALU.mult,
                       r=list(skeys) + ['cst'], w=[dkey])
            elif kind == 'copy':
                if eng == 'act':
                    S.act('activation', out=o, in_=v3(src), func=AF.Identity, scale=hm[e_], r=list(skeys) + ['cst'], w=[dkey])
                else:
                    S.op(eng, 'tensor_scalar', out=o, in0=v3(src), scalar1=hm[e_], scalar2=None, op0=ALU.mult,
                         r=list(skeys) + ['cst'], w=[dkey])
            elif kind == 'eye':
                S.dve('scalar_tensor_tensor', out=o, in0=v3(src), scalar=hm[e_], in1=v3(cH(0, e_)), op0=ALU.mult, op1=ALU.add,
                      r=list(skeys) + ['cst'], w=[dkey])
            else:
                S.dve('tensor_tensor', out=o, in0=v3(src), in1=v3(cH(kind, e_)), op=ALU.mult, r=list(skeys) + ['cst'], w=[dkey])

    if MERGE_LRU:
        C.nmain = 6
        for _ in lru_body(SubCtx(C, 'L_', defer=True), io, layer):
            pass
        S.side_every = SIDE_EVERY
    lg = iter(())
    cur = 0
    for sg in range(NSEG):
        t0 = sg * TS
        for k in range(8):
            xb = xk[k % 2]
            xkey = f"xk{k % 2}"
            for (o_, n_, src_) in x_pieces(io, layer, k, t0, t0 + TS):
                S.dma(out=xb[:, o_:o_ + n_], in_=src_, w=[xkey], allow_slow_non_contiguous=(n_ < 2))
            S.act('activation', out=hs[:, k, :], in_=xb[:], func=AF.Identity,
                  scale=mod_sb[:, 8 + k:9 + k], bias=mod_sb[:, k:k + 1], r=[xkey, 'mod'], w=['hs'])
        proj(1536, 64, zwp, 'zwp', 0); shift(zwp, 'zwp', 64, V_MUZW)
        S.act('activation', out=tzw[:], in_=zwp[:, 1:TS + 1], func=AF.Tanh, r=['zwp'], w=['tzw'])
        proj(1600, 64, zap, 'zap', 1); shift(zap, 'zap', 64, V_MUZA)
        proj(1664, 128, zgp, 'zgp', 2); shift(zgp, 'zgp', 128, V_MUZG)
        S.act('activation', out=szg[:], in_=zgp[:, 1:TS + 1], func=AF.Sigmoid, r=['zgp'], w=['szg'])
        if layer:
            proj(1792, 32, zvp, 'zvp', 3); shift(zvp, 'zvp', 32, V_MUZV1)
        for p in range(4):
            P_ = PP[p]
            K_ = lambda n, p=p: f"{n}{p}"
            proj(p * 128, 128, zr, 'zr', 4 + p); shift(zr, 'zr', 128, vcol('mu_r', p))
            proj(512 + p * 128, 128, zk, 'zk', 8 + p); shift(zk, 'zk', 128, vcol('mu_k', p))
            proj(1024 + p * 128, 128, zv, 'zv', 12 + p); shift(zv, 'zv', 128, vcol('mu_v', p))
            rs, ks, vs = zr[:, 1:TS + 1], zk[:, 1:TS + 1], zv[:, 1:TS + 1]
            pc = slice(p * 128, (p + 1) * 128)
            if layer == 0:
                S.dma(out=vfs[p * 128:(p + 1) * 128, t0:t0 + TS], in_=zv[:, 1:TS + 1], r=['zv'])
            else:
                pt, pk = C.pst()
                S.pe('matmul', pt[:, 0:TS], v2_sb[:, pc], zvp[:, 1:TS + 1], start=True, stop=True,
                     r=['v2c', 'zvp'], w=[pk])
                S.act('activation', out=Wt['lam'][:], in_=pt[:, 0:TS], func=AF.Sigmoid, bias=V('v0', p),
                      r=[pk, 'vec'], w=['w_lam'])
                S.dma(out=Wt['vf'][:], in_=vfs[p * 128:(p + 1) * 128, t0:t0 + TS], w=['w_vf'])
                S.dve('tensor_tensor', out=Wt['vf'][:], in0=Wt['vf'][:], in1=zv[:, 1:TS + 1], op=ALU.subtract,
                      r=['w_vf', 'zv'], w=['w_vf'])
                S.dve('tensor_tensor', out=Wt['vf'][:], in0=Wt['vf'][:], in1=Wt['lam'][:], op=ALU.mult,
                      r=['w_vf', 'w_lam'], w=['w_vf'])
                S.dve('tensor_tensor', out=zv[:, 1:TS + 1], in0=zv[:, 1:TS + 1], in1=Wt['vf'][:], op=ALU.add,
                      r=['w_vf', 'zv'], w=['zv'])
            pt, pk = C.pst()
            S.pe('matmul', pt[:, 0:TS], w2_sb[:, pc], tzw[:], start=True, stop=True,
                 r=['w2c', 'tzw'], w=[pk])
            S.act('activation', out=Wt['ld'][:], in_=pt[:, 0:TS], func=AF.Sigmoid, bias=V('w0', p),
                  r=[pk, 'vec'], w=['w_ld'])
            S.dve('tensor_scalar', out=Wt['ld'][:], in0=Wt['ld'][:], scalar1=-float(np.exp(-0.5)), scalar2=None,
                                            op0=ALU.mult, r=['w_ld'], w=['w_ld'])
            S.dve('tensor_tensor_scan', out=Wt['cl'][:], data0=rst, data1=Wt['ld'][:], initial=0.0,
                                                 op0=ALU.mult, op1=ALU.add, r=['w_ld', 'cst'], w=['w_cl'])
            pt, pk = C.pst()
            S.pe('matmul', pt[:, 0:TS], a2_sb[:, pc], zap[:, 1:TS + 1], start=True, stop=True,
                 r=['a2c', 'zap'], w=[pk])
            S.act('activation', out=Wt['a'][:], in_=pt[:, 0:TS], func=AF.Sigmoid, bias=V('a0', p),
                  r=[pk, 'vec'], w=['w_a'])
            pt, pk = C.pst()
            S.pe('matmul', pt[:, 0:TS], g2_sb[:, pc], szg[:], start=True, stop=True,
                 r=['g2c', 'szg'], w=[pk])
            S.act('activation', out=P_['gT'][:], in_=pt[:, 0:TS], func=AF.Identity, r=[pk], w=[K_('gT')])
            S.dve('tensor_scalar', out=Wt['kk'][:], in0=zk[:, 1:TS + 1], scalar1=V('kk', p), scalar2=None,
                                                 op0=ALU.mult, r=['zk', 'vec'], w=['w_kk'])
            S.pool('tensor_tensor', out=Wt['sq'][:], in0=Wt['kk'][:], in1=Wt['kk'][:], op=ALU.mult,
                   r=['w_kk'], w=['w_sq'])
            pt, pk = C.pst()
            S.pe('matmul', pt[:, 0:TS], bones, Wt['sq'][:], start=True, stop=True, r=['cst', 'w_sq'], w=[pk])
            S.act('activation', out=Wt['sq'][:], in_=pt[:, 0:TS], func=AF.Sqrt, bias=1e-12, r=[pk], w=['w_sq'])
            S.dve('reciprocal', out=Wt['sq'][:], in_=Wt['sq'][:], r=['w_sq'], w=['w_sq'])
            S.pool('tensor_tensor', out=Wt['kkn'][:], in0=Wt['kk'][:], in1=Wt['sq'][:], op=ALU.mult,
                  r=['w_kk', 'w_sq'], w=['w_kkn'])
            S.dve('tensor_scalar', out=Wt['km'][:], in0=Wt['a'][:], scalar1=V('ka', p),
                                                 scalar2=omv[:, vcol('ka', p):vcol('ka', p) + 1], op0=ALU.mult, op1=ALU.add,
                  r=['w_a', 'vec', 'omv'], w=['w_km'])
            S.pool('tensor_tensor', out=Wt['km'][:], in0=Wt['km'][:], in1=zk[:, 1:TS + 1], op=ALU.mult,
                  r=['w_km', 'zk'], w=['w_km'])
            S.act('activation', out=Wt['epos'][:], in_=Wt['cl'][:], func=AF.Exp, r=['w_cl'], w=['w_epos'])
            S.act('activation', out=Wt['eneg'][:], in_=Wt['cl'][:], func=AF.Exp, scale=-1.0, r=['w_cl'], w=['w_eneg'])
            S.pool('tensor_tensor', out=Wt['eprev'][:], in0=Wt['cl'][:], in1=Wt['ld'][:], op=ALU.subtract,
                  r=['w_cl', 'w_ld'], w=['w_eprev'])
            S.act('activation', out=Wt['eprev'][:], in_=Wt['eprev'][:], func=AF.Exp, r=['w_eprev'], w=['w_eprev'])
            S.dve('scalar_tensor_tensor', out=P_['At'][:], in0=Wt['kkn'][:], scalar=-1.0, in1=Wt['eprev'][:],
                                                   op0=ALU.mult, op1=ALU.mult, r=['w_kkn', 'w_eprev'], w=[K_('At')])
            S.pool('tensor_tensor', out=Wt['Bt'][:], in0=Wt['kkn'][:], in1=Wt['a'][:], op=ALU.mult,
                   r=['w_kkn', 'w_a'], w=['w_Bt'])
            S.pool('tensor_tensor', out=Wt['Bt'][:], in0=Wt['Bt'][:], in1=Wt['eneg'][:], op=ALU.mult,
                  r=['w_Bt', 'w_eneg'], w=['w_Bt'])
            S.pool('tensor_tensor', out=Wt['Kt'][:], in0=Wt['km'][:], in1=Wt['eneg'][:], op=ALU.mult,
                   r=['w_km', 'w_eneg'], w=['w_Kt'])
            S.pool('tensor_tensor', out=P_['Rt'][:], in0=zr[:, 1:TS + 1], in1=Wt['epos'][:], op=ALU.mult,
                  r=['zr', 'w_epos'], w=[K_('Rt')])
            S.dve('scalar_tensor_tensor', out=Wt['rkr'][:], in0=zr[:, 1:TS + 1], scalar=V('rk', p), in1=Wt['km'][:],
                                                        op0=ALU.mult, op1=ALU.mult, r=['zr', 'vec', 'w_km'], w=['w_rkr'])
            pt, pk = C.pst()
            S.pe('matmul', pt[:, 0:TS], bones, Wt['rkr'][:], start=True, stop=True, r=['cst', 'w_rkr'], w=[pk])
            S.dve('tensor_tensor', out=P_['bon'][:], in0=pt[:, 0:TS], in1=zv[:, 1:TS + 1], op=ALU.mult,
                  r=[pk, 'zv'], w=[K_('bon')])
            S.dve('tensor_copy', out=gam[:, p, :], in_=Wt['epos'][:].rearrange("q (c t) -> q c t", t=64)[:, :, 63],
                  r=['w_epos'], w=['gam'])
            to_bd('act', BD['At'], 'bd_At', P_['At'][:], [K_('At')])
            to_bd('pool', BD['Bt'], 'bd_Bt', Wt['Bt'][:], ['w_Bt'])
            to_bd('act', BD['Kt'], 'bd_Kt', Wt['Kt'][:], ['w_Kt'])
            to_bd('pool', BD['V'], 'bd_V', zv[:, 1:TS + 1], ['zv'])
            for (bdn, nm) in (('Bt', 'Btok'), ('Kt', 'Ktok'), ('V', 'Vtok')):
                pt, pk = C.pst()
                for c in range(NCH):
                    S.pe('matmul', pt[:, c * 64:(c + 1) * 64], BD[bdn][:, c, :], ident, start=True, stop=True,
                         r=['bd_' + bdn, 'cst'], w=[pk])
                S.act('activation', out=TOK[p][nm][:], in_=v3(pt[:, 0:TS]), func=AF.Identity, r=[pk], w=[K_(nm)])
            pt, pk = product_bd(BD['At'], 'bd_At', Wt['Bt'], 'w_Bt')
            S.dve('tensor_tensor', out=Nm[:], in0=pt[:, 0:TS], in1=m_ts, op=ALU.mult, r=[pk, 'cst'], w=['Nm'])
            to_bd('dve', BD['P0'], 'bd_P0', pt[:, 0:TS], [pk], kind=1)
            pt, pk = product_bd(BD['Bt'], 'bd_Bt', P_['At'], K_('At'))
            S.dve('tensor_tensor', out=NT[:], in0=pt[:, 0:TS], in1=m_st, op=ALU.mult, r=[pk, 'cst'], w=['NT'])
            to_bd('dve', BD['PT0'], 'bd_PT0', pt[:, 0:TS], [pk], kind=2)
            pt, pk = product_bd(BD['Kt'], 'bd_Kt', P_['At'], K_('At'))
            S.dve('tensor_tensor', out=P_['AKT'][:], in0=pt[:, 0:TS], in1=m_st, op=ALU.mult, r=[pk, 'cst'], w=[K_('AKT')])
            pt, pk = product_bd(BD['Bt'], 'bd_Bt', P_['Rt'], K_('Rt'))
            S.dve('tensor_tensor', out=P_['RBT'][:], in0=pt[:, 0:TS], in1=m_sti, op=ALU.mult, r=[pk, 'cst'], w=[K_('RBT')])
            pt, pk = product_bd(BD['Kt'], 'bd_Kt', P_['Rt'], K_('Rt'))
            S.dve('tensor_tensor', out=P_['RKT'][:], in0=pt[:, 0:TS], in1=m_sti, op=ALU.mult, r=[pk, 'cst'], w=[K_('RKT')])
            S.pool('tensor_tensor', out=TTw[0][:], in0=NT[:], in1=eye, op=ALU.add, r=['NT', 'cst'], w=['TTw0'])
            Pc, Pck, PTc, PTck = Nm, 'Nm', NT, 'NT'
            Pbd, Pbdk, PTbd, PTbdk = BD['P0'], 'bd_P0', BD['PT0'], 'bd_PT0'
            tcur = 0
            for lvl in range(5):
                nb = (lvl + 1) % 2
                pt, pk = product_bd(PTbd, PTbdk, Pc, Pck)
                to_bd('dve', BD['IP'], 'bd_IP', pt[:, 0:TS], [pk], kind='eye')
                if lvl < 4:
                    Pn, Pnk = Pw[nb], f"Pw{nb}"
                    S.dve('tensor_copy', out=Pn[:], in_=pt[:, 0:TS], r=[pk], w=[Pnk])
                    Pnbd, Pnbdk = BD[f'P{nb}'], f'bd_P{nb}'
                    to_bd('dve', Pnbd, Pnbdk, pt[:, 0:TS], [pk])
                    pt2, pk2 = product_bd(Pbd, Pbdk, PTc, PTck)
                    PTn, PTnk = PTw[nb], f"PTw{nb}"
                    S.act('activation', out=PTn[:], in_=pt2[:, 0:TS], func=AF.Identity, r=[pk2], w=[PTnk])
                    PTnbd, PTnbdk = BD[f'PT{nb}'], f'bd_PT{nb}'
                    to_bd('act', PTnbd, PTnbdk, pt2[:, 0:TS], [pk2])
                if lvl < 4:
                    dstT, dstk = TTw[1 - tcur], f"TTw{1 - tcur}"
                else:
                    dstT, dstk = P_['TT'], K_('TT')
                pt3, pk3 = product_bd(BD['IP'], 'bd_IP', TTw[tcur], f"TTw{tcur}")
                S.act('activation', out=dstT[:], in_=pt3[:, 0:TS], func=AF.Identity, r=[pk3], w=[dstk])
                tcur = 1 - tcur
                if lvl < 4:
                    Pc, Pck, PTc, PTck = Pn, Pnk, PTn, PTnk
                    Pbd, Pbdk, PTbd, PTbdk = Pnbd, Pnbdk, PTnbd, PTnbdk
            next(lg, None)
        for c in range(NCH):
            sl = slice(c * 64, (c + 1) * 64)
            Hc, Hck = Hb[cur], f"Hb{cur}"
            Hn, Hnk = Hb[1 - cur], f"Hb{1 - cur}"
            psX = []
            for e_ in range(2):
                pb = 64 * e_
                pt, pk = C.pst()
                psX.append((pt, pk))
                for p in range(4):
                    cs = slice(p * 64, (p + 1) * 64)
                    S.pe('matmul', pt[pb:pb + 64, cs], PP[p]['At'][pb:pb + 64, sl],
                                                                      Hc[pb:pb + 64, cs], start=True, stop=False,
                         r=[f"At{p}", Hck], w=[pk])
                    S.pe('matmul', pt[pb:pb + 64, cs], PP[p]['AKT'][pb:pb + 64, sl],
                                                                      TOK[p]['Vtok'][pb:pb + 64, c, :], start=False, stop=True,
                         r=[f"AKT{p}", f"Vtok{p}"], w=[pk])
                S.act('activation', out=Xsb[pb:pb + 64, :], in_=pt[pb:pb + 64, 0:256], func=AF.Identity,
                      r=[pk], w=['Xsb'])
            psU = []
            for e_ in range(2):
                pb = 64 * e_
                pt, pk = C.pst()
                for p in range(4):
                    cs = slice(p * 64, (p + 1) * 64)
                    S.pe('matmul', pt[pb:pb + 64, cs], PP[p]['TT'][pb:pb + 64, sl],
                                                                      Xsb[pb:pb + 64, cs], start=True, stop=True,
                         r=[f"TT{p}", 'Xsb'], w=[pk])
                S.dve('tensor_copy', out=Usb[pb:pb + 64, :], in_=pt[pb:pb + 64, 0:256], r=[pk], w=['Usb'])
            for e_ in range(2):
                pb = 64 * e_
                pt, pk = C.pst()
                for p in range(4):
                    cs = slice(p * 64, (p + 1) * 64)
                    S.pe('matmul', pt[pb:pb + 64, cs], TOK[p]['Btok'][pb:pb + 64, c, :],
                                                                      Usb[pb:pb + 64, cs], start=True, stop=False,
                         r=[f"Btok{p}", 'Usb'], w=[pk])
                    S.pe('matmul', pt[pb:pb + 64, cs], TOK[p]['Ktok'][pb:pb + 64, c, :],
                                                                      TOK[p]['Vtok'][pb:pb + 64, c, :], start=False, stop=True,
                         r=[f"Ktok{p}", f"Vtok{p}"], w=[pk])
                S.dve('tensor_tensor', out=Htmp[pb:pb + 64, :], in0=pt[pb:pb + 64, 0:256],
                                                              in1=Hc[pb:pb + 64, :], op=ALU.add, r=[pk, Hck], w=['Htmp'])
                S.pool('tensor_tensor',
                    out=Hn[pb:pb + 64, :].rearrange("q (a b) -> q a b", b=64),
                    in0=Htmp[pb:pb + 64, :].rearrange("q (a b) -> q a b", b=64),
                    in1=gam[pb:pb + 64, :, c:c + 1].broadcast_to([64, 4, 64]), op=ALU.mult,
                    r=['Htmp', 'gam'], w=[Hnk])
                pt, pk = C.pst()
                for p in range(4):
                    cs = slice(p * 64, (p + 1) * 64)
                    S.pe('matmul', pt[pb:pb + 64, cs], PP[p]['Rt'][pb:pb + 64, sl],
                                                                      Hc[pb:pb + 64, cs], start=True, stop=False,
                         r=[f"Rt{p}", Hck], w=[pk])
                    S.pe('matmul', pt[pb:pb + 64, cs], PP[p]['RBT'][pb:pb + 64, sl],
                                                                      Usb[pb:pb + 64, cs], start=False, stop=False,
                         r=[f"RBT{p}", 'Usb'], w=[pk])
                    S.pe('matmul', pt[pb:pb + 64, cs], PP[p]['RKT'][pb:pb + 64, sl],
                                                                      TOK[p]['Vtok'][pb:pb + 64, c, :], start=False, stop=True,
                         r=[f"RKT{p}", f"Vtok{p}"], w=[pk])
                S.act('activation',
                    out=Ysb[pb:pb + 64, c, :, :], in_=pt[pb:pb + 64, 0:256].rearrange("q (a b) -> q a b", b=64), func=AF.Identity,
                    r=[pk], w=['Ysb'])
            cur = 1 - cur
        Yv = Ysb[:].rearrange("q c p i -> q (c p) i")
        dv = dY[:].rearrange("q c p i -> q (c p) i")
        sv = sqY[:].rearrange("q c p i -> q (c p) i")
        G = NCH * 4
        S.dve('tensor_reduce', out=st1[:], in_=Yv, axis=AX.X, op=ALU.add, r=['Ysb'], w=['st1'])
        S.dve('tensor_scalar', out=st1[:], in0=st1[:], scalar1=-1.0 / 64, scalar2=None, op0=ALU.mult, r=['st1'], w=['st1'])
        S.dve('tensor_tensor', out=dv, in0=Yv, in1=st1[:].unsqueeze(2).broadcast_to([128, G, 64]), op=ALU.add,
              r=['Ysb', 'st1'], w=['dY'])
        S.pool('tensor_tensor', out=sv, in0=dv, in1=dv, op=ALU.mult, r=['dY'], w=['sqY'])
        S.dve('tensor_reduce', out=st2[:], in_=sv, axis=AX.X, op=ALU.add, r=['sqY'], w=['st2'])
        S.dve('tensor_scalar', out=st2[:], in0=st2[:], scalar1=1.0 / 64, scalar2=GN_EPS, op0=ALU.mult, op1=ALU.add,
              r=['st2'], w=['st2'])
        S.act('activation', out=st2[:], in_=st2[:], func=AF.Sqrt, r=['st2'], w=['st2'])
        S.dve('reciprocal', out=st2[:], in_=st2[:], r=['st2'], w=['st2'])
        S.dve('tensor_tensor', out=dv, in0=dv, in1=st2[:].unsqueeze(2).broadcast_to([128, G, 64]), op=ALU.mult,
              r=['dY', 'st2'], w=['dY'])
        for p in range(4):
            for e_ in range(2):
                pb = 64 * e_
                pt, pk = C.pst()
                for c in range(NCH):
                    S.pe('matmul', pt[pb:pb + 64, c * 64:(c + 1) * 64], dY[pb:pb + 64, c, p, :],
                                                                    ident[pb:pb + 64, :], start=True, stop=True,
                         r=['dY', 'cst'], w=[pk])
                S.act('activation',
                    out=ycs[p][pb:pb + 64, :], in_=pt[pb:pb + 64, 0:TS], func=AF.Identity,
                    scale=vec_sb[pb:pb + 64, vcol('lnw', p):vcol('lnw', p) + 1],
                    bias=vec_sb[pb:pb + 64, vcol('lnb', p):vcol('lnb', p) + 1], r=[pk, 'vec'], w=[f"ycs{p}"])
            S.dve('tensor_tensor', out=ycs[p][:], in0=ycs[p][:], in1=PP[p]['bon'][:], op=ALU.add,
                  r=[f"ycs{p}", f"bon{p}"], w=[f"ycs{p}"])
            S.pool('tensor_tensor', out=ycs[p][:], in0=ycs[p][:], in1=PP[p]['gT'][:], op=ALU.mult,
                   r=[f"ycs{p}", f"gT{p}"], w=[f"ycs{p}"])
            S.dma(out=io[f"ys{5 + p}"][:, t0:t0 + TS], in_=ycs[p][:], r=[f"ycs{p}"])
    S.replay_side()
    C.finish()


def pair_rows(hh):
    idx = np.zeros(512, np.int64)
    for p in range(4):
        for e in range(2):
            h = hh * 8 + p + 4 * e
            idx[p * 128 + e * 64:p * 128 + e * 64 + 64] = h * 64 + np.arange(64)
    return idx


def arr128(v):
    return np.ascontiguousarray(v.reshape(-1, 128).T)


def rwkv_inputs(inp, l, b, hh, xT_b, vfT=None):
    idx = pair_rows(hh)
    w_in = inp['w_in'][l]
    base = 1024 + 1024 + 768 * 3
    cols = np.concatenate([base + idx, base + 1024 + idx, base + 2048 + idx,
                           base + 3072 + np.arange(64), base + 3136 + np.arange(64), base + 3200 + np.arange(128)])
    wc = w_in[:, cols]
    if l:
        wc = np.concatenate([wc, inp['w_in_vres'][l - 1]], axis=1)
    mu = inp['rwkv_mu'][l]
    vec = np.zeros((128, NVEC), np.float32)

    def setv(name, full):
        for p in range(4):
            vec[:, vcol(name, p)] = full[idx[p * 128:(p + 1) * 128]]
    setv('mu_r', mu[0:1024]); setv('mu_k', mu[1024:2048]); setv('mu_v', mu[2048:3072])
    setv('w0', inp['w0'][l]); setv('a0', inp['a0'][l]); setv('kk', inp['k_k'][l]); setv('ka', inp['k_a'][l])
    setv('rk', inp['r_k'][l].reshape(-1)); setv('lnw', inp['ln_x_w'][l]); setv('lnb', inp['ln_x_b'][l])
    if l:
        setv('v0', inp['v0'][l - 1])
        vec[0:32, V_MUZV1] = inp['mu_vres'][l - 1]
    vec[0:64, V_MUZW] = mu[3072:3136]
    vec[0:64, V_MUZA] = mu[3136:3200]
    vec[:, V_MUZG] = mu[3200:3328]
    d = {
        'xT': xT_b, 'cT': arr128(inp['c'][b]),
        'modw': np.ascontiguousarray(inp['mod_w'][l][:, 0:2048]),
        'modb': arr128(inp['mod_b'][l][0:2048]),
        'wc': np.ascontiguousarray(wc), 'vec': vec,
        'w2c': np.ascontiguousarray(inp['w2'][l][:, idx]), 'a2c': np.ascontiguousarray(inp['a2'][l][:, idx]),
        'g2c': np.ascontiguousarray(inp['g2'][l][:, idx]), 'cst': rwkv_consts(),
    }
    if l:
        d['v2c'] = np.ascontiguousarray(inp['v2'][l - 1][:, idx])
    return d


def common_inputs(C):
    xT = C.din("xT", [1024, SEQ])
    cT = C.din("cT", [128, 8])
    modw = C.din("modw", [1024, 2048])
    modb = C.din("modb", [128, 16])
    return xT, cT, modw, modb


def common_mod(C, cT, modw, modb):
    S = C.S
    cT_sb = C.sb("cT", [128, 8]); S.dma(out=cT_sb[:], in_=cT, w=['cT'])
    modb_sb = C.sb("modb", [128, 16]); S.dma(out=modb_sb[:], in_=modb, w=['modb'])
    mod_sb = C.sb("mod", [128, 16])
    emit_mod(C, modw, modb_sb, cT_sb, 2048, mod_sb)
    S.dve('tensor_scalar', out=mod_sb[:, 8:16], in0=mod_sb[:, 8:16], scalar1=1.0, scalar2=None, op0=ALU.add,
          r=['mod'], w=['mod'])
    return mod_sb


LT = 256
LV_NAMES = ['cw0', 'cw1', 'cw2', 'cw3', 'cb', 'ba', 'bx', 'lam']


def emit_lru(nc, io, layer):
    C = Ctx(nc, f"lr{layer}")
    C.init_psum()
    for _ in lru_body(C, io, layer):
        pass
    C.finish()


def lru_body(C, io, layer):
    S = C.S
    wa = io[f'wa_{layer}']; vecl = io[f'vecl_{layer}']; gw = io[f'gw_{layer}']
    mod_sb = load_mod(C, io, layer)
    vl = C.sb("vecl", [128, 32]); S.dma(out=vl[:], in_=vecl, w=['vecl'])
    gw_sb = C.sb("gw", [128, 8, 128]); S.dma(out=gw_sb[:], in_=gw, w=['gw'])
    wab = C.sb("wab", [128, 8, 1024], BF16)
    wav = wa.rearrange("(k p) c -> p k c", p=128)
    for c0 in (0, 512):
        S.dma(out=wab[:, :, c0:c0 + 512], in_=wav[:, :, c0:c0 + 512], w=['wab'], q='pool')
    if not _os.environ.get("KDEBUG_NOCAST"):
        for m in range(8):
            S.dma(out=io[f'WGb_{layer}'][m].rearrange("p (k c) -> p k c", k=8),
                  in_=io[f'WG_{layer}'][m].rearrange("(k p) c -> p k c", p=128), q='pool')
            S.dma(out=io[f'WPb_{layer}'][m].rearrange("p (k c) -> p k c", k=18),
                  in_=io[f'WP_{layer}'][m].rearrange("(k p) c -> p k c", p=128), q='pool')
            S.dma(out=io[f'WOb_{layer}'][m].rearrange("p (k c) -> p k c", k=8),
                  in_=io[f'WO_{layer}'].rearrange("(k p) c -> p k c", p=128)[:, :, m * 128:(m + 1) * 128], q='pool')
            S.dma(out=io[f'WDb_{layer}'][m].rearrange("p (k c) -> p k c", k=22),
                  in_=io[f'WD_{layer}'].rearrange("(k p) c -> p k c", p=128)[:, :, m * 128:(m + 1) * 128], q='pool')
        for j in range(22):
            S.dma(out=io[f'WUb_{layer}'][j].rearrange("p (k c) -> p k c", k=8),
                  in_=io[f'WU_{layer}'][j].rearrange("(k p) c -> p k c", p=128), q='pool')
    LVc = lambda n, j: vl[:, LV_NAMES.index(n) * 4 + j:LV_NAMES.index(n) * 4 + j + 1]
    clam = C.sb("clam", [128, 4])
    S.act('activation', out=clam[:], in_=vl[:, 28:32], func=AF.Exp, scale=-1.0, r=['vecl'], w=['clam'])
    S.act('activation', out=clam[:], in_=clam[:], func=AF.Ln, bias=1.0, r=['clam'], w=['clam'])
    S.dve('tensor_scalar', out=clam[:], in0=clam[:], scalar1=-8.0, scalar2=None, op0=ALU.mult, r=['clam'], w=['clam'])
    hs = C.sb("hs", [128, 8, LT], BF16)
    xk = [C.sb("xk0", [128, LT]), C.sb("xk1", [128, LT])]
    xal = C.sb("xal", [128, 4, 3])
    S.pool('memset', xal[:], 0.0, w=['xal'])
    xa = C.sb("xa", [128, LT + 3]); gg = C.sb("gg", [128, LT]); xc = C.sb("xc", [128, LT])
    rr = C.sb("rr", [128, LT]); ii = C.sb("ii", [128, LT]); aa = C.sb("aa", [128, LT]); om = C.sb("om", [128, LT])
    hq = C.sb("hq", [128, LT]); carry = C.sb("carry", [128, 4])
    S.pool('memset', carry[:], 0.0, w=['carry'])
    yield
    for sg in range(SEQ // LT):
        t0 = sg * LT
        for k in range(8):
            xb, xkey = xk[k % 2], f"xk{k % 2}"
            for (o_, n_, src_) in x_pieces(io, layer, k, t0, t0 + LT):
                S.dma(out=xb[:, o_:o_ + n_], in_=src_, w=[xkey], allow_slow_non_contiguous=(n_ < 2))
            S.act('activation', out=hs[:, k, :], in_=xb[:], func=AF.Identity,
                  scale=mod_sb[:, 8 + k:9 + k], bias=mod_sb[:, k:k + 1], r=[xkey, 'mod'], w=['hs'])
        for j in range(4):
            pt, pk = C.pst()
            for k in range(8):
                S.pe('matmul', pt[:, 0:LT], wab[:, k, j * 128:(j + 1) * 128], hs[:, k, :],
                     start=(k == 0), stop=(k == 7), r=['wab', 'hs'], w=[pk])
            S.pool('tensor_copy', out=xa[:, 0:3], in_=xal[:, j, :], r=['xal'], w=['xa'])
            S.act('activation', out=xa[:, 3:LT + 3], in_=pt[:, 0:LT], func=AF.Identity, r=[pk], w=['xa'])
            S.pool('tensor_copy', out=xal[:, j, :], in_=xa[:, LT:LT + 3], r=['xa'], w=['xal'])
            pt, pk = C.pst()
            for k in range(8):
                S.pe('matmul', pt[:, 0:LT], wab[:, k, 512 + j * 128:512 + (j + 1) * 128], hs[:, k, :],
                     start=(k == 0), stop=(k == 7), r=['wab', 'hs'], w=[pk])
            S.act('activation', out=gg[:], in_=pt[:, 0:LT], func=AF.Gelu, r=[pk], w=['gg'])
            S.dve('tensor_scalar', out=xc[:], in0=xa[:, 3:LT + 3], scalar1=LVc('cw0', j), scalar2=LVc('cb', j),
                  op0=ALU.mult, op1=ALU.add, r=['xa', 'vecl'], w=['xc'])
            for jj in (1, 2, 3):
                S.dve('scalar_tensor_tensor', out=xc[:], in0=xa[:, 3 - jj:LT + 3 - jj], scalar=LVc(f'cw{jj}', j), in1=xc[:],
                      op0=ALU.mult, op1=ALU.add, r=['xa', 'vecl', 'xc'], w=['xc'])
            pt, pk = C.pst()
            S.pe('matmul', pt[:, 0:LT], gw_sb[:, j, :], xc[:], start=True, stop=True, r=['gw', 'xc'], w=[pk])
            S.act('activation', out=rr[:], in_=pt[:, 0:LT], func=AF.Sigmoid, bias=LVc('ba', j), r=[pk, 'vecl'], w=['rr'])
            pt, pk = C.pst()
            S.pe('matmul', pt[:, 0:LT], gw_sb[:, 4 + j, :], xc[:], start=True, stop=True, r=['gw', 'xc'], w=[pk])
            S.act('activation', out=ii[:], in_=pt[:, 0:LT], func=AF.Sigmoid, bias=LVc('bx', j), r=[pk, 'vecl'], w=['ii'])
            S.act('activation', out=aa[:], in_=rr[:], func=AF.Exp, scale=clam[:, j:j + 1], r=['rr', 'clam'], w=['aa'])
            S.pool('tensor_tensor', out=om[:], in0=aa[:], in1=aa[:], op=ALU.mult, r=['aa'], w=['om'])
            S.dve('tensor_scalar', out=om[:], in0=om[:], scalar1=-1.0, scalar2=1.0, op0=ALU.mult, op1=ALU.add, r=['om'], w=['om'])
            S.dve('tensor_scalar', out=om[:], in0=om[:], scalar1=1e-30, scalar2=None, op0=ALU.max, r=['om'], w=['om'])
            S.act('activation', out=om[:], in_=om[:], func=AF.Sqrt, r=['om'], w=['om'])
            S.dve('tensor_tensor', out=ii[:], in0=ii[:], in1=xc[:], op=ALU.mult, r=['ii', 'xc'], w=['ii'])
            S.dve('tensor_tensor', out=ii[:], in0=ii[:], in1=om[:], op=ALU.mult, r=['ii', 'om'], w=['ii'])
            S.dve('tensor_tensor_scan', out=hq[:], data0=aa[:], data1=ii[:], initial=carry[:, j:j + 1], op0=ALU.mult, op1=ALU.add,
                  r=['aa', 'ii', 'carry'], w=['hq'])
            S.dve('tensor_copy', out=carry[:, j:j + 1], in_=hq[:, LT - 1:LT], r=['hq'], w=['carry'])
            S.pool('tensor_tensor', out=hq[:], in0=hq[:], in1=gg[:], op=ALU.mult, r=['hq', 'gg'], w=['hq'])
            S.dma(out=io[f'ys{j}'][:, t0:t0 + LT], in_=hq[:], r=['hq'])
            yield


def mod_inputs(inp, l, b, xT_b, lo=0):
    return {'xT': xT_b, 'cT': arr128(inp['c'][b]),
            'modw': np.ascontiguousarray(inp['mod_w'][l][:, lo:lo + 2048]),
            'modb': arr128(inp['mod_b'][l][lo:lo + 2048])}


def lru_inputs(inp, l, b, hh, xT_b):
    d = mod_inputs(inp, l, b, xT_b)
    ch = hh * 512 + np.arange(512)
    w_in = inp['w_in'][l]
    d['wa'] = np.ascontiguousarray(np.concatenate([w_in[:, ch], w_in[:, 1024 + ch]], axis=1))
    vecl = np.zeros((128, 32), np.float32)
    srcs = {'cw0': inp['conv_a_w'][l][0], 'cw1': inp['conv_a_w'][l][1], 'cw2': inp['conv_a_w'][l][2],
            'cw3': inp['conv_a_w'][l][3], 'cb': inp['conv_a_b'][l], 'ba': inp['lru_ba'][l], 'bx': inp['lru_bx'][l],
            'lam': inp['lru_lambda'][l]}
    for n, v in srcs.items():
        vecl[:, LV_NAMES.index(n) * 4:LV_NAMES.index(n) * 4 + 4] = arr128(v[ch])
    d['vecl'] = vecl
    gw = np.zeros((128, 8, 128), np.float32)
    for j in range(4):
        for e in range(2):
            g = hh * 8 + j * 2 + e
            gw[e * 64:(e + 1) * 64, j, e * 64:(e + 1) * 64] = inp['lru_wa'][l][g]
            gw[e * 64:(e + 1) * 64, 4 + j, e * 64:(e + 1) * 64] = inp['lru_wx'][l][g]
    d['gw'] = gw
    return d


AT = 512
DMAXS = (1, 4, 16)
DILS = (1, 4, 16)
NSTRIP = sum(d + 1 for d in DMAXS)
TWO_PI = float(2 * np.pi)
CW1 = 6.28125
CW2 = float(2 * np.pi - 6.28125)


def attn_strips():
    st = np.zeros((128, NSTRIP, 128), np.float32)
    tk = np.arange(128)[:, None]
    tq = np.arange(128)[None, :]
    o = 0
    for g in range(3):
        dil, dm = DILS[g], DMAXS[g]
        for bi in range(dm + 1):
            delta = dm - bi
            dist = 128 * delta + tq - tk
            st[:, o + bi, :] = (dist >= 0) & (dist <= 128 * dil) & (dist % dil == 0)
        o += dm + 1
    return st.reshape(128, NSTRIP * 128)


def emit_attn(nc, io, layer):
    C = Ctx(nc, f"at{layer}"); S = C.S
    wqk = io[f'wqk_{layer}']; wv = io[f'wv_{layer}']
    pos = io['pos']; freq = io['freq']; strips = io['strips']
    ybT = io['ys4']
    C.init_psum()
    mod_sb = load_mod(C, io, layer)
    fr = C.sb("freq", [128, 1]); S.dma(out=fr[:], in_=freq, w=['freq'])
    strip_sb = C.sb("strips", [128, NSTRIP * 128]); S.dma(out=strip_sb[:], in_=strips, w=['strips'])
    wqkb = C.sb("wqkb", [128, 8, 768], BF16)
    S.dma(out=wqkb[:, :, 0:384], in_=wqk.rearrange("(k p) c -> p k c", p=128)[:, :, 0:384], w=['wqkb'], q='pool')
    S.dma(out=wqkb[:, :, 384:768], in_=wqk.rearrange("(k p) c -> p k c", p=128)[:, :, 384:768], w=['wqkb'], q='pool')
    wvb = C.sb("wvb", [128, 8, 384], BF16)
    S.dma(out=wvb[:], in_=wv.rearrange("(k p) c -> p k c", p=128), w=['wvb'], q='pool')
    W2 = C.sb("W2", [128, 8, 768], BF16)
    S.pool('memset', W2[:], 0.0, w=['W2'])
    for ch in range(6):
        for hd in range(2):
            bs = ch * 128 + hd * 64
            S.dve('tensor_scalar', out=W2[:, :, bs:bs + 8], in0=wqkb[:, :, bs + 8:bs + 16], scalar1=-1.0, scalar2=None,
                  op0=ALU.mult, r=['wqkb', 'W2'], w=['W2'])
            S.dve('tensor_copy', out=W2[:, :, bs + 8:bs + 16], in_=wqkb[:, :, bs:bs + 8], r=['wqkb', 'W2'], w=['W2'])
    ones_bf = C.sb("ones_bf", [128, 64], BF16)
    S.pool('memset', ones_bf[:], 1.0, w=['ones_bf'])
    hs = C.sb("hs", [128, 8, AT], BF16)
    xk = [C.sb("xk0", [128, AT]), C.sb("xk1", [128, AT])]
    qk = [C.sb(f"qk{ch}", [128, SEQ], BF16) for ch in range(6)]
    Vtok = C.sb("Vtok", [128, SEQ // 128, 384], BF16)
    posi = C.sb("posi", [128, AT], I32)
    ang = C.sb("ang", [128, AT]); a2 = C.sb("a2", [128, AT]); tq_ = C.sb("tq", [128, AT]); ki = C.sb("ki", [128, AT], I32)
    kf = C.sb("kf", [128, AT]); mk = C.sb("mk", [128, AT])
    Ct = C.sb("Ct", [128, AT]); St = C.sb("St", [128, AT])
    t1 = C.sb("t1", [128, AT]); t2 = C.sb("t2", [128, AT])
    Pbuf = C.sb("Pbuf", [128, 2, NSTRIP, 128], BF16)
    rd = C.sb("rd", [64, 2, 128]); yb = C.sb("yb", [64, 2, 128])

    for tl in range(SEQ // AT):
        t0 = tl * AT
        for k in range(8):
            xb, xkey = xk[k % 2], f"xk{k % 2}"
            for (o_, n_, src_) in x_pieces(io, layer, k, t0, t0 + AT):
                S.dma(out=xb[:, o_:o_ + n_], in_=src_, w=[xkey], allow_slow_non_contiguous=(n_ < 2))
            S.act('activation', out=hs[:, k, :], in_=xb[:], func=AF.Identity,
                  scale=mod_sb[:, 8 + k:9 + k], bias=mod_sb[:, k:k + 1], r=[xkey, 'mod'], w=['hs'])
        S.dma(out=posi[:], in_=pos[:, t0:t0 + AT].partition_broadcast(128), w=['posi'])
        S.dve('tensor_copy', out=ang[:], in_=posi[:], r=['posi'], w=['ang'])
        S.dve('tensor_scalar', out=ang[:], in0=ang[:], scalar1=fr[:, 0:1], scalar2=None, op0=ALU.mult, r=['ang', 'freq'], w=['ang'])
        for (tbl, tkey, shf) in ((St, 'St', 0.0), (Ct, 'Ct', float(np.pi / 2))):
            S.dve('tensor_scalar', out=a2[:], in0=ang[:], scalar1=shf, scalar2=None, op0=ALU.add, r=['ang'], w=['a2'])
            S.dve('tensor_scalar', out=tq_[:], in0=a2[:], scalar1=1.0 / TWO_PI, scalar2=None, op0=ALU.mult, r=['a2'], w=['tq'])
            S.dve('tensor_copy', out=ki[:], in_=tq_[:], r=['tq'], w=['ki'])
            S.dve('tensor_copy', out=kf[:], in_=ki[:], r=['ki'], w=['kf'])
            S.dve('scalar_tensor_tensor', out=a2[:], in0=kf[:], scalar=-CW1, in1=a2[:], op0=ALU.mult, op1=ALU.add,
                  r=['kf', 'a2'], w=['a2'])
            S.dve('scalar_tensor_tensor', out=a2[:], in0=kf[:], scalar=-CW2, in1=a2[:], op0=ALU.mult, op1=ALU.add,
                  r=['kf', 'a2'], w=['a2'])
            S.dve('tensor_scalar', out=mk[:], in0=a2[:], scalar1=float(np.pi), scalar2=None, op0=ALU.is_gt, r=['a2'], w=['mk'])
            S.dve('scalar_tensor_tensor', out=a2[:], in0=mk[:], scalar=-TWO_PI, in1=a2[:], op0=ALU.mult, op1=ALU.add,
                  r=['mk', 'a2'], w=['a2'])
            S.dve('tensor_scalar', out=mk[:], in0=a2[:], scalar1=-float(np.pi), scalar2=None, op0=ALU.is_lt, r=['a2'], w=['mk'])
            S.dve('scalar_tensor_tensor', out=a2[:], in0=mk[:], scalar=TWO_PI, in1=a2[:], op0=ALU.mult, op1=ALU.add,
                  r=['mk', 'a2'], w=['a2'])
            S.dve('tensor_scalar', out=a2[:], in0=a2[:], scalar1=-3.1415925, scalar2=3.1415925, op0=ALU.max, op1=ALU.min,
                  r=['a2'], w=['a2'])
            S.act('activation', out=tbl[:], in_=a2[:], func=AF.Sin, r=['a2'], w=[tkey])
        for ch in range(6):
            p1, p1k = C.pst()
            for k in range(8):
                S.pe('matmul', p1[:, 0:AT], wqkb[:, k, ch * 128:(ch + 1) * 128], hs[:, k, :], start=(k == 0), stop=(k == 7),
                     r=['wqkb', 'hs'], w=[p1k])
            p2, p2k = C.pst()
            for k in range(8):
                S.pe('matmul', p2[:, 0:AT], W2[:, k, ch * 128:(ch + 1) * 128], hs[:, k, :], start=(k == 0), stop=(k == 7),
                     r=['W2', 'hs'], w=[p2k])
            S.dve('tensor_tensor', out=t1[:], in0=p1[:, 0:AT], in1=Ct[:], op=ALU.mult, r=[p1k, 'Ct'], w=['t1'])
            S.dve('tensor_tensor', out=t2[:], in0=p2[:, 0:AT], in1=St[:], op=ALU.mult, r=[p2k, 'St'], w=['t2'])
            S.pool('tensor_tensor', out=qk[ch][:, t0:t0 + AT], in0=t1[:], in1=t2[:], op=ALU.add, r=['t1', 't2'], w=[f"qk{ch}"])
        for sub in range(AT // 128):
            blk = tl * (AT // 128) + sub
            pt, pk = C.pst()
            for k in range(8):
                S.pe('matmul', pt[:, 0:384], hs[:, k, sub * 128:(sub + 1) * 128], wvb[:, k, :], start=(k == 0), stop=(k == 7),
                     r=['hs', 'wvb'], w=[pk])
            S.act('activation', out=Vtok[:, blk, :], in_=pt[:, 0:384], func=AF.Identity, r=[pk], w=['Vtok'])

    for i in range(SEQ // 128):
        entries = []
        j = 0
        for g in range(3):
            kb0 = max(0, i - DMAXS[g])
            sbase = sum(d + 1 for d in DMAXS[:g])
            kbs = list(range(kb0, i + 1))
            for u0 in range(0, len(kbs), 4):
                grp = kbs[u0:u0 + 4]
                n = len(grp)
                for s in range(2):
                    pt, pk = C.pst()
                    for u, kb in enumerate(grp):
                        S.pe('matmul', pt[:, u * 128:(u + 1) * 128], qk[3 + g][s * 64:(s + 1) * 64, kb * 128:(kb + 1) * 128],
                             qk[g][s * 64:(s + 1) * 64, i * 128:(i + 1) * 128], start=True, stop=True,
                             r=[f"qk{3 + g}", f"qk{g}"], w=[pk])
                    S.act('activation', out=Pbuf[:, s, j:j + n, :], in_=pt[:, 0:n * 128].rearrange("p (n q) -> p n q", q=128),
                          func=AF.Exp, scale=0.125, r=[pk], w=['Pbuf'])
                so = sbase + (DMAXS[g] - (i - grp[0]))
                mview = strip_sb[:, so * 128:(so + n) * 128].rearrange("p (n q) -> p n q", q=128).unsqueeze(1).broadcast_to([128, 2, n, 128])
                S.dve('tensor_tensor', out=Pbuf[:, :, j:j + n, :], in0=Pbuf[:, :, j:j + n, :], in1=mview, op=ALU.mult,
                      r=['Pbuf', 'strips'], w=['Pbuf'])
                for kb in grp:
                    entries.append((g, kb, j))
                    j += 1
        pt, pk = C.pst()
        ne = len(entries)
        for s in range(2):
            for idx, (g, kb, jj) in enumerate(entries):
                S.pe('matmul', pt[0:64, s * 128:(s + 1) * 128], Vtok[:, kb, g * 128 + s * 64:g * 128 + (s + 1) * 64], Pbuf[:, s, jj, :],
                     start=(idx == 0), stop=(idx == ne - 1), r=['Vtok', 'Pbuf'], w=[pk])
        for idx, (g, kb, jj) in enumerate(entries):
            S.pe('matmul', pt[0:64, 256:512], ones_bf[:], Pbuf[:, :, jj, :],
                 start=(idx == 0), stop=(idx == ne - 1), r=['ones_bf', 'Pbuf'], w=[pk])
        S.dve('reciprocal', out=rd[:], in_=pt[0:64, 256:512].rearrange("p (s q) -> p s q", s=2), r=[pk], w=['rd'])
        S.dve('tensor_tensor', out=yb[:], in0=pt[0:64, 0:256].rearrange("p (s q) -> p s q", s=2), in1=rd[:], op=ALU.mult,
              r=[pk, 'rd'], w=['yb'])
        S.dma(out=ybT.rearrange("(s d) t -> d s t", d=64)[:, :, i * 128:(i + 1) * 128], in_=yb[:], r=['yb'])
    C.finish()


def attn_inputs(inp, l, b, hh, xT_b):
    d = mod_inputs(inp, l, b, xT_b)
    w_in = inp['w_in'][l]
    qb, kb_, vb = 2048, 2048 + 768, 2048 + 1536
    cols = np.concatenate([(g * 4 + 2 * hh) * 64 + np.arange(128) for g in range(3)])
    d['wqk'] = np.ascontiguousarray(np.concatenate([w_in[:, qb + cols], w_in[:, kb_ + cols]], axis=1))
    d['wv'] = np.ascontiguousarray(w_in[:, vb + cols])
    d['pos'] = np.ascontiguousarray(inp['positions'][b][None, :]).astype(np.int32)
    fr = np.zeros((128, 1), np.float32)
    inv = (500000.0 ** (-np.arange(8, dtype=np.float32) / np.float32(8))).astype(np.float32)
    for p in range(128):
        if p % 64 < 16:
            fr[p, 0] = inv[(p % 64) % 8]
    d['freq'] = fr
    d['strips'] = attn_strips()
    return d


DN = 410
DTOK = 2050
NDT = 5
DV_NAMES = ['ln1w', 'ln1b', 'ln2w', 'ln2b']
FV_NAMES = ['fw0', 'fw1', 'fw2', 'fb']


def emit_dense(nc, io, layer):
    C = Ctx(nc, f"dn{layer}"); S = C.S
    WG = io[f'WGb_{layer}']; WP = io[f'WPb_{layer}']; WO = io[f'WOb_{layer}']; WU = io[f'WUb_{layer}']; WD = io[f'WDb_{layer}']
    dvec = io[f'dvec_{layer}']; fvec = io[f'fvec_{layer}']
    flag = io['flag']; onesm = io['onesm']
    C.init_psum()
    mod_sb = load_mod(C, io, layer)
    SH1, SC1, GT1, SH2, SC2, GT2 = 0, 8, 16, 24, 32, 40
    dv = C.sb("dvec", [128, 32]); S.dma(out=dv[:], in_=dvec, w=['dvec'])
    fv = C.sb("fvec", [128, 176]); S.dma(out=fv[:], in_=fvec, w=['fvec'])
    fl = C.sb("flag", [128, 2]); S.dma(out=fl[:], in_=flag, w=['flag'])
    om = C.sb("onesm", [128, 128]); S.dma(out=om[:], in_=onesm, w=['onesm'])
    DVc = lambda n, k: dv[:, DV_NAMES.index(n) * 8 + k:DV_NAMES.index(n) * 8 + k + 1]
    FVc = lambda n, j: fv[:, FV_NAMES.index(n) * 44 + j:FV_NAMES.index(n) * 44 + j + 1]
    wg = [C.sb(f"wg{i}", [128, 8, 384], BF16) for i in range(2)]
    wp = [C.sb(f"wp{i}", [128, 18, 128], BF16) for i in range(2)]
    wo = [C.sb(f"wo{i}", [128, 8, 128], BF16) for i in range(2)]
    wu = [C.sb(f"wu{i}", [128, 8, 256], BF16) for i in range(2)]
    wd = [C.sb(f"wd{i}", [128, 22, 128], BF16) for i in range(2)]
    xs = C.sb("xs", [128, 8, DN]); hs = C.sb("hs", [128, 8, DN], BF16)
    ya = C.sb("ya", [128, 8, DN], BF16); yc = C.sb("yc", [128, 8, DN], BF16); yb = C.sb("yb", [128, 2, DN], BF16)
    yA = C.sb("yA", [128, 8, DN]); yB = C.sb("yB", [128, 8, DN])
    S.pool('memset', yA[:], 0.0, w=['yA'])
    S.pool('memset', yB[:], 0.0, w=['yB'])
    mg = C.sb("mg", [128, 8, DN], BF16)
    r1b = [C.sb("r1_0", [128, 8, DN]), C.sb("r1_1", [128, 8, DN])]; r2 = C.sb("r2", [128, 8, DN])
    sg_ = [C.sb(f"sg{i}", [128, DN]) for i in range(3)]
    acc = C.sb("acc", [128, DN]); tmp = C.sb("tmp", [128, DN]); tmpB = C.sb("tmpB", [128, DN])
    lnt = {st_: {'mu': C.sb(f"mu{st_}", [128, DN]), 'rstd': C.sb(f"rstd{st_}", [128, DN]),
                 'sq': [C.sb(f"sq{st_}0", [128, DN]), C.sb(f"sq{st_}1", [128, DN])]} for st_ in 'AB'}
    h2 = C.sb("h2", [128, 8, DN], BF16)
    ss = C.sb("ss", [128, 22, DN], BF16)
    upg = C.sb("upg", [128, DN + 2]); upv = C.sb("upv", [128, DN + 2])
    ucg = C.sb("ucg", [128, DN]); ucv = C.sb("ucv", [128, DN])
    tail = C.sb("tail", [128, 44, 2])
    S.pool('memset', tail[:], 0.0, w=['tail'])

    def layer_norm(Sx, pstf, st_, src, skey, wn, bn):
        T = lnt[st_]
        mu, rstd = T['mu'], T['rstd']
        mk_, rk_ = 'mu' + st_, 'rstd' + st_
        pt, pk = pstf()
        for k in range(8):
            Sx.pe('matmul', pt[:, 0:DN], om[:], src[:, k, :], start=(k == 0), stop=(k == 7), r=['onesm', skey], w=[pk])
        Sx.act('activation', out=mu[:], in_=pt[:, 0:DN], func=AF.Identity, r=[pk], w=[mk_])
        for k in range(8):
            Sx.dve('tensor_tensor', out=src[:, k, :], in0=src[:, k, :], in1=mu[:], op=ALU.subtract, r=[skey, mk_], w=[skey])
        pt, pk = pstf()
        for k in range(8):
            sqt, sqk = T['sq'][k % 2], f"sq{st_}{k % 2}"
            Sx.pool('tensor_tensor', out=sqt[:], in0=src[:, k, :], in1=src[:, k, :], op=ALU.mult, r=[skey], w=[sqk])
            Sx.pe('matmul', pt[:, 0:DN], om[:], sqt[:], start=(k == 0), stop=(k == 7), r=['onesm', sqk], w=[pk])
        Sx.act('activation', out=rstd[:], in_=pt[:, 0:DN], func=AF.Sqrt, bias=LN_EPS_AP[0], r=[pk, 'eps'], w=[rk_])
        Sx.dve('reciprocal', out=rstd[:], in_=rstd[:], r=[rk_], w=[rk_])
        for k in range(8):
            Sx.dve('tensor_tensor', out=src[:, k, :], in0=src[:, k, :], in1=rstd[:], op=ALU.mult, r=[skey, rk_], w=[skey])
            Sx.act('activation', out=src[:, k, :], in_=src[:, k, :], func=AF.Identity, scale=DVc(wn, k), bias=DVc(bn, k),
                   r=[skey, 'dvec'], w=[skey])

    epsT = C.sb("epsT", [128, 1])
    S.pool('memset', epsT[:], LN_EPS, w=['eps'])
    LN_EPS_AP = [epsT[:, 0:1]]
    C.nmain = 6
    SB = PrefSched(S, '', defer=True)
    S.side_every = 1

    def emit_A(tl):
        c0 = tl * DN
        r1, r1k = r1b[tl % 2], f"r1_{tl % 2}"
        for k in range(8):
            if layer == 0:
                S.dma(out=xs[:, k, :], in_=io['xh'][k * 128:(k + 1) * 128, c0:c0 + DN], w=['xs'])
            elif tl == 0:
                S.dma(out=xs[:, k, 0:2], in_=io[f'xg{k // 2}'][(k % 2) * 128:(k % 2) * 128 + 128, 2046:2048], w=['xs'])
                S.dma(out=xs[:, k, 2:DN], in_=io[f'xo{k // 2}'][(k % 2) * 128:(k % 2) * 128 + 128, 0:DN - 2], w=['xs'])
            else:
                S.dma(out=xs[:, k, :], in_=io[f'xo{k // 2}'][(k % 2) * 128:(k % 2) * 128 + 128, c0 - 2:c0 - 2 + DN], w=['xs'])
            S.act('activation', out=hs[:, k, :], in_=xs[:, k, :], func=AF.Identity,
                  scale=mod_sb[:, SC1 + k:SC1 + k + 1], bias=mod_sb[:, SH1 + k:SH1 + k + 1], r=['xs', 'mod'], w=['hs'])
        for (dst, dkey, j0, kk) in ((ya, 'ya', 0, 4), (yb, 'yb', 4, 1), (yc, 'yc', 5, 4)):
            for fi, (buf, bkey) in enumerate(((yA, 'yA'), (yB, 'yB'))):
                for r_ in range(2):
                    for k_ in range(kk):
                        src = io[f'yg{j0 + k_}'][r_ * 128:(r_ + 1) * 128, :]
                        d_ = buf[:, r_ * kk + k_, :]
                        if fi == 1:
                            S.dma(out=d_, in_=src[:, 2046 + c0:2046 + c0 + DN], w=[bkey])
                        elif tl == 0:
                            S.dma(out=d_[:, 2:DN], in_=src[:, 0:DN - 2], w=[bkey])
                        else:
                            S.dma(out=d_, in_=src[:, c0 - 2:c0 - 2 + DN], w=[bkey])
            S.dve('tensor_scalar', out=dst[:, 0:2 * kk, :], in0=yA[:, 0:2 * kk, :], scalar1=fl[:, 0:1], scalar2=None, op0=ALU.mult,
                  r=['yA', 'flag'], w=[dkey])
            S.dve('scalar_tensor_tensor', out=dst[:, 0:2 * kk, :], in0=yB[:, 0:2 * kk, :], scalar=fl[:, 1:2], in1=dst[:, 0:2 * kk, :],
                  op0=ALU.mult, op1=ALU.add, r=['yB', 'flag', dkey], w=[dkey])
        for m in range(8):
            wgb, wgk = wg[m % 2], f"wg{m % 2}"
            wpb, wpk = wp[m % 2], f"wp{m % 2}"
            S.dma(out=wgb[:], in_=WG[m].rearrange("p (k c) -> p k c", k=8), w=[wgk])
            S.dma(out=wpb[:], in_=WP[m].rearrange("p (k c) -> p k c", k=18), w=[wpk])
            for br in range(3):
                pt, pk = C.pst()
                for k in range(8):
                    S.pe('matmul', pt[:, 0:DN], wgb[:, k, br * 128:(br + 1) * 128], hs[:, k, :], start=(k == 0), stop=(k == 7),
                         r=[wgk, 'hs'], w=[pk])
                S.act('activation', out=sg_[br][:], in_=pt[:, 0:DN], func=AF.Sigmoid, r=[pk], w=[f"sg{br}"])
            for br, (src, skey, k0, nk) in enumerate(((ya, 'ya', 0, 8), (yb, 'yb', 8, 2), (yc, 'yc', 10, 8))):
                pt, pk = C.pst()
                for k in range(nk):
                    S.pe('matmul', pt[:, 0:DN], wpb[:, k0 + k, :], src[:, k, :], start=(k == 0), stop=(k == nk - 1),
                         r=[wpk, skey], w=[pk])
                if br == 0:
                    S.dve('tensor_tensor', out=acc[:], in0=pt[:, 0:DN], in1=sg_[0][:], op=ALU.mult, r=[pk, 'sg0'], w=['acc'])
                else:
                    S.dve('tensor_tensor', out=tmp[:], in0=pt[:, 0:DN], in1=sg_[br][:], op=ALU.mult, r=[pk, f"sg{br}"], w=['tmp'])
                    if br == 1:
                        S.pool('tensor_tensor', out=acc[:], in0=acc[:], in1=tmp[:], op=ALU.add, r=['acc', 'tmp'], w=['acc'])
                    else:
                        S.pool('tensor_tensor', out=mg[:, m, :], in0=acc[:], in1=tmp[:], op=ALU.add, r=['acc', 'tmp'], w=['mg'])
        for m in range(8):
            wob, wok = wo[m % 2], f"wo{m % 2}"
            S.dma(out=wob[:], in_=WO[m].rearrange("p (k c) -> p k c", k=8), w=[wok])
            pt, pk = C.pst()
            for k in range(8):
                S.pe('matmul', pt[:, 0:DN], wob[:, k, :], mg[:, k, :], start=(k == 0), stop=(k == 7), r=[wok, 'mg'], w=[pk])
            S.act('activation', out=tmp[:], in_=pt[:, 0:DN], func=AF.Identity, scale=mod_sb[:, GT1 + m:GT1 + m + 1],
                  r=[pk, 'mod'], w=['tmp'])
            S.dve('scalar_tensor_tensor', out=r1[:, m, :], in0=xs[:, m, :], scalar=float(ALPHA), in1=tmp[:], op0=ALU.mult, op1=ALU.add,
                  r=['xs', 'tmp'], w=[r1k])
        layer_norm(S, C.pst, 'A', r1, r1k, 'ln1w', 'ln1b')

    def emit_B(tl):
        c0 = tl * DN
        r1, r1k = r1b[tl % 2], f"r1_{tl % 2}"
        for k in range(8):
            SB.act('activation', out=h2[:, k, :], in_=r1[:, k, :], func=AF.Identity,
                  scale=mod_sb[:, SC2 + k:SC2 + k + 1], bias=mod_sb[:, SH2 + k:SH2 + k + 1], r=[r1k, 'mod'], w=['h2'])
        for j in range(22):
            wub, wuk = wu[j % 2], f"wu{j % 2}"
            SB.dma(out=wub[:], in_=WU[j].rearrange("p (k c) -> p k c", k=8), w=[wuk])
            for half, (up, ukey, uc, uckey, jj) in enumerate(((upg, 'upg', ucg, 'ucg', j), (upv, 'upv', ucv, 'ucv', 22 + j))):
                pt, pk = C.pst_side()
                for k in range(8):
                    SB.pe('matmul', pt[:, 0:DN], wub[:, k, half * 128:(half + 1) * 128], h2[:, k, :], start=(k == 0), stop=(k == 7),
                         r=[wuk, 'h2'], w=[pk])
                SB.act('activation', out=up[:, 2:DN + 2], in_=pt[:, 0:DN], func=AF.Identity, r=[pk], w=[ukey])
                SB.pool('tensor_copy', out=up[:, 0:2], in_=tail[:, jj, :], r=['tail'], w=[ukey])
                if tl == 0:
                    SB.dve('tensor_scalar', out=up[:, 2:4], in0=up[:, 2:4], scalar1=fl[:, 1:2], scalar2=None, op0=ALU.mult,
                          r=[ukey, 'flag'], w=[ukey])
                SB.pool('tensor_copy', out=tail[:, jj, :], in_=up[:, DN:DN + 2], r=[ukey], w=['tail'])
                SB.dve('tensor_scalar', out=uc[:], in0=up[:, 2:DN + 2], scalar1=FVc('fw0', jj), scalar2=FVc('fb', jj),
                      op0=ALU.mult, op1=ALU.add, r=[ukey, 'fvec'], w=[uckey])
                SB.dve('scalar_tensor_tensor', out=uc[:], in0=up[:, 1:DN + 1], scalar=FVc('fw1', jj), in1=uc[:], op0=ALU.mult, op1=ALU.add,
                      r=[ukey, 'fvec', uckey], w=[uckey])
                SB.dve('scalar_tensor_tensor', out=uc[:], in0=up[:, 0:DN], scalar=FVc('fw2', jj), in1=uc[:], op0=ALU.mult, op1=ALU.add,
                      r=[ukey, 'fvec', uckey], w=[uckey])
            SB.act('activation', out=ucg[:], in_=ucg[:], func=AF.Silu, r=['ucg'], w=['ucg'])
            SB.pool('tensor_tensor', out=ss[:, j, :], in0=ucg[:], in1=ucv[:], op=ALU.mult, r=['ucg', 'ucv'], w=['ss'])
        for m in range(8):
            wdb, wdk = wd[m % 2], f"wd{m % 2}"
            SB.dma(out=wdb[:], in_=WD[m].rearrange("p (k c) -> p k c", k=22), w=[wdk])
            pt, pk = C.pst_side()
            for k in range(22):
                SB.pe('matmul', pt[:, 0:DN], wdb[:, k, :], ss[:, k, :], start=(k == 0), stop=(k == 21), r=[wdk, 'ss'], w=[pk])
            SB.act('activation', out=tmpB[:], in_=pt[:, 0:DN], func=AF.Identity, scale=mod_sb[:, GT2 + m:GT2 + m + 1],
                  r=[pk, 'mod'], w=['tmpB'])
            SB.dve('scalar_tensor_tensor', out=r2[:, m, :], in0=r1[:, m, :], scalar=float(ALPHA), in1=tmpB[:], op0=ALU.mult, op1=ALU.add,
                  r=[r1k, 'tmpB'], w=['r2'])
        layer_norm(SB, C.pst_side, 'B', r2, 'r2', 'ln2w', 'ln2b')
        lo_c = 2 if tl == 0 else 0
        for k in range(8):
            if layer == DEPTH - 1 or _os.environ.get('KDEBUG_OUT0'):
                SB.dma(out=io['outT'][k * 128:(k + 1) * 128, c0 - 2 + lo_c:c0 - 2 + DN], in_=r2[:, k, lo_c:DN], r=['r2'])
            else:
                SB.dma(out=io[f'xo{k // 2}'][(k % 2) * 128:(k % 2) * 128 + 128, c0 - 2 + lo_c:c0 - 2 + DN], in_=r2[:, k, lo_c:DN], r=['r2'])

    nB_prev = 0
    for tl in range(NDT):
        if len(S.side) > nB_prev:
            S.replay_side(len(S.side) - nB_prev)
        emit_A(tl)
        n0 = len(S.side)
        emit_B(tl)
        nB_prev = len(S.side) - n0
    S.replay_side()
    C.finish()


def dense_weights(inp, l, perm_c=None):
    w_in = inp['w_in'][l]
    zg = w_in[:, 7680:10752]
    WG = np.stack([np.concatenate([zg[:, br * 1024 + m * 128: br * 1024 + (m + 1) * 128] for br in range(3)], axis=1) for m in range(8)])
    pc_ = inp['proj_c'][l] if perm_c is None else inp['proj_c'][l][perm_c]
    pall = np.concatenate([inp['proj_a'][l], inp['proj_b'][l], pc_], axis=0)
    WP = np.stack([pall[:, m * 128:(m + 1) * 128] for m in range(8)])
    fu = inp['ffn_up'][l]
    WU = np.stack([np.concatenate([fu[:, j * 128:(j + 1) * 128], fu[:, 2816 + j * 128:2816 + (j + 1) * 128]], axis=1) for j in range(22)])
    dvec = np.concatenate([arr128(inp[n][l]) for n in ('ln1_w', 'ln1_b', 'ln2_w', 'ln2_b')], axis=1)
    fcw = inp['ffn_conv_w'][l]
    fvec = np.concatenate([arr128(fcw[0]), arr128(fcw[1]), arr128(fcw[2]), arr128(inp['ffn_conv_b'][l])], axis=1)
    return {'WG': np.ascontiguousarray(WG), 'WP': np.ascontiguousarray(WP), 'WO': np.ascontiguousarray(inp['w_o'][l]),
            'WU': np.ascontiguousarray(WU), 'WD': np.ascontiguousarray(inp['ffn_down'][l]),
            'dvec': np.ascontiguousarray(dvec), 'fvec': np.ascontiguousarray(fvec),
            'onesm': np.full((128, 128), 1.0 / 1024, np.float32),
            'modw': np.ascontiguousarray(inp['mod_w'][l]), 'modb': arr128(inp['mod_b'][l])}


def halo_T(full_T, sh):
    F = full_T.shape[0]
    out = np.zeros((F, DTOK), np.float32)
    if sh == 0:
        out[:, 2:] = full_T[:, 0:2048]
    else:
        out[:, :] = full_T[:, 2046:4096]
    return out


def dense_inputs(dw, c_b, sh, xT_b, yaT_b, ybT_b, ycT_b):
    d = dict(dw)
    d['cT'] = arr128(c_b)
    d['xT'] = halo_T(xT_b, sh); d['yaT'] = halo_T(yaT_b, sh); d['ybT'] = halo_T(ybT_b, sh); d['ycT'] = halo_T(ycT_b, sh)
    d['flag'] = np.full((128, 1), float(sh), np.float32)
    return d


PAIRS = [[0, 1], [2, 3], [4, 5], [6, 7]]


def emit_modphase(nc, io, layer):
    C = Ctx(nc, f"md{layer}"); S = C.S
    C.init_psum()
    cT_sb = C.sb("cT", [128, 8]); S.dma(out=cT_sb[:], in_=io['cT'], w=['cT'])
    modb_sb = C.sb("modb", [128, 48]); S.dma(out=modb_sb[:], in_=io[f'modb_{layer}'], w=['modb'])
    mod_sb = C.sb("mod", [128, 48])
    emit_mod(C, io[f'modw_{layer}'], modb_sb, cT_sb, 6144, mod_sb)
    for lo_ in (8, 16, 32, 40):
        S.dve('tensor_scalar', out=mod_sb[:, lo_:lo_ + 8], in0=mod_sb[:, lo_:lo_ + 8], scalar1=1.0, scalar2=None, op0=ALU.add,
              r=['mod'], w=['mod'])
    S.dma(out=io['modv'][layer], in_=mod_sb[:], r=['mod'])
    C.finish()


import os as _os


def emit_allgather(nc, src_h, dst_h, tag):
    if _os.environ.get("KDEBUG_NOCC"):
        return
    cc = nc.alloc_semaphore(name=f"cc_{tag}")
    with nc.Block() as block:
        @block.gpsimd
        def _(g):
            g.collective_compute("AllGather", ALU.bypass, replica_groups=PAIRS,
                                 ins=[src_h.ap().opt()], outs=[dst_h.ap().opt()]).then_inc(cc)
            g.wait_ge(cc, 1)
    nc.clear_and_free_semaphores([cc])
    nc.all_engine_barrier()


PER_LAYER = [('modw', [1024, 6144]), ('modb', [128, 48]), ('wa', [1024, 1024]), ('vecl', [128, 32]), ('gw', [128, 8, 128]),
             ('wqk', [1024, 768]), ('wv', [1024, 384]), ('vec', [128, NVEC]), ('w2c', [64, 512]), ('a2c', [64, 512]),
             ('g2c', [128, 512]), ('WG', [8, 1024, 384]), ('WP', [8, 2304, 128]), ('WO', [1024, 1024]),
             ('WU', [22, 1024, 256]), ('WD', [2816, 1024]), ('dvec', [128, 32]), ('fvec', [128, 176])]


def build_fused():
    nc = bass.Bass("TRN2", target_bir_lowering=False)
    io = {}

    def din(name, shape, dt=F32):
        io[name] = nc.dram_tensor(name, list(shape), dt, kind="ExternalInput").ap()
    din('xT', [1024, SEQ]); din('xh', [1024, DTOK]); din('cT', [128, 8]); din('pos', [1, SEQ], I32)
    din('freq', [128, 1]); din('strips', [128, NSTRIP * 128]); din('cst', [128, NCST]); din('flag', [128, 2])
    din('onesm', [128, 128]); din('v2c_1', [32, 512])
    for l in range(DEPTH):
        for (n, shp) in PER_LAYER:
            din(f'{n}_{l}', shp)
        din(f'wc_{l}', [1024, 1792 + (32 if l else 0)])
    io['outT'] = nc.dram_tensor('outT', [1024, 2048], F32, kind="ExternalOutput").ap()
    H = {}

    def dint(n, shp):
        H[n] = nc.dram_tensor(n, shp, F32)
        io[n] = H[n].ap()
    dint('modv', [DEPTH, 128, 48]); dint('vfs', [512, SEQ])
    for j in range(9):
        dint(f'ys{j}', [128, SEQ]); dint(f'yg{j}', [256, SEQ])
    for j in range(4):
        dint(f'xo{j}', [256, 2048]); dint(f'xg{j}', [512, 2048])
    for l in range(DEPTH):
        for (n, shp) in (('WG', [8, 128, 8 * 384]), ('WP', [8, 128, 18 * 128]), ('WO', [8, 128, 8 * 128]),
                         ('WU', [22, 128, 8 * 256]), ('WD', [8, 128, 22 * 128])):
            io[f'{n}b_{l}'] = nc.dram_tensor(f'{n}b_{l}', shp, BF16).ap()
    ph = _os.environ.get("KDEBUG_PH", "md,lr,at,rw,ag,dn").split(",")
    nl = int(_os.environ.get("KDEBUG_L", str(DEPTH)))
    for l in range(DEPTH):
        if 'md' in ph:
            emit_modphase(nc, io, l)
    for l in range(nl):
        if 'lr' in ph and not MERGE_LRU:
            emit_lru(nc, io, l)
        if 'at' in ph:
            emit_attn(nc, io, l)
        if 'rw' in ph:
            emit_rwkv(nc, io, l)
        if 'ag' in ph:
            for j in range(9):
                emit_allgather(nc, H[f'ys{j}'], H[f'yg{j}'], f"y{l}_{j}")
        if 'dn' in ph:
            emit_dense(nc, io, l)
        if l < DEPTH - 1 and 'ag' in ph:
            for j in range(4):
                emit_allgather(nc, H[f'xo{j}'], H[f'xg{j}'], f"x{l}_{j}")
    if _os.environ.get("KDEBUG_TAP"):
        tap = nc.dram_tensor('tap', [1152, SEQ], F32, kind="ExternalOutput").ap()
        C = Ctx(nc, "tap")
        for j in range(9):
            C.S.dma(out=tap[j * 128:(j + 1) * 128, :], in_=io[f'ys{j}'][:, :])
        C.finish()
    return nc


def core_inputs(inp, b, r):
    xT_b = np.ascontiguousarray(inp['x'][b].astype(np.float32).T)
    d = {'xT': xT_b, 'xh': halo_T(xT_b, r), 'cT': arr128(inp['c'][b]),
         'flag': np.tile(np.array([[1.0 - r, float(r)]], np.float32), (128, 1)),
         'onesm': np.full((128, 128), 1.0 / 1024, np.float32)}
    perm_c = np.concatenate([pair_rows(0), pair_rows(1)])
    for l in range(DEPTH):
        li = lru_inputs(inp, l, b, r, xT_b)
        ai = attn_inputs(inp, l, b, r, xT_b)
        ri = rwkv_inputs(inp, l, b, r, xT_b)
        dw = dense_weights(inp, l, perm_c)
        d[f'modw_{l}'] = np.ascontiguousarray(inp['mod_w'][l]); d[f'modb_{l}'] = arr128(inp['mod_b'][l])
        for n in ('wa', 'vecl', 'gw'):
            d[f'{n}_{l}'] = li[n]
        for n in ('wqk', 'wv'):
            d[f'{n}_{l}'] = ai[n]
        for n in ('wc', 'vec', 'w2c', 'a2c', 'g2c'):
            d[f'{n}_{l}'] = ri[n]
        if l:
            d['v2c_1'] = ri['v2c']
        for n in ('WG', 'WP', 'WO', 'WU', 'WD', 'dvec', 'fvec'):
            d[f'{n}_{l}'] = dw[n]
        if l == 0:
            d['pos'] = ai['pos']; d['freq'] = ai['freq']; d['strips'] = ai['strips']; d['cst'] = ri['cst']
    return d


_NC = []


def kernel(**inputs):
    inp = {k: np.asarray(v) for k, v in inputs.items()}
    if not _NC:
        _NC.append(build_fused())
    cores = [(b, r) for b in range(BATCH) for r in range(2)]
    maps = [core_inputs(inp, b, r) for (b, r) in cores]
    res = run_bass_kernel_spmd(_NC[0], maps, core_ids=list(range(8))).results
    out = np.zeros((BATCH, SEQ, D_MODEL), np.float32)
    for (b, r), rr in zip(cores, res):
        out[b, r * 2048:(r + 1) * 2048, :] = rr['outT'].T
    return out
```

```python
import contextlib
import numpy as np
import concourse.bass as bass
import concourse.mybir as mybir
from concourse.bass_utils import run_bass_kernel_spmd

F32 = mybir.dt.float32
BF16 = mybir.dt.bfloat16
I32 = mybir.dt.int32
AF = mybir.ActivationFunctionType
ALU = mybir.AluOpType
AX = mybir.AxisListType

D_MODEL = 1024
SEQ = 4096
BATCH = 4
DEPTH = 2
D_FF = 2816
ALPHA = (2 * DEPTH) ** 0.25
LN_EPS = 1e-5
GN_EPS = 64e-5


class Sched:
    ENGS = ('pe', 'act', 'dve', 'pool', 'sp')
    EP = 30000
    R = 8

    def __init__(self, nc, tag=""):
        self.nc = nc
        self.tag = tag
        self.ops = {e: [] for e in self.ENGS}
        self.ncomp = {e: 0 for e in self.ENGS}
        self.ndma = {e: 0 for e in self.ENGS}
        self.last_w = {}
        self.readers = {}
        self.side = []
        self.side_every = 0
        self._cnt = 0
        self._in_side = False

    def add(self, eng, fn, reads=(), writes=(), dma=False):
        me = self._add(eng, fn, reads, writes, dma)
        if self.side and not self._in_side and self.side_every:
            self._cnt += 1
            if self._cnt % self.side_every == 0:
                self.replay_side(1)
        return me

    def replay_side(self, n=None):
        self._in_side = True
        k = 0
        while self.side and (n is None or k < n):
            self._add(*self.side.pop(0))
            k += 1
        self._in_side = False

    def _add(self, eng, fn, reads=(), writes=(), dma=False):
        deps = set()
        for k in reads:
            w = self.last_w.get(k)
            if w is not None:
                deps.add(w)
            if k.startswith('ps'):
                for r in self.readers.get(k, ()):
                    if r[1] != eng:
                        deps.add(r)
        for k in writes:
            w = self.last_w.get(k)
            if w is not None:
                deps.add(w)
            for r in self.readers.get(k, ()):
                deps.add(r)
        if dma:
            me = ('d', eng, self.ndma[eng])
            self.ndma[eng] += 1
        else:
            me = ('c', eng, self.ncomp[eng])
            self.ncomp[eng] += 1
        deps.discard(me)
        self.ops[eng].append((me, fn, deps))
        for k in reads:
            self.readers.setdefault(k, []).append(me)
        for k in writes:
            self.last_w[k] = me
            self.readers[k] = []
        return me

    def op(self, eng, name, *args, r=(), w=(), **kw):
        def fn(e, name=name, args=args, kw=kw):
            return getattr(e, name)(*args, **kw)
        return self.add(eng, fn, r, w)

    def pe(self, name, *args, r=(), w=(), **kw): return self.op('pe', name, *args, r=r, w=w, **kw)
    def act(self, name, *args, r=(), w=(), **kw): return self.op('act', name, *args, r=r, w=w, **kw)
    def dve(self, name, *args, r=(), w=(), **kw): return self.op('dve', name, *args, r=r, w=w, **kw)
    def pool(self, name, *args, r=(), w=(), **kw): return self.op('pool', name, *args, r=r, w=w, **kw)

    def dma(self, out, in_, r=(), w=(), q='sp', **kw):
        def fn(e, out=out, in_=in_, kw=kw):
            return e.dma_start(out=out, in_=in_, **kw)
        return self.add(q, fn, r, w, dma=True)

    def emit(self):
        nc = self.nc
        sems = []

        def new_sem(name):
            h = nc.alloc_semaphore(name=name)
            sems.append(h)
            return h
        with contextlib.ExitStack() as st:
            csem = {}
            for e in self.ENGS:
                nep = (self.ncomp[e] + self.EP - 1) // self.EP
                csem[e] = [new_sem(f"c_{self.tag}_{e}_{i}") for i in range(nep)]
            dsem = {}
            for e in self.ENGS:
                if self.ndma[e]:
                    dsem[e] = [new_sem(f"d_{self.tag}_{e}_{i}") for i in range(min(self.R, self.ndma[e]))]
            block = st.enter_context(nc.Block())
            engobj = {'pe': block.tensor, 'act': block.scalar, 'dve': block.vector,
                      'pool': block.gpsimd, 'sp': block.sync}
            ops, ndma, EP, R = self.ops, self.ndma, self.EP, self.R

            def make(e):
                def body(eng):
                    waited_c = {}
                    waited_d = {}

                    def wait_dep(d):
                        kind, D, n = d
                        if kind == 'c':
                            if D == e and e == 'pe':
                                return
                            if waited_c.get(D, -1) >= n:
                                return
                            waited_c[D] = n
                            eng.wait_ge(csem[D][n // EP], (n % EP) + 1)
                        else:
                            slot = n % R
                            val = 16 * (n // R + 1)
                            if waited_d.get((D, slot), 0) >= val:
                                return
                            waited_d[(D, slot)] = val
                            eng.wait_ge(dsem[D][slot], val)

                    for (me, fn, deps) in ops[e]:
                        red = {}
                        for d in deps:
                            kk_ = (d[0], d[1]) if d[0] == 'c' else (d[0], d[1], d[2] % R)
                            if kk_ not in red or red[kk_][2] < d[2]:
                                red[kk_] = d
                        for d in sorted(red.values()):
                            wait_dep(d)
                        kind, _, n = me
                        if kind == 'd':
                            if n >= R:
                                wait_dep(('d', e, n - R))
                            fn(eng).then_inc(dsem[e][n % R], 16)
                        else:
                            fn(eng).then_inc(csem[e][n // EP], 1)
                    for j in range(max(0, ndma[e] - R), ndma[e]):
                        wait_dep(('d', e, j))
                return body
            for e in self.ENGS:
                if ops[e]:
                    engobj[e](make(e))
        nc.clear_and_free_semaphores(sems)
        nc.all_engine_barrier()


class Ctx:
    def __init__(self, nc, tag):
        self.nc = nc
        self.tag = tag
        self.st = contextlib.ExitStack()
        self.S = Sched(self.nc, tag)
        self.nps = 0
        self.nps_side = 0
        self.ps = None

    def din(self, name, shape, dt=F32):
        return self.nc.dram_tensor(name, list(shape), dt, kind="ExternalInput").ap()

    def dout(self, name, shape, dt=F32):
        return self.nc.dram_tensor(name, list(shape), dt, kind="ExternalOutput").ap()

    def sb(self, name, shape, dt=F32):
        return self.st.enter_context(self.nc.sbuf_tensor(f"sb_{self.tag}_{name}", list(shape), dt))

    def init_psum(self):
        self.ps = [self.st.enter_context(self.nc.psum_tensor(f"ps_{self.tag}_{i}", [128, 512], F32)) for i in range(8)]

    nmain = 8

    def pst(self):
        i = self.nps % self.nmain
        self.nps += 1
        return self.ps[i], f"ps{i}"

    def pst_side(self):
        n = 8 - self.nmain
        i = self.nmain + (self.nps_side % n)
        self.nps_side += 1
        return self.ps[i], f"ps{i}"

    def finish(self):
        self.S.emit()
        self.st.close()


class PrefSched:
    def __init__(self, S, pfx, defer=False):
        self.S, self.pfx, self.defer = S, pfx, defer

    def _k(self, keys):
        return [k if k.startswith('ps') else self.pfx + k for k in keys]

    def op(self, eng, name, *args, r=(), w=(), **kw):
        if self.defer:
            def fn(e, name=name, args=args, kw=kw):
                return getattr(e, name)(*args, **kw)
            self.S.side.append((eng, fn, self._k(r), self._k(w), False))
            return None
        return self.S.op(eng, name, *args, r=self._k(r), w=self._k(w), **kw)

    def pe(self, name, *args, r=(), w=(), **kw): return self.op('pe', name, *args, r=r, w=w, **kw)
    def act(self, name, *args, r=(), w=(), **kw): return self.op('act', name, *args, r=r, w=w, **kw)
    def dve(self, name, *args, r=(), w=(), **kw): return self.op('dve', name, *args, r=r, w=w, **kw)
    def pool(self, name, *args, r=(), w=(), **kw): return self.op('pool', name, *args, r=r, w=w, **kw)

    def dma(self, out, in_, r=(), w=(), q='sp', **kw):
        if self.defer:
            def fn(e, out=out, in_=in_, kw=kw):
                return e.dma_start(out=out, in_=in_, **kw)
            self.S.side.append((q, fn, self._k(r), self._k(w), True))
            return None
        return self.S.dma(out, in_, r=self._k(r), w=self._k(w), q=q, **kw)


class SubCtx:
    def __init__(self, C, pfx, defer=False):
        self.C, self.pfx = C, pfx
        self.nc = C.nc
        self.S = PrefSched(C.S, pfx, defer)

    def sb(self, name, shape, dt=F32):
        return self.C.sb(self.pfx + name, shape, dt)

    def pst(self):
        return self.C.pst_side() if self.S.defer else self.C.pst()


def emit_mod(C, modw, modb_sb, cT_sb, ncols, out_sb):
    S = C.S
    sc = C.sb("silu_c", [128, 8])
    wbuf = [C.sb(f"modw{i}", [128, 8, 512]) for i in range(3)]
    S.act('activation', out=sc[:], in_=cT_sb[:], func=AF.Silu, r=['cT'], w=['silu_c'])
    nj = ncols // 128
    pt, pk = C.pst()
    wv = modw.rearrange("(k p) c -> p k c", p=128)
    for blk in range(ncols // 512):
        wt = wbuf[blk % 3]
        wkey = f"modw{blk % 3}"
        S.dma(out=wt[:], in_=wv[:, :, blk * 512:(blk + 1) * 512], w=[wkey], q=('sp' if blk % 2 == 0 else 'act'))
        for jj in range(4):
            j = blk * 4 + jj
            for k in range(8):
                S.pe('matmul',
                    pt[:, j:j + 1], wt[:, k, jj * 128:(jj + 1) * 128], sc[:, k:k + 1],
                    start=(k == 0), stop=(k == 7), r=[wkey, 'silu_c'], w=[pk])
    S.dve('tensor_tensor', out=out_sb[:], in0=pt[:, 0:nj], in1=modb_sb[:], op=ALU.add,
          r=[pk, 'modb'], w=['mod'])


MERGE_LRU = True
SIDE_EVERY = 4
TS = 256
NSEG = SEQ // TS
NCH = TS // 64
VEC_NAMES = ['mu_r', 'mu_k', 'mu_v', 'w0', 'a0', 'kk', 'ka', 'rk', 'lnw', 'lnb', 'v0']


def vcol(name, p):
    return VEC_NAMES.index(name) * 4 + p


V_MUZW, V_MUZA, V_MUZG, V_MUZV1 = 44, 45, 46, 47
NVEC = 48
C_ID, C_MTS, C_MST, C_MSTI, C_RST, C_EYE, C_BONES = 0, 64, 64 + TS, 64 + 2 * TS, 64 + 3 * TS, 64 + 4 * TS, 64 + 5 * TS
C_HM = C_BONES + 128
C_H = C_HM + 2
C_OH = C_H + 6 * TS
NCST = C_OH + 2 * TS


def rwkv_consts():
    c = np.zeros((128, NCST), np.float32)
    pp = np.arange(128)[:, None] % 64
    f = np.arange(TS)[None, :] % 64
    c[:, C_ID:C_ID + 64] = (pp == np.arange(64)[None, :])
    c[:, C_MTS:C_MTS + TS] = (pp > f)
    c[:, C_MST:C_MST + TS] = (f > pp)
    c[:, C_MSTI:C_MSTI + TS] = (f >= pp)
    c[:, C_RST:C_RST + TS] = (f != 0)
    c[:, C_EYE:C_EYE + TS] = (f == pp)
    c[:, C_BONES:C_BONES + 128] = (np.arange(128)[:, None] // 64 == np.arange(128)[None, :] // 64)
    hm = np.stack([(np.arange(128) < 64), (np.arange(128) >= 64)], axis=1).astype(np.float32)
    c[:, C_HM:C_HM + 2] = hm
    for bi, base in enumerate((C_EYE, C_MTS, C_MST)):
        for e in range(2):
            c[:, C_H + (bi * 2 + e) * TS:C_H + (bi * 2 + e + 1) * TS] = c[:, base:base + TS] * hm[:, e:e + 1]
    for e in range(2):
        c[:, C_OH + e * TS:C_OH + (e + 1) * TS] = hm[:, e:e + 1]
    return c


def x_pieces(io, l, k, ta, tb):
    if l == 0:
        return [(0, tb - ta, io['xT'][k * 128:(k + 1) * 128, ta:tb])]
    out = []
    for r in range(2):
        a_, b_ = max(ta, r * 2048), min(tb, (r + 1) * 2048)
        if a_ < b_:
            rows = r * 256 + (k % 2) * 128
            out.append((a_ - ta, b_ - a_, io[f'xg{k // 2}'][rows:rows + 128, a_ - r * 2048:b_ - r * 2048]))
    return out


def load_mod(C, io, layer):
    mod_sb = C.sb("mod", [128, 48])
    C.S.dma(out=mod_sb[:], in_=io['modv'][layer], w=['mod'])
    return mod_sb


def emit_rwkv(nc, io, layer):
    C = Ctx(nc, f"rw{layer}")
    S = C.S
    NCW = 1792 + (32 if layer else 0)
    wc = io[f'wc_{layer}']; vec = io[f'vec_{layer}']
    w2c = io[f'w2c_{layer}']; a2c = io[f'a2c_{layer}']; g2c = io[f'g2c_{layer}']
    cst = io['cst']
    vfs = io['vfs']
    if layer:
        v2c = io['v2c_1']
    C.init_psum()

    def ld(name, ap, shape, q='sp'):
        t = C.sb(name, shape)
        S.dma(out=t[:], in_=ap, w=[name], q=q)
        return t
    vec_sb = ld("vec", vec, [128, NVEC])
    w2_sb = ld("w2c", w2c, [64, 512])
    a2_sb = ld("a2c", a2c, [64, 512])
    g2_sb = ld("g2c", g2c, [128, 512])
    cst_sb = ld("cst", cst, [128, NCST])
    if layer:
        v2_sb = ld("v2c", v2c, [32, 512])
    wcb = C.sb("wcb", [128, 8, NCW], BF16)
    wcv = wc.rearrange("(k p) c -> p k c", p=128)
    c0 = 0
    while c0 < NCW:
        n = min(512, NCW - c0)
        S.dma(out=wcb[:, :, c0:c0 + n], in_=wcv[:, :, c0:c0 + n], w=['wcb'], q='pool')
        c0 += n
    mod_sb = load_mod(C, io, layer)
    omv = C.sb("omv", [128, NVEC])
    S.dve('tensor_scalar', out=omv[:], in0=vec_sb[:], scalar1=-1.0, scalar2=1.0, op0=ALU.mult, op1=ALU.add,
          r=['vec'], w=['omv'])

    def V(name, p, rows=128):
        c = vcol(name, p)
        return vec_sb[0:rows, c:c + 1]

    ident = cst_sb[:, C_ID:C_ID + 64]
    m_ts = cst_sb[:, C_MTS:C_MTS + TS]
    m_st = cst_sb[:, C_MST:C_MST + TS]
    m_sti = cst_sb[:, C_MSTI:C_MSTI + TS]
    rst = cst_sb[:, C_RST:C_RST + TS]
    eye = cst_sb[:, C_EYE:C_EYE + TS]
    bones = cst_sb[:, C_BONES:C_BONES + 128]

    hs = C.sb("hs", [128, 8, TS], BF16)
    xk = [C.sb("xk0", [128, TS]), C.sb("xk1", [128, TS])]
    zwp = C.sb("zwp", [64, TS + 1]); tzw = C.sb("tzw", [64, TS])
    zap = C.sb("zap", [64, TS + 1])
    zgp = C.sb("zgp", [128, TS + 1]); szg = C.sb("szg", [128, TS])
    if layer:
        zvp = C.sb("zvp", [32, TS + 1])
    tmpsh = C.sb("tmpsh", [128, TS])
    zr = C.sb("zr", [128, TS + 1]); zk = C.sb("zk", [128, TS + 1]); zv = C.sb("zv", [128, TS + 1])
    Wt = {n: C.sb("w_" + n, [128, TS]) for n in
          ['ld', 'cl', 'a', 'kk', 'sq', 'kkn', 'km', 'epos', 'eneg', 'eprev', 'Bt', 'Kt', 'rkr', 'lam', 'vf']}
    PP = [{n: C.sb(f"{n}{p}", [128, TS]) for n in ['At', 'Rt', 'AKT', 'RBT', 'RKT', 'TT', 'bon', 'gT']}
          for p in range(4)]
    TOK = [{n: C.sb(f"{n}{p}", [128, NCH, 64]) for n in ['Btok', 'Ktok', 'Vtok']} for p in range(4)]
    gam = C.sb("gam", [128, 4, NCH])
    Nm = C.sb("Nm", [128, TS]); NT = C.sb("NT", [128, TS])
    Pw = [C.sb("Pw0", [128, TS]), C.sb("Pw1", [128, TS])]
    PTw = [C.sb("PTw0", [128, TS]), C.sb("PTw1", [128, TS])]
    TTw = [C.sb("TTw0", [128, TS]), C.sb("TTw1", [128, TS])]
    BD = {n: C.sb("bd_" + n, [128, NCH, 128]) for n in ['At', 'Bt', 'Kt', 'V', 'P0', 'P1', 'PT0', 'PT1', 'IP']}
    Hb = [C.sb("Hb0", [128, 256]), C.sb("Hb1", [128, 256])]
    Xsb = C.sb("Xsb", [128, 256]); Usb = C.sb("Usb", [128, 256]); Htmp = C.sb("Htmp", [128, 256])
    Ysb = C.sb("Ysb", [128, NCH, 4, 64]); dY = C.sb("dY", [128, NCH, 4, 64]); sqY = C.sb("sqY", [128, NCH, 4, 64])
    st1 = C.sb("st1", [128, NCH * 4]); st2 = C.sb("st2", [128, NCH * 4])
    ycs = [C.sb(f"ycs{p}", [128, TS]) for p in range(4)]
    S.pool('memset', Hb[0][:], 0.0, w=['Hb0'])

    zl = C.sb("zl", [128, 16])
    S.pool('memset', zl[:], 0.0, w=['zl'])

    def proj(cols0, M, dst, key, zid):
        pt, pk = C.pst()
        for k in range(8):
            S.pe('matmul', pt[0:M, 0:TS], wcb[:, k, cols0:cols0 + M], hs[:, k, :], start=(k == 0), stop=(k == 7),
                 r=['wcb', 'hs'], w=[pk])
        S.act('activation', out=dst[0:M, 1:TS + 1], in_=pt[0:M, 0:TS], func=AF.Identity, r=[pk], w=[key])
        S.pool('tensor_copy', out=dst[0:M, 0:1], in_=zl[0:M, zid:zid + 1], r=['zl'], w=[key])
        S.pool('tensor_copy', out=zl[0:M, zid:zid + 1], in_=dst[0:M, TS:TS + 1], r=[key], w=['zl'])

    def shift(z, key, M, mucol):
        S.dve('tensor_scalar', out=tmpsh[0:M, :], in0=z[0:M, 0:TS], scalar1=vec_sb[0:M, mucol:mucol + 1],
                                        scalar2=None, op0=ALU.mult, r=[key, 'vec'], w=['tmpsh'])
        S.dve('scalar_tensor_tensor', out=z[0:M, 1:TS + 1], in0=z[0:M, 1:TS + 1],
                                               scalar=omv[0:M, mucol:mucol + 1], in1=tmpsh[0:M, :],
                                               op0=ALU.mult, op1=ALU.add, r=[key, 'omv', 'tmpsh'], w=[key])

    hm = [cst_sb[:, C_HM:C_HM + 1], cst_sb[:, C_HM + 1:C_HM + 2]]
    cH = lambda bi, e_: cst_sb[:, C_H + (bi * 2 + e_) * TS:C_H + (bi * 2 + e_ + 1) * TS]
    v3 = lambda ap: ap.rearrange("q (c t) -> q c t", t=64)

    def product_bd(Lbd, Lk, R, Rk, roff=0):
        pt, pk = C.pst()
        for c in range(NCH):
            S.pe('matmul', pt[:, c * 64:(c + 1) * 64], Lbd[:, c, :], R[:, roff + c * 64:roff + (c + 1) * 64],
                 start=True, stop=True, r=[Lk, Rk], w=[pk])
        return pt, pk

    def to_bd(eng, dst, dkey, src, skeys, kind='copy'):
        for e_ in range(2):
            o = dst[:, :, e_ * 64:(e_ + 1) * 64]
            if kind == 'copy' and eng == 'pool':
                S.pool('tensor_tensor', out=o, in0=v3(src), in1=v3(cst_sb[:, C_OH + e_ * TS:C_OH + (e_ + 1) * TS]), op=ALU.mult,
                       r=list(skeys) + ['cst'], w=[dkey])
            elif kind == 'copy':
                if eng == 'act':
                    S.act('activation', out=o, in_=v3(src), func=AF.Identity, scale=hm[e_], r=list(skeys) + ['cst'], w=[dkey])
                else:
                    S.op(eng, 'tensor_scalar', out=o, in0=v3(src), scalar1=hm[e_], scalar2=None, op0=ALU.mult,
                         r=list(skeys) + ['cst'], w=[dkey])
            elif kind == 'eye':
                S.dve('scalar_tensor_tensor', out=o, in0=v3(src), scalar=hm[e_], in1=v3(cH(0, e_)), op0=ALU.mult, op1=ALU.add,
                      r=list(skeys) + ['cst'], w=[dkey])
            else:
                S.dve('tensor_tensor', out=o, in0=v3(src), in1=v3(cH(kind, e_)), op=ALU.mult, r=list(skeys) + ['cst'], w=[dkey])

    if MERGE_LRU:
        C.nmain = 6
        for _ in lru_body(SubCtx(C, 'L_', defer=True), io, layer):
            pass
        S.side_every = SIDE_EVERY
    lg = iter(())
    cur = 0
    for sg in range(NSEG):
        t0 = sg * TS
        for k in range(8):
            xb = xk[k % 2]
            xkey = f"xk{k % 2}"
            for (o_, n_, src_) in x_pieces(io, layer, k, t0, t0 + TS):
                S.dma(out=xb[:, o_:o_ + n_], in_=src_, w=[xkey], allow_slow_non_contiguous=(n_ < 2))
            S.act('activation', out=hs[:, k, :], in_=xb[:], func=AF.Identity,
                  scale=mod_sb[:, 8 + k:9 + k], bias=mod_sb[:, k:k + 1], r=[xkey, 'mod'], w=['hs'])
        proj(1536, 64, zwp, 'zwp', 0); shift(zwp, 'zwp', 64, V_MUZW)
        S.act('activation', out=tzw[:], in_=zwp[:, 1:TS + 1], func=AF.Tanh, r=['zwp'], w=['tzw'])
        proj(1600, 64, zap, 'zap', 1); shift(zap, 'zap', 64, V_MUZA)
        proj(1664, 128, zgp, 'zgp', 2); shift(zgp, 'zgp', 128, V_MUZG)
        S.act('activation', out=szg[:], in_=zgp[:, 1:TS + 1], func=AF.Sigmoid, r=['zgp'], w=['szg'])
        if layer:
            proj(1792, 32, zvp, 'zvp', 3); shift(zvp, 'zvp', 32, V_MUZV1)
        for p in range(4):
            P_ = PP[p]
            K_ = lambda n, p=p: f"{n}{p}"
            proj(p * 128, 128, zr, 'zr', 4 + p); shift(zr, 'zr', 128, vcol('mu_r', p))
            proj(512 + p * 128, 128, zk, 'zk', 8 + p); shift(zk, 'zk', 128, vcol('mu_k', p))
            proj(1024 + p * 128, 128, zv, 'zv', 12 + p); shift(zv, 'zv', 128, vcol('mu_v', p))
            rs, ks, vs = zr[:, 1:TS + 1], zk[:, 1:TS + 1], zv[:, 1:TS + 1]
            pc = slice(p * 128, (p + 1) * 128)
            if layer == 0:
                S.dma(out=vfs[p * 128:(p + 1) * 128, t0:t0 + TS], in_=zv[:, 1:TS + 1], r=['zv'])
            else:
                pt, pk = C.pst()
                S.pe('matmul', pt[:, 0:TS], v2_sb[:, pc], zvp[:, 1:TS + 1], start=True, stop=True,
                     r=['v2c', 'zvp'], w=[pk])
                S.act('activation', out=Wt['lam'][:], in_=pt[:, 0:TS], func=AF.Sigmoid, bias=V('v0', p),
                      r=[pk, 'vec'], w=['w_lam'])
                S.dma(out=Wt['vf'][:], in_=vfs[p * 128:(p + 1) * 128, t0:t0 + TS], w=['w_vf'])
                S.dve('tensor_tensor', out=Wt['vf'][:], in0=Wt['vf'][:], in1=zv[:, 1:TS + 1], op=ALU.subtract,
                      r=['w_vf', 'zv'], w=['w_vf'])
                S.dve('tensor_tensor', out=Wt['vf'][:], in0=Wt['vf'][:], in1=Wt['lam'][:], op=ALU.mult,
                      r=['w_vf', 'w_lam'], w=['w_vf'])
                S.dve('tensor_tensor', out=zv[:, 1:TS + 1], in0=zv[:, 1:TS + 1], in1=Wt['vf'][:], op=ALU.add,
                      r=['w_vf', 'zv'], w=['zv'])
            pt, pk = C.pst()
            S.pe('matmul', pt[:, 0:TS], w2_sb[:, pc], tzw[:], start=True, stop=True,
                 r=['w2c', 'tzw'], w=[pk])
            S.act('activation', out=Wt['ld'][:], in_=pt[:, 0:TS], func=AF.Sigmoid, bias=V('w0', p),
                  r=[pk, 'vec'], w=['w_ld'])
            S.dve('tensor_scalar', out=Wt['ld'][:], in0=Wt['ld'][:], scalar1=-float(np.exp(-0.5)), scalar2=None,
                                            op0=ALU.mult, r=['w_ld'], w=['w_ld'])
            S.dve('tensor_tensor_scan', out=Wt['cl'][:], data0=rst, data1=Wt['ld'][:], initial=0.0,
                                                 op0=ALU.mult, op1=ALU.add, r=['w_ld', 'cst'], w=['w_cl'])
            pt, pk = C.pst()
            S.pe('matmul', pt[:, 0:TS], a2_sb[:, pc], zap[:, 1:TS + 1], start=True, stop=True,
                 r=['a2c', 'zap'], w=[pk])
            S.act('activation', out=Wt['a'][:], in_=pt[:, 0:TS], func=AF.Sigmoid, bias=V('a0', p),
                  r=[pk, 'vec'], w=['w_a'])
            pt, pk = C.pst()
            S.pe('matmul', pt[:, 0:TS], g2_sb[:, pc], szg[:], start=True, stop=True,
                 r=['g2c', 'szg'], w=[pk])
            S.act('activation', out=P_['gT'][:], in_=pt[:, 0:TS], func=AF.Identity, r=[pk], w=[K_('gT')])
            S.dve('tensor_scalar', out=Wt['kk'][:], in0=zk[:, 1:TS + 1], scalar1=V('kk', p), scalar2=None,
                                                 op0=ALU.mult, r=['zk', 'vec'], w=['w_kk'])
            S.pool('tensor_tensor', out=Wt['sq'][:], in0=Wt['kk'][:], in1=Wt['kk'][:], op=ALU.mult,
                   r=['w_kk'], w=['w_sq'])
            pt, pk = C.pst()
            S.pe('matmul', pt[:, 0:TS], bones, Wt['sq'][:], start=True, stop=True, r=['cst', 'w_sq'], w=[pk])
            S.act('activation', out=Wt['sq'][:], in_=pt[:, 0:TS], func=AF.Sqrt, bias=1e-12, r=[pk], w=['w_sq'])
            S.dve('reciprocal', out=Wt['sq'][:], in_=Wt['sq'][:], r=['w_sq'], w=['w_sq'])
            S.pool('tensor_tensor', out=Wt['kkn'][:], in0=Wt['kk'][:], in1=Wt['sq'][:], op=ALU.mult,
                  r=['w_kk', 'w_sq'], w=['w_kkn'])
            S.dve('tensor_scalar', out=Wt['km'][:], in0=Wt['a'][:], scalar1=V('ka', p),
                                                 scalar2=omv[:, vcol('ka', p):vcol('ka', p) + 1], op0=ALU.mult, op1=ALU.add,
                  r=['w_a', 'vec', 'omv'], w=['w_km'])
            S.pool('tensor_tensor', out=Wt['km'][:], in0=Wt['km'][:], in1=zk[:, 1:TS + 1], op=ALU.mult,
                  r=['w_km', 'zk'], w=['w_km'])
            S.act('activation', out=Wt['epos'][:], in_=Wt['cl'][:], func=AF.Exp, r=['w_cl'], w=['w_epos'])
            S.act('activation', out=Wt['eneg'][:], in_=Wt['cl'][:], func=AF.Exp, scale=-1.0, r=['w_cl'], w=['w_eneg'])
            S.pool('tensor_tensor', out=Wt['eprev'][:], in0=Wt['cl'][:], in1=Wt['ld'][:], op=ALU.subtract,
                  r=['w_cl', 'w_ld'], w=['w_eprev'])
            S.act('activation', out=Wt['eprev'][:], in_=Wt['eprev'][:], func=AF.Exp, r=['w_eprev'], w=['w_eprev'])
            S.dve('scalar_tensor_tensor', out=P_['At'][:], in0=Wt['kkn'][:], scalar=-1.0, in1=Wt['eprev'][:],
                                                   op0=ALU.mult, op1=ALU.mult, r=['w_kkn', 'w_eprev'], w=[K_('At')])
            S.pool('tensor_tensor', out=Wt['Bt'][:], in0=Wt['kkn'][:], in1=Wt['a'][:], op=ALU.mult,
                   r=['w_kkn', 'w_a'], w=['w_Bt'])
            S.pool('tensor_tensor', out=Wt['Bt'][:], in0=Wt['Bt'][:], in1=Wt['eneg'][:], op=ALU.mult,
                  r=['w_Bt', 'w_eneg'], w=['w_Bt'])
            S.pool('tensor_tensor', out=Wt['Kt'][:], in0=Wt['km'][:], in1=Wt['eneg'][:], op=ALU.mult,
                   r=['w_km', 'w_eneg'], w=['w_Kt'])
            S.pool('tensor_tensor', out=P_['Rt'][:], in0=zr[:, 1:TS + 1], in1=Wt['epos'][:], op=ALU.mult,
                  r=['zr', 'w_epos'], w=[K_('Rt')])
            S.dve('scalar_tensor_tensor', out=Wt['rkr'][:], in0=zr[:, 1:TS + 1], scalar=V('rk', p), in1=Wt['km'][:],
                                                        op0=ALU.mult, op1=ALU.mult, r=['zr', 'vec', 'w_km'], w=['w_rkr'])
            pt, pk = C.pst()
            S.pe('matmul', pt[:, 0:TS], bones, Wt['rkr'][:], start=True, stop=True, r=['cst', 'w_rkr'], w=[pk])
            S.dve('tensor_tensor', out=P_['bon'][:], in0=pt[:, 0:TS], in1=zv[:, 1:TS + 1], op=ALU.mult,
                  r=[pk, 'zv'], w=[K_('bon')])
            S.dve('tensor_copy', out=gam[:, p, :], in_=Wt['epos'][:].rearrange("q (c t) -> q c t", t=64)[:, :, 63],
                  r=['w_epos'], w=['gam'])
            to_bd('act', BD['At'], 'bd_At', P_['At'][:], [K_('At')])
            to_bd('pool', BD['Bt'], 'bd_Bt', Wt['Bt'][:], ['w_Bt'])
            to_bd('act', BD['Kt'], 'bd_Kt', Wt['Kt'][:], ['w_Kt'])
            to_bd('pool', BD['V'], 'bd_V', zv[:, 1:TS + 1], ['zv'])
            for (bdn, nm) in (('Bt', 'Btok'), ('Kt', 'Ktok'), ('V', 'Vtok')):
                pt, pk = C.pst()
                for c in range(NCH):
                    S.pe('matmul', pt[:, c * 64:(c + 1) * 64], BD[bdn][:, c, :], ident, start=True, stop=True,
                         r=['bd_' + bdn, 'cst'], w=[pk])
                S.act('activation', out=TOK[p][nm][:], in_=v3(pt[:, 0:TS]), func=AF.Identity, r=[pk], w=[K_(nm)])
            pt, pk = product_bd(BD['At'], 'bd_At', Wt['Bt'], 'w_Bt')
            S.dve('tensor_tensor', out=Nm[:], in0=pt[:, 0:TS], in1=m_ts, op=ALU.mult, r=[pk, 'cst'], w=['Nm'])
            to_bd('dve', BD['P0'], 'bd_P0', pt[:, 0:TS], [pk], kind=1)
            pt, pk = product_bd(BD['Bt'], 'bd_Bt', P_['At'], K_('At'))
            S.dve('tensor_tensor', out=NT[:], in0=pt[:, 0:TS], in1=m_st, op=ALU.mult, r=[pk, 'cst'], w=['NT'])
            to_bd('dve', BD['PT0'], 'bd_PT0', pt[:, 0:TS], [pk], kind=2)
            pt, pk = product_bd(BD['Kt'], 'bd_Kt', P_['At'], K_('At'))
            S.dve('tensor_tensor', out=P_['AKT'][:], in0=pt[:, 0:TS], in1=m_st, op=ALU.mult, r=[pk, 'cst'], w=[K_('AKT')])
            pt, pk = product_bd(BD['Bt'], 'bd_Bt', P_['Rt'], K_('Rt'))
            S.dve('tensor_tensor', out=P_['RBT'][:], in0=pt[:, 0:TS], in1=m_sti, op=ALU.mult, r=[pk, 'cst'], w=[K_('RBT')])
            pt, pk = product_bd(BD['Kt'], 'bd_Kt', P_['Rt'], K_('Rt'))
            S.dve('tensor_tensor', out=P_['RKT'][:], in0=pt[:, 0:TS], in1=m_sti, op=ALU.mult, r=[pk, 'cst'], w=[K_('RKT')])
            S.pool('tensor_tensor', out=TTw[0][:], in0=NT[:], in1=eye, op=ALU.add, r=['NT', 'cst'], w=['TTw0'])
            Pc, Pck, PTc, PTck = Nm, 'Nm', NT, 'NT'
            Pbd, Pbdk, PTbd, PTbdk = BD['P0'], 'bd_P0', BD['PT0'], 'bd_PT0'
            tcur = 0
            for lvl in range(5):
                nb = (lvl + 1) % 2
                pt, pk = product_bd(PTbd, PTbdk, Pc, Pck)
                to_bd('dve', BD['IP'], 'bd_IP', pt[:, 0:TS], [pk], kind='eye')
                if lvl < 4:
                    Pn, Pnk = Pw[nb], f"Pw{nb}"
                    S.dve('tensor_copy', out=Pn[:], in_=pt[:, 0:TS], r=[pk], w=[Pnk])
                    Pnbd, Pnbdk = BD[f'P{nb}'], f'bd_P{nb}'
                    to_bd('dve', Pnbd, Pnbdk, pt[:, 0:TS], [pk])
                    pt2, pk2 = product_bd(Pbd, Pbdk, PTc, PTck)
                    PTn, PTnk = PTw[nb], f"PTw{nb}"
                    S.act('activation', out=PTn[:], in_=pt2[:, 0:TS], func=AF.Identity, r=[pk2], w=[PTnk])
                    PTnbd, PTnbdk = BD[f'PT{nb}'], f'bd_PT{nb}'
                    to_bd('act', PTnbd, PTnbdk, pt2[:, 0:TS], [pk2])
                if lvl < 4:
                    dstT, dstk = TTw[1 - tcur], f"TTw{1 - tcur}"
                else:
                    dstT, dstk = P_['TT'], K_('TT')
                pt3, pk3 = product_bd(BD['IP'], 'bd_IP', TTw[tcur], f"TTw{tcur}")
                S.act('activation', out=dstT[:], in_=pt3[:, 0:TS], func=AF.Identity, r=[pk3], w=[dstk])
                tcur = 1 - tcur
                if lvl < 4:
                    Pc, Pck, PTc, PTck = Pn, Pnk, PTn, PTnk
                    Pbd, Pbdk, PTbd, PTbdk = Pnbd, Pnbdk, PTnbd, PTnbdk
            next(lg, None)
        for c in range(NCH):
            sl = slice(c * 64, (c + 1) * 64)
            Hc, Hck = Hb[cur], f"Hb{cur}"
            Hn, Hnk = Hb[1 - cur], f"Hb{1 - cur}"
            psX = []
            for e_ in range(2):
                pb = 64 * e_
                pt, pk = C.pst()
                psX.append((pt, pk))
                for p in range(4):
                    cs = slice(p * 64, (p + 1) * 64)
                    S.pe('matmul', pt[pb:pb + 64, cs], PP[p]['At'][pb:pb + 64, sl],
                                                                      Hc[pb:pb + 64, cs], start=True, stop=False,
                         r=[f"At{p}", Hck], w=[pk])
                    S.pe('matmul', pt[pb:pb + 64, cs], PP[p]['AKT'][pb:pb + 64, sl],
                                                                      TOK[p]['Vtok'][pb:pb + 64, c, :], start=False, stop=True,
                         r=[f"AKT{p}", f"Vtok{p}"], w=[pk])
                S.act('activation', out=Xsb[pb:pb + 64, :], in_=pt[pb:pb + 64, 0:256], func=AF.Identity,
                      r=[pk], w=['Xsb'])
            psU = []
            for e_ in range(2):
                pb = 64 * e_
                pt, pk = C.pst()
                for p in range(4):
                    cs = slice(p * 64, (p + 1) * 64)
                    S.pe('matmul', pt[pb:pb + 64, cs], PP[p]['TT'][pb:pb + 64, sl],
                                                                      Xsb[pb:pb + 64, cs], start=True, stop=True,
                         r=[f"TT{p}", 'Xsb'], w=[pk])
                S.dve('tensor_copy', out=Usb[pb:pb + 64, :], in_=pt[pb:pb + 64, 0:256], r=[pk], w=['Usb'])
            for e_ in range(2):
                pb = 64 * e_
                pt, pk = C.pst()
                for p in range(4):
                    cs = slice(p * 64, (p + 1) * 64)
                    S.pe('matmul', pt[pb:pb + 64, cs], TOK[p]['Btok'][pb:pb + 64, c, :],
                                                                      Usb[pb:pb + 64, cs], start=True, stop=False,
                         r=[f"Btok{p}", 'Usb'], w=[pk])
                    S.pe('matmul', pt[pb:pb + 64, cs], TOK[p]['Ktok'][pb:pb + 64, c, :],
                                                                      TOK[p]['Vtok'][pb:pb + 64, c, :], start=False, stop=True,
                         r=[f"Ktok{p}", f"Vtok{p}"], w=[pk])
                S.dve('tensor_tensor', out=Htmp[pb:pb + 64, :], in0=pt[pb:pb + 64, 0:256],
                                                              in1=Hc[pb:pb + 64, :], op=ALU.add, r=[pk, Hck], w=['Htmp'])
                S.pool('tensor_tensor',
                    out=Hn[pb:pb + 64, :].rearrange("q (a b) -> q a b", b=64),
                    in0=Htmp[pb:pb + 64, :].rearrange("q (a b) -> q a b", b=64),
                    in1=gam[pb:pb + 64, :, c:c + 1].broadcast_to([64, 4, 64]), op=ALU.mult,
                    r=['Htmp', 'gam'], w=[Hnk])
                pt, pk = C.pst()
                for p in range(4):
                    cs = slice(p * 64, (p + 1) * 64)
                    S.pe('matmul', pt[pb:pb + 64, cs], PP[p]['Rt'][pb:pb + 64, sl],
                                                                      Hc[pb:pb + 64, cs], start=True, stop=False,
                         r=[f"Rt{p}", Hck], w=[pk])
                    S.pe('matmul', pt[pb:pb + 64, cs], PP[p]['RBT'][pb:pb + 64, sl],
                                                                      Usb[pb:pb + 64, cs], start=False, stop=False,
                         r=[f"RBT{p}", 'Usb'], w=[pk])
                    S.pe('matmul', pt[pb:pb + 64, cs], PP[p]['RKT'][pb:pb + 64, sl],
                                                                      TOK[p]['Vtok'][pb:pb + 64, c, :], start=False, stop=True,
                         r=[f"RKT{p}", f"Vtok{p}"], w=[pk])
                S.act('activation',
                    out=Ysb[pb:pb + 64, c, :, :], in_=pt[pb:pb + 64, 0:256].rearrange("q (a b) -> q a b", b=64), func=AF.Identity,
                    r=[pk], w=['Ysb'])
            cur = 1 - cur
        Yv = Ysb[:].rearrange("q c p i -> q (c p) i")
        dv = dY[:].rearrange("q c p i -> q (c p) i")
        sv = sqY[:].rearrange("q c p i -> q (c p) i")
        G = NCH * 4
        S.dve('tensor_reduce', out=st1[:], in_=Yv, axis=AX.X, op=ALU.add, r=['Ysb'], w=['st1'])
        S.dve('tensor_scalar', out=st1[:], in0=st1[:], scalar1=-1.0 / 64, scalar2=None, op0=ALU.mult, r=['st1'], w=['st1'])
        S.dve('tensor_tensor', out=dv, in0=Yv, in1=st1[:].unsqueeze(2).broadcast_to([128, G, 64]), op=ALU.add,
              r=['Ysb', 'st1'], w=['dY'])
        S.pool('tensor_tensor', out=sv, in0=dv, in1=dv, op=ALU.mult, r=['dY'], w=['sqY'])
        S.dve('tensor_reduce', out=st2[:], in_=sv, axis=AX.X, op=ALU.add, r=['sqY'], w=['st2'])
        S.dve('tensor_scalar', out=st2[:], in0=st2[:], scalar1=1.0 / 64, scalar2=GN_EPS, op0=ALU.mult, op1=ALU.add,
              r=['st2'], w=['st2'])
        S.act('activation', out=st2[:], in_=st2[:], func=AF.Sqrt, r=['st2'], w=['st2'])
        S.dve('reciprocal', out=st2[:], in_=st2[:], r=['st2'], w=['st2'])
        S.dve('tensor_tensor', out=dv, in0=dv, in1=st2[:].unsqueeze(2).broadcast_to([128, G, 64]), op=ALU.mult,
              r=['dY', 'st2'], w=['dY'])
        for p in range(4):
            for e_ in range(2):
                pb = 64 * e_
                pt, pk = C.pst()
                for c in range(NCH):
                    S.pe('matmul', pt[pb:pb + 64, c * 64:(c + 1) * 64], dY[pb:pb + 64, c, p, :],
                                                                    ident[pb:pb + 64, :], start=True, stop=True,
                         r=['dY', 'cst'], w=[pk])
                S.act('activation',
                    out=ycs[p][pb:pb + 64, :], in_=pt[pb:pb + 64, 0:TS], func=AF.Identity,
                    scale=vec_sb[pb:pb + 64, vcol('lnw', p):vcol('lnw', p) + 1],
                    bias=vec_sb[pb:pb + 64, vcol('lnb', p):vcol('lnb', p) + 1], r=[pk, 'vec'], w=[f"ycs{p}"])
            S.dve('tensor_tensor', out=ycs[p][:], in0=ycs[p][:], in1=PP[p]['bon'][:], op=ALU.add,
                  r=[f"ycs{p}", f"bon{p}"], w=[f"ycs{p}"])
            S.pool('tensor_tensor', out=ycs[p][:], in0=ycs[p][:], in1=PP[p]['gT'][:], op=ALU.mult,
                   r=[f"ycs{p}", f"gT{p}"], w=[f"ycs{p}"])
            S.dma(out=io[f"ys{5 + p}"][:, t0:t0 + TS], in_=ycs[p][:], r=[f"ycs{p}"])
    S.replay_side()
    C.finish()


def pair_rows(hh):
    idx = np.zeros(512, np.int64)
    for p in range(4):
        for e in range(2):
            h = hh * 8 + p + 4 * e
            idx[p * 128 + e * 64:p * 128 + e * 64 + 64] = h * 64 + np.arange(64)
    return idx


def arr128(v):
    return np.ascontiguousarray(v.reshape(-1, 128).T)


def rwkv_inputs(inp, l, b, hh, xT_b, vfT=None):
    idx = pair_rows(hh)
    w_in = inp['w_in'][l]
    base = 1024 + 1024 + 768 * 3
    cols = np.concatenate([base + idx, base + 1024 + idx, base + 2048 + idx,
                           base + 3072 + np.arange(64), base + 3136 + np.arange(64), base + 3200 + np.arange(128)])
    wc = w_in[:, cols]
    if l:
        wc = np.concatenate([wc, inp['w_in_vres'][l - 1]], axis=1)
    mu = inp['rwkv_mu'][l]
    vec = np.zeros((128, NVEC), np.float32)

    def setv(name, full):
        for p in range(4):
            vec[:, vcol(name, p)] = full[idx[p * 128:(p + 1) * 128]]
    setv('mu_r', mu[0:1024]); setv('mu_k', mu[1024:2048]); setv('mu_v', mu[2048:3072])
    setv('w0', inp['w0'][l]); setv('a0', inp['a0'][l]); setv('kk', inp['k_k'][l]); setv('ka', inp['k_a'][l])
    setv('rk', inp['r_k'][l].reshape(-1)); setv('lnw', inp['ln_x_w'][l]); setv('lnb', inp['ln_x_b'][l])
    if l:
        setv('v0', inp['v0'][l - 1])
        vec[0:32, V_MUZV1] = inp['mu_vres'][l - 1]
    vec[0:64, V_MUZW] = mu[3072:3136]
    vec[0:64, V_MUZA] = mu[3136:3200]
    vec[:, V_MUZG] = mu[3200:3328]
    d = {
        'xT': xT_b, 'cT': arr128(inp['c'][b]),
        'modw': np.ascontiguousarray(inp['mod_w'][l][:, 0:2048]),
        'modb': arr128(inp['mod_b'][l][0:2048]),
        'wc': np.ascontiguousarray(wc), 'vec': vec,
        'w2c': np.ascontiguousarray(inp['w2'][l][:, idx]), 'a2c': np.ascontiguousarray(inp['a2'][l][:, idx]),
        'g2c': np.ascontiguousarray(inp['g2'][l][:, idx]), 'cst': rwkv_consts(),
    }
    if l:
        d['v2c'] = np.ascontiguousarray(inp['v2'][l - 1][:, idx])
    return d


def common_inputs(C):
    xT = C.din("xT", [1024, SEQ])
    cT = C.din("cT", [128, 8])
    modw = C.din("modw", [1024, 2048])
    modb = C.din("modb", [128, 16])
    return xT, cT, modw, modb


def common_mod(C, cT, modw, modb):
    S = C.S
    cT_sb = C.sb("cT", [128, 8]); S.dma(out=cT_sb[:], in_=cT, w=['cT'])
    modb_sb = C.sb("modb", [128, 16]); S.dma(out=modb_sb[:], in_=modb, w=['modb'])
    mod_sb = C.sb("mod", [128, 16])
    emit_mod(C, modw, modb_sb, cT_sb, 2048, mod_sb)
    S.dve('tensor_scalar', out=mod_sb[:, 8:16], in0=mod_sb[:, 8:16], scalar1=1.0, scalar2=None, op0=ALU.add,
          r=['mod'], w=['mod'])
    return mod_sb


LT = 256
LV_NAMES = ['cw0', 'cw1', 'cw2', 'cw3', 'cb', 'ba', 'bx', 'lam']


def emit_lru(nc, io, layer):
    C = Ctx(nc, f"lr{layer}")
    C.init_psum()
    for _ in lru_body(C, io, layer):
        pass
    C.finish()


def lru_body(C, io, layer):
    S = C.S
    wa = io[f'wa_{layer}']; vecl = io[f'vecl_{layer}']; gw = io[f'gw_{layer}']
    mod_sb = load_mod(C, io, layer)
    vl = C.sb("vecl", [128, 32]); S.dma(out=vl[:], in_=vecl, w=['vecl'])
    gw_sb = C.sb("gw", [128, 8, 128]); S.dma(out=gw_sb[:], in_=gw, w=['gw'])
    wab = C.sb("wab", [128, 8, 1024], BF16)
    wav = wa.rearrange("(k p) c -> p k c", p=128)
    for c0 in (0, 512):
        S.dma(out=wab[:, :, c0:c0 + 512], in_=wav[:, :, c0:c0 + 512], w=['wab'], q='pool')
    if not _os.environ.get("KDEBUG_NOCAST"):
        for m in range(8):
            S.dma(out=io[f'WGb_{layer}'][m].rearrange("p (k c) -> p k c", k=8),
                  in_=io[f'WG_{layer}'][m].rearrange("(k p) c -> p k c", p=128), q='pool')
            S.dma(out=io[f'WPb_{layer}'][m].rearrange("p (k c) -> p k c", k=18),
                  in_=io[f'WP_{layer}'][m].rearrange("(k p) c -> p k c", p=128), q='pool')
            S.dma(out=io[f'WOb_{layer}'][m].rearrange("p (k c) -> p k c", k=8),
                  in_=io[f'WO_{layer}'].rearrange("(k p) c -> p k c", p=128)[:, :, m * 128:(m + 1) * 128], q='pool')
            S.dma(out=io[f'WDb_{layer}'][m].rearrange("p (k c) -> p k c", k=22),
                  in_=io[f'WD_{layer}'].rearrange("(k p) c -> p k c", p=128)[:, :, m * 128:(m + 1) * 128], q='pool')
        for j in range(22):
            S.dma(out=io[f'WUb_{layer}'][j].rearrange("p (k c) -> p k c", k=8),
                  in_=io[f'WU_{layer}'][j].rearrange("(k p) c -> p k c", p=128), q='pool')
    LVc = lambda n, j: vl[:, LV_NAMES.index(n) * 4 + j:LV_NAMES.index(n) * 4 + j + 1]
    clam = C.sb("clam", [128, 4])
    S.act('activation', out=clam[:], in_=vl[:, 28:32], func=AF.Exp, scale=-1.0, r=['vecl'], w=['clam'])
    S.act('activation', out=clam[:], in_=clam[:], func=AF.Ln, bias=1.0, r=['clam'], w=['clam'])
    S.dve('tensor_scalar', out=clam[:], in0=clam[:], scalar1=-8.0, scalar2=None, op0=ALU.mult, r=['clam'], w=['clam'])
    hs = C.sb("hs", [128, 8, LT], BF16)
    xk = [C.sb("xk0", [128, LT]), C.sb("xk1", [128, LT])]
    xal = C.sb("xal", [128, 4, 3])
    S.pool('memset', xal[:], 0.0, w=['xal'])
    xa = C.sb("xa", [128, LT + 3]); gg = C.sb("gg", [128, LT]); xc = C.sb("xc", [128, LT])
    rr = C.sb("rr", [128, LT]); ii = C.sb("ii", [128, LT]); aa = C.sb("aa", [128, LT]); om = C.sb("om", [128, LT])
    hq = C.sb("hq", [128, LT]); carry = C.sb("carry", [128, 4])
    S.pool('memset', carry[:], 0.0, w=['carry'])
    yield
    for sg in range(SEQ // LT):
        t0 = sg * LT
        for k in range(8):
            xb, xkey = xk[k % 2], f"xk{k % 2}"
            for (o_, n_, src_) in x_pieces(io, layer, k, t0, t0 + LT):
                S.dma(out=xb[:, o_:o_ + n_], in_=src_, w=[xkey], allow_slow_non_contiguous=(n_ < 2))
            S.act('activation', out=hs[:, k, :], in_=xb[:], func=AF.Identity,
                  scale=mod_sb[:, 8 + k:9 + k], bias=mod_sb[:, k:k + 1], r=[xkey, 'mod'], w=['hs'])
        for j in range(4):
            pt, pk = C.pst()
            for k in range(8):
                S.pe('matmul', pt[:, 0:LT], wab[:, k, j * 128:(j + 1) * 128], hs[:, k, :],
                     start=(k == 0), stop=(k == 7), r=['wab', 'hs'], w=[pk])
            S.pool('tensor_copy', out=xa[:, 0:3], in_=xal[:, j, :], r=['xal'], w=['xa'])
            S.act('activation', out=xa[:, 3:LT + 3], in_=pt[:, 0:LT], func=AF.Identity, r=[pk], w=['xa'])
            S.pool('tensor_copy', out=xal[:, j, :], in_=xa[:, LT:LT + 3], r=['xa'], w=['xal'])
            pt, pk = C.pst()
            for k in range(8):
                S.pe('matmul', pt[:, 0:LT], wab[:, k, 512 + j * 128:512 + (j + 1) * 128], hs[:, k, :],
                     start=(k == 0), stop=(k == 7), r=['wab', 'hs'], w=[pk])
            S.act('activation', out=gg[:], in_=pt[:, 0:LT], func=AF.Gelu, r=[pk], w=['gg'])
            S.dve('tensor_scalar', out=xc[:], in0=xa[:, 3:LT + 3], scalar1=LVc('cw0', j), scalar2=LVc('cb', j),
                  op0=ALU.mult, op1=ALU.add, r=['xa', 'vecl'], w=['xc'])
            for jj in (1, 2, 3):
                S.dve('scalar_tensor_tensor', out=xc[:], in0=xa[:, 3 - jj:LT + 3 - jj], scalar=LVc(f'cw{jj}', j), in1=xc[:],
                      op0=ALU.mult, op1=ALU.add, r=['xa', 'vecl', 'xc'], w=['xc'])
            pt, pk = C.pst()
            S.pe('matmul', pt[:, 0:LT], gw_sb[:, j, :], xc[:], start=True, stop=True, r=['gw', 'xc'], w=[pk])
            S.act('activation', out=rr[:], in_=pt[:, 0:LT], func=AF.Sigmoid, bias=LVc('ba', j), r=[pk, 'vecl'], w=['rr'])
            pt, pk = C.pst()
            S.pe('matmul', pt[:, 0:LT], gw_sb[:, 4 + j, :], xc[:], start=True, stop=True, r=['gw', 'xc'], w=[pk])
            S.act('activation', out=ii[:], in_=pt[:, 0:LT], func=AF.Sigmoid, bias=LVc('bx', j), r=[pk, 'vecl'], w=['ii'])
            S.act('activation', out=aa[:], in_=rr[:], func=AF.Exp, scale=clam[:, j:j + 1], r=['rr', 'clam'], w=['aa'])
            S.pool('tensor_tensor', out=om[:], in0=aa[:], in1=aa[:], op=ALU.mult, r=['aa'], w=['om'])
            S.dve('tensor_scalar', out=om[:], in0=om[:], scalar1=-1.0, scalar2=1.0, op0=ALU.mult, op1=ALU.add, r=['om'], w=['om'])
            S.dve('tensor_scalar', out=om[:], in0=om[:], scalar1=1e-30, scalar2=None, op0=ALU.max, r=['om'], w=['om'])
            S.act('activation', out=om[:], in_=om[:], func=AF.Sqrt, r=['om'], w=['om'])
            S.dve('tensor_tensor', out=ii[:], in0=ii[:], in1=xc[:], op=ALU.mult, r=['ii', 'xc'], w=['ii'])
            S.dve('tensor_tensor', out=ii[:], in0=ii[:], in1=om[:], op=ALU.mult, r=['ii', 'om'], w=['ii'])
            S.dve('tensor_tensor_scan', out=hq[:], data0=aa[:], data1=ii[:], initial=carry[:, j:j + 1], op0=ALU.mult, op1=ALU.add,
                  r=['aa', 'ii', 'carry'], w=['hq'])
            S.dve('tensor_copy', out=carry[:, j:j + 1], in_=hq[:, LT - 1:LT], r=['hq'], w=['carry'])
            S.pool('tensor_tensor', out=hq[:], in0=hq[:], in1=gg[:], op=ALU.mult, r=['hq', 'gg'], w=['hq'])
            S.dma(out=io[f'ys{j}'][:, t0:t0 + LT], in_=hq[:], r=['hq'])
            yield


def mod_inputs(inp, l, b, xT_b, lo=0):
    return {'xT': xT_b, 'cT': arr128(inp['c'][b]),
            'modw': np.ascontiguousarray(inp['mod_w'][l][:, lo:lo + 2048]),
            'modb': arr128(inp['mod_b'][l][lo:lo + 2048])}


def lru_inputs(inp, l, b, hh, xT_b):
    d = mod_inputs(inp, l, b, xT_b)
    ch = hh * 512 + np.arange(512)
    w_in = inp['w_in'][l]
    d['wa'] = np.ascontiguousarray(np.concatenate([w_in[:, ch], w_in[:, 1024 + ch]], axis=1))
    vecl = np.zeros((128, 32), np.float32)
    srcs = {'cw0': inp['conv_a_w'][l][0], 'cw1': inp['conv_a_w'][l][1], 'cw2': inp['conv_a_w'][l][2],
            'cw3': inp['conv_a_w'][l][3], 'cb': inp['conv_a_b'][l], 'ba': inp['lru_ba'][l], 'bx': inp['lru_bx'][l],
            'lam': inp['lru_lambda'][l]}
    for n, v in srcs.items():
        vecl[:, LV_NAMES.index(n) * 4:LV_NAMES.index(n) * 4 + 4] = arr128(v[ch])
    d['vecl'] = vecl
    gw = np.zeros((128, 8, 128), np.float32)
    for j in range(4):
        for e in range(2):
            g = hh * 8 + j * 2 + e
            gw[e * 64:(e + 1) * 64, j, e * 64:(e + 1) * 64] = inp['lru_wa'][l][g]
            gw[e * 64:(e + 1) * 64, 4 + j, e * 64:(e + 1) * 64] = inp['lru_wx'][l][g]
    d['gw'] = gw
    return d


AT = 512
DMAXS = (1, 4, 16)
DILS = (1, 4, 16)
NSTRIP = sum(d + 1 for d in DMAXS)
TWO_PI = float(2 * np.pi)
CW1 = 6.28125
CW2 = float(2 * np.pi - 6.28125)


def attn_strips():
    st = np.zeros((128, NSTRIP, 128), np.float32)
    tk = np.arange(128)[:, None]
    tq = np.arange(128)[None, :]
    o = 0
    for g in range(3):
        dil, dm = DILS[g], DMAXS[g]
        for bi in range(dm + 1):
            delta = dm - bi
            dist = 128 * delta + tq - tk
            st[:, o + bi, :] = (dist >= 0) & (dist <= 128 * dil) & (dist % dil == 0)
        o += dm + 1
    return st.reshape(128, NSTRIP * 128)


def emit_attn(nc, io, layer):
    C = Ctx(nc, f"at{layer}"); S = C.S
    wqk = io[f'wqk_{layer}']; wv = io[f'wv_{layer}']
    pos = io['pos']; freq = io['freq']; strips = io['strips']
    ybT = io['ys4']
    C.init_psum()
    mod_sb = load_mod(C, io, layer)
    fr = C.sb("freq", [128, 1]); S.dma(out=fr[:], in_=freq, w=['freq'])
    strip_sb = C.sb("strips", [128, NSTRIP * 128]); S.dma(out=strip_sb[:], in_=strips, w=['strips'])
    wqkb = C.sb("wqkb", [128, 8, 768], BF16)
    S.dma(out=wqkb[:, :, 0:384], in_=wqk.rearrange("(k p) c -> p k c", p=128)[:, :, 0:384], w=['wqkb'], q='pool')
    S.dma(out=wqkb[:, :, 384:768], in_=wqk.rearrange("(k p) c -> p k c", p=128)[:, :, 384:768], w=['wqkb'], q='pool')
    wvb = C.sb("wvb", [128, 8, 384], BF16)
    S.dma(out=wvb[:], in_=wv.rearrange("(k p) c -> p k c", p=128), w=['wvb'], q='pool')
    W2 = C.sb("W2", [128, 8, 768], BF16)
    S.pool('memset', W2[:], 0.0, w=['W2'])
    for ch in range(6):
        for hd in range(2):
            bs = ch * 128 + hd * 64
            S.dve('tensor_scalar', out=W2[:, :, bs:bs + 8], in0=wqkb[:, :, bs + 8:bs + 16], scalar1=-1.0, scalar2=None,
                  op0=ALU.mult, r=['wqkb', 'W2'], w=['W2'])
            S.dve('tensor_copy', out=W2[:, :, bs + 8:bs + 16], in_=wqkb[:, :, bs:bs + 8], r=['wqkb', 'W2'], w=['W2'])
    ones_bf = C.sb("ones_bf", [128, 64], BF16)
    S.pool('memset', ones_bf[:], 1.0, w=['ones_bf'])
    hs = C.sb("hs", [128, 8, AT], BF16)
    xk = [C.sb("xk0", [128, AT]), C.sb("xk1", [128, AT])]
    qk = [C.sb(f"qk{ch}", [128, SEQ], BF16) for ch in range(6)]
    Vtok = C.sb("Vtok", [128, SEQ // 128, 384], BF16)
    posi = C.sb("posi", [128, AT], I32)
    ang = C.sb("ang", [128, AT]); a2 = C.sb("a2", [128, AT]); tq_ = C.sb("tq", [128, AT]); ki = C.sb("ki", [128, AT], I32)
    kf = C.sb("kf", [128, AT]); mk = C.sb("mk", [128, AT])
    Ct = C.sb("Ct", [128, AT]); St = C.sb("St", [128, AT])
    t1 = C.sb("t1", [128, AT]); t2 = C.sb("t2", [128, AT])
    Pbuf = C.sb("Pbuf", [128, 2, NSTRIP, 128], BF16)
    rd = C.sb("rd", [64, 2, 128]); yb = C.sb("yb", [64, 2, 128])

    for tl in range(SEQ // AT):
        t0 = tl * AT
        for k in range(8):
            xb, xkey = xk[k % 2], f"xk{k % 2}"
            for (o_, n_, src_) in x_pieces(io, layer, k, t0, t0 + AT):
                S.dma(out=xb[:, o_:o_ + n_], in_=src_, w=[xkey], allow_slow_non_contiguous=(n_ < 2))
            S.act('activation', out=hs[:, k, :], in_=xb[:], func=AF.Identity,
                  scale=mod_sb[:, 8 + k:9 + k], bias=mod_sb[:, k:k + 1], r=[xkey, 'mod'], w=['hs'])
        S.dma(out=posi[:], in_=pos[:, t0:t0 + AT].partition_broadcast(128), w=['posi'])
        S.dve('tensor_copy', out=ang[:], in_=posi[:], r=['posi'], w=['ang'])
        S.dve('tensor_scalar', out=ang[:], in0=ang[:], scalar1=fr[:, 0:1], scalar2=None, op0=ALU.mult, r=['ang', 'freq'], w=['ang'])
        for (tbl, tkey, shf) in ((St, 'St', 0.0), (Ct, 'Ct', float(np.pi / 2))):
            S.dve('tensor_scalar', out=a2[:], in0=ang[:], scalar1=shf, scalar2=None, op0=ALU.add, r=['ang'], w=['a2'])
            S.dve('tensor_scalar', out=tq_[:], in0=a2[:], scalar1=1.0 / TWO_PI, scalar2=None, op0=ALU.mult, r=['a2'], w=['tq'])
            S.dve('tensor_copy', out=ki[:], in_=tq_[:], r=['tq'], w=['ki'])
            S.dve('tensor_copy', out=kf[:], in_=ki[:], r=['ki'], w=['kf'])
            S.dve('scalar_tensor_tensor', out=a2[:], in0=kf[:], scalar=-CW1, in1=a2[:], op0=ALU.mult, op1=ALU.add,
                  r=['kf', 'a2'], w=['a2'])
            S.dve('scalar_tensor_tensor', out=a2[:], in0=kf[:], scalar=-CW2, in1=a2[:], op0=ALU.mult, op1=ALU.add,
                  r=['kf', 'a2'], w=['a2'])
            S.dve('tensor_scalar', out=mk[:], in0=a2[:], scalar1=float(np.pi), scalar2=None, op0=ALU.is_gt, r=['a2'], w=['mk'])
            S.dve('scalar_tensor_tensor', out=a2[:], in0=mk[:], scalar=-TWO_PI, in1=a2[:], op0=ALU.mult, op1=ALU.add,
                  r=['mk', 'a2'], w=['a2'])
            S.dve('tensor_scalar', out=mk[:], in0=a2[:], scalar1=-float(np.pi), scalar2=None, op0=ALU.is_lt, r=['a2'], w=['mk'])
            S.dve('scalar_tensor_tensor', out=a2[:], in0=mk[:], scalar=TWO_PI, in1=a2[:], op0=ALU.mult, op1=ALU.add,
                  r=['mk', 'a2'], w=['a2'])
            S.dve('tensor_scalar', out=a2[:], in0=a2[:], scalar1=-3.1415925, scalar2=3.1415925, op0=ALU.max, op1=ALU.min,
                  r=['a2'], w=['a2'])
            S.act('activation', out=tbl[:], in_=a2[:], func=AF.Sin, r=['a2'], w=[tkey])
        for ch in range(6):
            p1, p1k = C.pst()
            for k in range(8):
                S.pe('matmul', p1[:, 0:AT], wqkb[:, k, ch * 128:(ch + 1) * 128], hs[:, k, :], start=(k == 0), stop=(k == 7),
                     r=['wqkb', 'hs'], w=[p1k])
            p2, p2k = C.pst()
            for k in range(8):
                S.pe('matmul', p2[:, 0:AT], W2[:, k, ch * 128:(ch + 1) * 128], hs[:, k, :], start=(k == 0), stop=(k == 7),
                     r=['W2', 'hs'], w=[p2k])
            S.dve('tensor_tensor', out=t1[:], in0=p1[:, 0:AT], in1=Ct[:], op=ALU.mult, r=[p1k, 'Ct'], w=['t1'])
            S.dve('tensor_tensor', out=t2[:], in0=p2[:, 0:AT], in1=St[:], op=ALU.mult, r=[p2k, 'St'], w=['t2'])
            S.pool('tensor_tensor', out=qk[ch][:, t0:t0 + AT], in0=t1[:], in1=t2[:], op=ALU.add, r=['t1', 't2'], w=[f"qk{ch}"])
        for sub in range(AT // 128):
            blk = tl * (AT // 128) + sub
            pt, pk = C.pst()
            for k in range(8):
                S.pe('matmul', pt[:, 0:384], hs[:, k, sub * 128:(sub + 1) * 128], wvb[:, k, :], start=(k == 0), stop=(k == 7),
                     r=['hs', 'wvb'], w=[pk])
            S.act('activation', out=Vtok[:, blk, :], in_=pt[:, 0:384], func=AF.Identity, r=[pk], w=['Vtok'])

    for i in range(SEQ // 128):
        entries = []
        j = 0
        for g in range(3):
            kb0 = max(0, i - DMAXS[g])
            sbase = sum(d + 1 for d in DMAXS[:g])
            kbs = list(range(kb0, i + 1))
            for u0 in range(0, len(kbs), 4):
                grp = kbs[u0:u0 + 4]
                n = len(grp)
                for s in range(2):
                    pt, pk = C.pst()
                    for u, kb in enumerate(grp):
                        S.pe('matmul', pt[:, u * 128:(u + 1) * 128], qk[3 + g][s * 64:(s + 1) * 64, kb * 128:(kb + 1) * 128],
                             qk[g][s * 64:(s + 1) * 64, i * 128:(i + 1) * 128], start=True, stop=True,
                             r=[f"qk{3 + g}", f"qk{g}"], w=[pk])
                    S.act('activation', out=Pbuf[:, s, j:j + n, :], in_=pt[:, 0:n * 128].rearrange("p (n q) -> p n q", q=128),
                          func=AF.Exp, scale=0.125, r=[pk], w=['Pbuf'])
                so = sbase + (DMAXS[g] - (i - grp[0]))
                mview = strip_sb[:, so * 128:(so + n) * 128].rearrange("p (n q) -> p n q", q=128).unsqueeze(1).broadcast_to([128, 2, n, 128])
                S.dve('tensor_tensor', out=Pbuf[:, :, j:j + n, :], in0=Pbuf[:, :, j:j + n, :], in1=mview, op=ALU.mult,
                      r=['Pbuf', 'strips'], w=['Pbuf'])
                for kb in grp:
                    entries.append((g, kb, j))
                    j += 1
        pt, pk = C.pst()
        ne = len(entries)
        for s in range(2):
            for idx, (g, kb, jj) in enumerate(entries):
                S.pe('matmul', pt[0:64, s * 128:(s + 1) * 128], Vtok[:, kb, g * 128 + s * 64:g * 128 + (s + 1) * 64], Pbuf[:, s, jj, :],
                     start=(idx == 0), stop=(idx == ne - 1), r=['Vtok', 'Pbuf'], w=[pk])
        for idx, (g, kb, jj) in enumerate(entries):
            S.pe('matmul', pt[0:64, 256:512], ones_bf[:], Pbuf[:, :, jj, :],
                 start=(idx == 0), stop=(idx == ne - 1), r=['ones_bf', 'Pbuf'], w=[pk])
        S.dve('reciprocal', out=rd[:], in_=pt[0:64, 256:512].rearrange("p (s q) -> p s q", s=2), r=[pk], w=['rd'])
        S.dve('tensor_tensor', out=yb[:], in0=pt[0:64, 0:256].rearrange("p (s q) -> p s q", s=2), in1=rd[:], op=ALU.mult,
              r=[pk, 'rd'], w=['yb'])
        S.dma(out=ybT.rearrange("(s d) t -> d s t", d=64)[:, :, i * 128:(i + 1) * 128], in_=yb[:], r=['yb'])
    C.finish()


def attn_inputs(inp, l, b, hh, xT_b):
    d = mod_inputs(inp, l, b, xT_b)
    w_in = inp['w_in'][l]
    qb, kb_, vb = 2048, 2048 + 768, 2048 + 1536
    cols = np.concatenate([(g * 4 + 2 * hh) * 64 + np.arange(128) for g in range(3)])
    d['wqk'] = np.ascontiguousarray(np.concatenate([w_in[:, qb + cols], w_in[:, kb_ + cols]], axis=1))
    d['wv'] = np.ascontiguousarray(w_in[:, vb + cols])
    d['pos'] = np.ascontiguousarray(inp['positions'][b][None, :]).astype(np.int32)
    fr = np.zeros((128, 1), np.float32)
    inv = (500000.0 ** (-np.arange(8, dtype=np.float32) / np.float32(8))).astype(np.float32)
    for p in range(128):
        if p % 64 < 16:
            fr[p, 0] = inv[(p % 64) % 8]
    d['freq'] = fr
    d['strips'] = attn_strips()
    return d


DN = 410
DTOK = 2050
NDT = 5
DV_NAMES = ['ln1w', 'ln1b', 'ln2w', 'ln2b']
FV_NAMES = ['fw0', 'fw1', 'fw2', 'fb']


def emit_dense(nc, io, layer):
    C = Ctx(nc, f"dn{layer}"); S = C.S
    WG = io[f'WGb_{layer}']; WP = io[f'WPb_{layer}']; WO = io[f'WOb_{layer}']; WU = io[f'WUb_{layer}']; WD = io[f'WDb_{layer}']
    dvec = io[f'dvec_{layer}']; fvec = io[f'fvec_{layer}']
    flag = io['flag']; onesm = io['onesm']
    C.init_psum()
    mod_sb = load_mod(C, io, layer)
    SH1, SC1, GT1, SH2, SC2, GT2 = 0, 8, 16, 24, 32, 40
    dv = C.sb("dvec", [128, 32]); S.dma(out=dv[:], in_=dvec, w=['dvec'])
    fv = C.sb("fvec", [128, 176]); S.dma(out=fv[:], in_=fvec, w=['fvec'])
    fl = C.sb("flag", [128, 2]); S.dma(out=fl[:], in_=flag, w=['flag'])
    om = C.sb("onesm", [128, 128]); S.dma(out=om[:], in_=onesm, w=['onesm'])
    DVc = lambda n, k: dv[:, DV_NAMES.index(n) * 8 + k:DV_NAMES.index(n) * 8 + k + 1]
    FVc = lambda n, j: fv[:, FV_NAMES.index(n) * 44 + j:FV_NAMES.index(n) * 44 + j + 1]
    wg = [C.sb(f"wg{i}", [128, 8, 384], BF16) for i in range(2)]
    wp = [C.sb(f"wp{i}", [128, 18, 128], BF16) for i in range(2)]
    wo = [C.sb(f"wo{i}", [128, 8, 128], BF16) for i in range(2)]
    wu = [C.sb(f"wu{i}", [128, 8, 256], BF16) for i in range(2)]
    wd = [C.sb(f"wd{i}", [128, 22, 128], BF16) for i in range(2)]
    xs = C.sb("xs", [128, 8, DN]); hs = C.sb("hs", [128, 8, DN], BF16)
    ya = C.sb("ya", [128, 8, DN], BF16); yc = C.sb("yc", [128, 8, DN], BF16); yb = C.sb("yb", [128, 2, DN], BF16)
    yA = C.sb("yA", [128, 8, DN]); yB = C.sb("yB", [128, 8, DN])
    S.pool('memset', yA[:], 0.0, w=['yA'])
    S.pool('memset', yB[:], 0.0, w=['yB'])
    mg = C.sb("mg", [128, 8, DN], BF16)
    r1b = [C.sb("r1_0", [128, 8, DN]), C.sb("r1_1", [128, 8, DN])]; r2 = C.sb("r2", [128, 8, DN])
    sg_ = [C.sb(f"sg{i}", [128, DN]) for i in range(3)]
    acc = C.sb("acc", [128, DN]); tmp = C.sb("tmp", [128, DN]); tmpB = C.sb("tmpB", [128, DN])
    lnt = {st_: {'mu': C.sb(f"mu{st_}", [128, DN]), 'rstd': C.sb(f"rstd{st_}", [128, DN]),
                 'sq': [C.sb(f"sq{st_}0", [128, DN]), C.sb(f"sq{st_}1", [128, DN])]} for st_ in 'AB'}
    h2 = C.sb("h2", [128, 8, DN], BF16)
    ss = C.sb("ss", [128, 22, DN], BF16)
    upg = C.sb("upg", [128, DN + 2]); upv = C.sb("upv", [128, DN + 2])
    ucg = C.sb("ucg", [128, DN]); ucv = C.sb("ucv", [128, DN])
    tail = C.sb("tail", [128, 44, 2])
    S.pool('memset', tail[:], 0.0, w=['tail'])

    def layer_norm(Sx, pstf, st_, src, skey, wn, bn):
        T = lnt[st_]
        mu, rstd = T['mu'], T['rstd']
        mk_, rk_ = 'mu' + st_, 'rstd' + st_
        pt, pk = pstf()
        for k in range(8):
            Sx.pe('matmul', pt[:, 0:DN], om[:], src[:, k, :], start=(k == 0), stop=(k == 7), r=['onesm', skey], w=[pk])
        Sx.act('activation', out=mu[:], in_=pt[:, 0:DN], func=AF.Identity, r=[pk], w=[mk_])
        for k in range(8):
            Sx.dve('tensor_tensor', out=src[:, k, :], in0=src[:, k, :], in1=mu[:], op=ALU.subtract, r=[skey, mk_], w=[skey])
        pt, pk = pstf()
        for k in range(8):
            sqt, sqk = T['sq'][k % 2], f"sq{st_}{k % 2}"
            Sx.pool('tensor_tensor', out=sqt[:], in0=src[:, k, :], in1=src[:, k, :], op=ALU.mult, r=[skey], w=[sqk])
            Sx.pe('matmul', pt[:, 0:DN], om[:], sqt[:], start=(k == 0), stop=(k == 7), r=['onesm', sqk], w=[pk])
        Sx.act('activation', out=rstd[:], in_=pt[:, 0:DN], func=AF.Sqrt, bias=LN_EPS_AP[0], r=[pk, 'eps'], w=[rk_])
        Sx.dve('reciprocal', out=rstd[:], in_=rstd[:], r=[rk_], w=[rk_])
        for k in range(8):
            Sx.dve('tensor_tensor', out=src[:, k, :], in0=src[:, k, :], in1=rstd[:], op=ALU.mult, r=[skey, rk_], w=[skey])
            Sx.act('activation', out=src[:, k, :], in_=src[:, k, :], func=AF.Identity, scale=DVc(wn, k), bias=DVc(bn, k),
                   r=[skey, 'dvec'], w=[skey])

    epsT = C.sb("epsT", [128, 1])
    S.pool('memset', epsT[:], LN_EPS, w=['eps'])
    LN_EPS_AP = [epsT[:, 0:1]]
    C.nmain = 6
    SB = PrefSched(S, '', defer=True)
    S.side_every = 1

    def emit_A(tl):
        c0 = tl * DN
        r1, r1k = r1b[tl % 2], f"r1_{tl % 2}"
        for k in range(8):
            if layer == 0:
                S.dma(out=xs[:, k, :], in_=io['xh'][k * 128:(k + 1) * 128, c0:c0 + DN], w=['xs'])
            elif tl == 0:
                S.dma(out=xs[:, k, 0:2], in_=io[f'xg{k // 2}'][(k % 2) * 128:(k % 2) * 128 + 128, 2046:2048], w=['xs'])
                S.dma(out=xs[:, k, 2:DN], in_=io[f'xo{k // 2}'][(k % 2) * 128:(k % 2) * 128 + 128, 0:DN - 2], w=['xs'])
            else:
                S.dma(out=xs[:, k, :], in_=io[f'xo{k // 2}'][(k % 2) * 128:(k % 2) * 128 + 128, c0 - 2:c0 - 2 + DN], w=['xs'])
            S.act('activation', out=hs[:, k, :], in_=xs[:, k, :], func=AF.Identity,
                  scale=mod_sb[:, SC1 + k:SC1 + k + 1], bias=mod_sb[:, SH1 + k:SH1 + k + 1], r=['xs', 'mod'], w=['hs'])
        for (dst, dkey, j0, kk) in ((ya, 'ya', 0, 4), (yb, 'yb', 4, 1), (yc, 'yc', 5, 4)):
            for fi, (buf, bkey) in enumerate(((yA, 'yA'), (yB, 'yB'))):
                for r_ in range(2):
                    for k_ in range(kk):
                        src = io[f'yg{j0 + k_}'][r_ * 128:(r_ + 1) * 128, :]
                        d_ = buf[:, r_ * kk + k_, :]
                        if fi == 1:
                            S.dma(out=d_, in_=src[:, 2046 + c0:2046 + c0 + DN], w=[bkey])
                        elif tl == 0:
                            S.dma(out=d_[:, 2:DN], in_=src[:, 0:DN - 2], w=[bkey])
                        else:
                            S.dma(out=d_, in_=src[:, c0 - 2:c0 - 2 + DN], w=[bkey])
            S.dve('tensor_scalar', out=dst[:, 0:2 * kk, :], in0=yA[:, 0:2 * kk, :], scalar1=fl[:, 0:1], scalar2=None, op0=ALU.mult,
                  r=['yA', 'flag'], w=[dkey])
            S.dve('scalar_tensor_tensor', out=dst[:, 0:2 * kk, :], in0=yB[:, 0:2 * kk, :], scalar=fl[:, 1:2], in1=dst[:, 0:2 * kk, :],
                  op0=ALU.mult, op1=ALU.add, r=['yB', 'flag', dkey], w=[dkey])
        for m in range(8):
            wgb, wgk = wg[m % 2], f"wg{m % 2}"
            wpb, wpk = wp[m % 2], f"wp{m % 2}"
            S.dma(out=wgb[:], in_=WG[m].rearrange("p (k c) -> p k c", k=8), w=[wgk])
            S.dma(out=wpb[:], in_=WP[m].rearrange("p (k c) -> p k c", k=18), w=[wpk])
            for br in range(3):
                pt, pk = C.pst()
                for k in range(8):
                    S.pe('matmul', pt[:, 0:DN], wgb[:, k, br * 128:(br + 1) * 128], hs[:, k, :], start=(k == 0), stop=(k == 7),
                         r=[wgk, 'hs'], w=[pk])
                S.act('activation', out=sg_[br][:], in_=pt[:, 0:DN], func=AF.Sigmoid, r=[pk], w=[f"sg{br}"])
            for br, (src, skey, k0, nk) in enumerate(((ya, 'ya', 0, 8), (yb, 'yb', 8, 2), (yc, 'yc', 10, 8))):
                pt, pk = C.pst()
                for k in range(nk):
                    S.pe('matmul', pt[:, 0:DN], wpb[:, k0 + k, :], src[:, k, :], start=(k == 0), stop=(k == nk - 1),
                         r=[wpk, skey], w=[pk])
                if br == 0:
                    S.dve('tensor_tensor', out=acc[:], in0=pt[:, 0:DN], in1=sg_[0][:], op=ALU.mult, r=[pk, 'sg0'], w=['acc'])
                else:
                    S.dve('tensor_tensor', out=tmp[:], in0=pt[:, 0:DN], in1=sg_[br][:], op=ALU.mult, r=[pk, f"sg{br}"], w=['tmp'])
                    if br == 1:
                        S.pool('tensor_tensor', out=acc[:], in0=acc[:], in1=tmp[:], op=ALU.add, r=['acc', 'tmp'], w=['acc'])
                    else:
                        S.pool('tensor_tensor', out=mg[:, m, :], in0=acc[:], in1=tmp[:], op=ALU.add, r=['acc', 'tmp'], w=['mg'])
        for m in range(8):
            wob, wok = wo[m % 2], f"wo{m % 2}"
            S.dma(out=wob[:], in_=WO[m].rearrange("p (k c) -> p k c", k=8), w=[wok])
            pt, pk = C.pst()
            for k in range(8):
                S.pe('matmul', pt[:, 0:DN], wob[:, k, :], mg[:, k, :], start=(k == 0), stop=(k == 7), r=[wok, 'mg'], w=[pk])
            S.act('activation', out=tmp[:], in_=pt[:, 0:DN], func=AF.Identity, scale=mod_sb[:, GT1 + m:GT1 + m + 1],
                  r=[pk, 'mod'], w=['tmp'])
            S.dve('scalar_tensor_tensor', out=r1[:, m, :], in0=xs[:, m, :], scalar=float(ALPHA), in1=tmp[:], op0=ALU.mult, op1=ALU.add,
                  r=['xs', 'tmp'], w=[r1k])
        layer_norm(S, C.pst, 'A', r1, r1k, 'ln1w', 'ln1b')

    def emit_B(tl):
        c0 = tl * DN
        r1, r1k = r1b[tl % 2], f"r1_{tl % 2}"
        for k in range(8):
            SB.act('activation', out=h2[:, k, :], in_=r1[:, k, :], func=AF.Identity,
                  scale=mod_sb[:, SC2 + k:SC2 + k + 1], bias=mod_sb[:, SH2 + k:SH2 + k + 1], r=[r1k, 'mod'], w=['h2'])
        for j in range(22):
            wub, wuk = wu[j % 2], f"wu{j % 2}"
            SB.dma(out=wub[:], in_=WU[j].rearrange("p (k c) -> p k c", k=8), w=[wuk])
            for half, (up, ukey, uc, uckey, jj) in enumerate(((upg, 'upg', ucg, 'ucg', j), (upv, 'upv', ucv, 'ucv', 22 + j))):
                pt, pk = C.pst_side()
                for k in range(8):
                    SB.pe('matmul', pt[:, 0:DN], wub[:, k, half * 128:(half + 1) * 128], h2[:, k, :], start=(k == 0), stop=(k == 7),
                         r=[wuk, 'h2'], w=[pk])
                SB.act('activation', out=up[:, 2:DN + 2], in_=pt[:, 0:DN], func=AF.Identity, r=[pk], w=[ukey])
                SB.pool('tensor_copy', out=up[:, 0:2], in_=tail[:, jj, :], r=['tail'], w=[ukey])
                if tl == 0:
                    SB.dve('tensor_scalar', out=up[:, 2:4], in0=up[:, 2:4], scalar1=fl[:, 1:2], scalar2=None, op0=ALU.mult,
                          r=[ukey, 'flag'], w=[ukey])
                SB.pool('tensor_copy', out=tail[:, jj, :], in_=up[:, DN:DN + 2], r=[ukey], w=['tail'])
                SB.dve('tensor_scalar', out=uc[:], in0=up[:, 2:DN + 2], scalar1=FVc('fw0', jj), scalar2=FVc('fb', jj),
                      op0=ALU.mult, op1=ALU.add, r=[ukey, 'fvec'], w=[uckey])
                SB.dve('scalar_tensor_tensor', out=uc[:], in0=up[:, 1:DN + 1], scalar=FVc('fw1', jj), in1=uc[:], op0=ALU.mult, op1=ALU.add,
                      r=[ukey, 'fvec', uckey], w=[uckey])
                SB.dve('scalar_tensor_tensor', out=uc[:], in0=up[:, 0:DN], scalar=FVc('fw2', jj), in1=uc[:], op0=ALU.mult, op1=ALU.add,
                      r=[ukey, 'fvec', uckey], w=[uckey])
            SB.act('activation', out=ucg[:], in_=ucg[:], func=AF.Silu, r=['ucg'], w=['ucg'])
            SB.pool('tensor_tensor', out=ss[:, j, :], in0=ucg[:], in1=ucv[:], op=ALU.mult, r=['ucg', 'ucv'], w=['ss'])
        for m in range(8):
            wdb, wdk = wd[m % 2], f"wd{m % 2}"
            SB.dma(out=wdb[:], in_=WD[m].rearrange("p (k c) -> p k c", k=22), w=[wdk])
            pt, pk = C.pst_side()
            for k in range(22):
                SB.pe('matmul', pt[:, 0:DN], wdb[:, k, :], ss[:, k, :], start=(k == 0), stop=(k == 21), r=[wdk, 'ss'], w=[pk])
            SB.act('activation', out=tmpB[:], in_=pt[:, 0:DN], func=AF.Identity, scale=mod_sb[:, GT2 + m:GT2 + m + 1],
                  r=[pk, 'mod'], w=['tmpB'])
            SB.dve('scalar_tensor_tensor', out=r2[:, m, :], in0=r1[:, m, :], scalar=float(ALPHA), in1=tmpB[:], op0=ALU.mult, op1=ALU.add,
                  r=[r1k, 'tmpB'], w=['r2'])
        layer_norm(SB, C.pst_side, 'B', r2, 'r2', 'ln2w', 'ln2b')
        lo_c = 2 if tl == 0 else 0
        for k in range(8):
            if layer == DEPTH - 1 or _os.environ.get('KDEBUG_OUT0'):
                SB.dma(out=io['outT'][k * 128:(k + 1) * 128, c0 - 2 + lo_c:c0 - 2 + DN], in_=r2[:, k, lo_c:DN], r=['r2'])
            else:
                SB.dma(out=io[f'xo{k // 2}'][(k % 2) * 128:(k % 2) * 128 + 128, c0 - 2 + lo_c:c0 - 2 + DN], in_=r2[:, k, lo_c:DN], r=['r2'])

    nB_prev = 0
    for tl in range(NDT):
        if len(S.side) > nB_prev:
            S.replay_side(len(S.side) - nB_prev)
        emit_A(tl)
        n0 = len(S.side)
        emit_B(tl)
        nB_prev = len(S.side) - n0
    S.replay_side()
    C.finish()


def dense_weights(inp, l, perm_c=None):
    w_in = inp['w_in'][l]
    zg = w_in[:, 7680:10752]
    WG = np.stack([np.concatenate([zg[:, br * 1024 + m * 128: br * 1024 + (m + 1) * 128] for br in range(3)], axis=1) for m in range(8)])
    pc_ = inp['proj_c'][l] if perm_c is None else inp['proj_c'][l][perm_c]
    pall = np.concatenate([inp['proj_a'][l], inp['proj_b'][l], pc_], axis=0)
    WP = np.stack([pall[:, m * 128:(m + 1) * 128] for m in range(8)])
    fu = inp['ffn_up'][l]
    WU = np.stack([np.concatenate([fu[:, j * 128:(j + 1) * 128], fu[:, 2816 + j * 128:2816 + (j + 1) * 128]], axis=1) for j in range(22)])
    dvec = np.concatenate([arr128(inp[n][l]) for n in ('ln1_w', 'ln1_b', 'ln2_w', 'ln2_b')], axis=1)
    fcw = inp['ffn_conv_w'][l]
    fvec = np.concatenate([arr128(fcw[0]), arr128(fcw[1]), arr128(fcw[2]), arr128(inp['ffn_conv_b'][l])], axis=1)
    return {'WG': np.ascontiguousarray(WG), 'WP': np.ascontiguousarray(WP), 'WO': np.ascontiguousarray(inp['w_o'][l]),
            'WU': np.ascontiguousarray(WU), 'WD': np.ascontiguousarray(inp['ffn_down'][l]),
            'dvec': np.ascontiguousarray(dvec), 'fvec': np.ascontiguousarray(fvec),
            'onesm': np.full((128, 128), 1.0 / 1024, np.float32),
            'modw': np.ascontiguousarray(inp['mod_w'][l]), 'modb': arr128(inp['mod_b'][l])}


def halo_T(full_T, sh):
    F = full_T.shape[0]
    out = np.zeros((F, DTOK), np.float32)
    if sh == 0:
        out[:, 2:] = full_T[:, 0:2048]
    else:
        out[:, :] = full_T[:, 2046:4096]
    return out


def dense_inputs(dw, c_b, sh, xT_b, yaT_b, ybT_b, ycT_b):
    d = dict(dw)
    d['cT'] = arr128(c_b)
    d['xT'] = halo_T(xT_b, sh); d['yaT'] = halo_T(yaT_b, sh); d['ybT'] = halo_T(ybT_b, sh); d['ycT'] = halo_T(ycT_b, sh)
    d['flag'] = np.full((128, 1), float(sh), np.float32)
    return d


PAIRS = [[0, 1], [2, 3], [4, 5], [6, 7]]


def emit_modphase(nc, io, layer):
    C = Ctx(nc, f"md{layer}"); S = C.S
    C.init_psum()
    cT_sb = C.sb("cT", [128, 8]); S.dma(out=cT_sb[:], in_=io['cT'], w=['cT'])
    modb_sb = C.sb("modb", [128, 48]); S.dma(out=modb_sb[:], in_=io[f'modb_{layer}'], w=['modb'])
    mod_sb = C.sb("mod", [128, 48])
    emit_mod(C, io[f'modw_{layer}'], modb_sb, cT_sb, 6144, mod_sb)
    for lo_ in (8, 16, 32, 40):
        S.dve('tensor_scalar', out=mod_sb[:, lo_:lo_ + 8], in0=mod_sb[:, lo_:lo_ + 8], scalar1=1.0, scalar2=None, op0=ALU.add,
              r=['mod'], w=['mod'])
    S.dma(out=io['modv'][layer], in_=mod_sb[:], r=['mod'])
    C.finish()


import os as _os


def emit_allgather(nc, src_h, dst_h, tag):
    if _os.environ.get("KDEBUG_NOCC"):
        return
    cc = nc.alloc_semaphore(name=f"cc_{tag}")
    with nc.Block() as block:
        @block.gpsimd
        def _(g):
            g.collective_compute("AllGather", ALU.bypass, replica_groups=PAIRS,
                                 ins=[src_h.ap().opt()], outs=[dst_h.ap().opt()]).then_inc(cc)
            g.wait_ge(cc, 1)
    nc.clear_and_free_semaphores([cc])
    nc.all_engine_barrier()


PER_LAYER = [('modw', [1024, 6144]), ('modb', [128, 48]), ('wa', [1024, 1024]), ('vecl', [128, 32]), ('gw', [128, 8, 128]),
             ('wqk', [1024, 768]), ('wv', [1024, 384]), ('vec', [128, NVEC]), ('w2c', [64, 512]), ('a2c', [64, 512]),
             ('g2c', [128, 512]), ('WG', [8, 1024, 384]), ('WP', [8, 2304, 128]), ('WO', [1024, 1024]),
             ('WU', [22, 1024, 256]), ('WD', [2816, 1024]), ('dvec', [128, 32]), ('fvec', [128, 176])]


def build_fused():
    nc = bass.Bass("TRN2", target_bir_lowering=False)
    io = {}

    def din(name, shape, dt=F32):
        io[name] = nc.dram_tensor(name, list(shape), dt, kind="ExternalInput").ap()
    din('xT', [1024, SEQ]); din('xh', [1024, DTOK]); din('cT', [128, 8]); din('pos', [1, SEQ], I32)
    din('freq', [128, 1]); din('strips', [128, NSTRIP * 128]); din('cst', [128, NCST]); din('flag', [128, 2])
    din('onesm', [128, 128]); din('v2c_1', [32, 512])
    for l in range(DEPTH):
        for (n, shp) in PER_LAYER:
            din(f'{n}_{l}', shp)
        din(f'wc_{l}', [1024, 1792 + (32 if l else 0)])
    io['outT'] = nc.dram_tensor('outT', [1024, 2048], F32, kind="ExternalOutput").ap()
    H = {}

    def dint(n, shp):
        H[n] = nc.dram_tensor(n, shp, F32)
        io[n] = H[n].ap()
    dint('modv', [DEPTH, 128, 48]); dint('vfs', [512, SEQ])
    for j in range(9):
        dint(f'ys{j}', [128, SEQ]); dint(f'yg{j}', [256, SEQ])
    for j in range(4):
        dint(f'xo{j}', [256, 2048]); dint(f'xg{j}', [512, 2048])
    for l in range(DEPTH):
        for (n, shp) in (('WG', [8, 128, 8 * 384]), ('WP', [8, 128, 18 * 128]), ('WO', [8, 128, 8 * 128]),
                         ('WU', [22, 128, 8 * 256]), ('WD', [8, 128, 22 * 128])):
            io[f'{n}b_{l}'] = nc.dram_tensor(f'{n}b_{l}', shp, BF16).ap()
    ph = _os.environ.get("KDEBUG_PH", "md,lr,at,rw,ag,dn").split(",")
    nl = int(_os.environ.get("KDEBUG_L", str(DEPTH)))
    for l in range(DEPTH):
        if 'md' in ph:
            emit_modphase(nc, io, l)
    for l in range(nl):
        if 'lr' in ph and not MERGE_LRU:
            emit_lru(nc, io, l)
        if 'at' in ph:
            emit_attn(nc, io, l)
        if 'rw' in ph:
            emit_rwkv(nc, io, l)
        if 'ag' in ph:
            for j in range(9):
                emit_allgather(nc, H[f'ys{j}'], H[f'yg{j}'], f"y{l}_{j}")
        if 'dn' in ph:
            emit_dense(nc, io, l)
        if l < DEPTH - 1 and 'ag' in ph:
            for j in range(4):
                emit_allgather(nc, H[f'xo{j}'], H[f'xg{j}'], f"x{l}_{j}")
    if _os.environ.get("KDEBUG_TAP"):
        tap = nc.dram_tensor('tap', [1152, SEQ], F32, kind="ExternalOutput").ap()
        C = Ctx(nc, "tap")
        for j in range(9):
            C.S.dma(out=tap[j * 128:(j + 1) * 128, :], in_=io[f'ys{j}'][:, :])
        C.finish()
    return nc


def core_inputs(inp, b, r):
    xT_b = np.ascontiguousarray(inp['x'][b].astype(np.float32).T)
    d = {'xT': xT_b, 'xh': halo_T(xT_b, r), 'cT': arr128(inp['c'][b]),
         'flag': np.tile(np.array([[1.0 - r, float(r)]], np.float32), (128, 1)),
         'onesm': np.full((128, 128), 1.0 / 1024, np.float32)}
    perm_c = np.concatenate([pair_rows(0), pair_rows(1)])
    for l in range(DEPTH):
        li = lru_inputs(inp, l, b, r, xT_b)
        ai = attn_inputs(inp, l, b, r, xT_b)
        ri = rwkv_inputs(inp, l, b, r, xT_b)
        dw = dense_weights(inp, l, perm_c)
        d[f'modw_{l}'] = np.ascontiguousarray(inp['mod_w'][l]); d[f'modb_{l}'] = arr128(inp['mod_b'][l])
        for n in ('wa', 'vecl', 'gw'):
            d[f'{n}_{l}'] = li[n]
        for n in ('wqk', 'wv'):
            d[f'{n}_{l}'] = ai[n]
        for n in ('wc', 'vec', 'w2c', 'a2c', 'g2c'):
            d[f'{n}_{l}'] = ri[n]
        if l:
            d['v2c_1'] = ri['v2c']
        for n in ('WG', 'WP', 'WO', 'WU', 'WD', 'dvec', 'fvec'):
            d[f'{n}_{l}'] = dw[n]
        if l == 0:
            d['pos'] = ai['pos']; d['freq'] = ai['freq']; d['strips'] = ai['strips']; d['cst'] = ri['cst']
    return d


_NC = []


def kernel(**inputs):
    inp = {k: np.asarray(v) for k, v in inputs.items()}
    if not _NC:
        _NC.append(build_fused())
    cores = [(b, r) for b in range(BATCH) for r in range(2)]
    maps = [core_inputs(inp, b, r) for (b, r) in cores]
    res = run_bass_kernel_spmd(_NC[0], maps, core_ids=list(range(8))).results
    out = np.zeros((BATCH, SEQ, D_MODEL), np.float32)
    for (b, r), rr in zip(cores, res):
        out[b, r * 2048:(r + 1) * 2048, :] = rr['outT'].T
    return out
```
